# Optimizing a Trainium2 kernel written in Bass

```python
import math
import jax, jax.numpy as jnp
from jax import lax
import numpy as np

D_MODEL = 2048
BATCH = 1
SEQ = 16384
DEPTH = 2

N_BRANCH = 4
BRANCH_W = 1024

POOL_W = BRANCH_W
POOL_WINDOWS = (2, 4, 8, 16)
POOL_GROUPS = len(POOL_WINDOWS)
POOL_GW = POOL_W // POOL_GROUPS

SGU_W = BRANCH_W
SGU_HEADS = 8
SGU_HD = SGU_W // SGU_HEADS
CHUNK = 128

N_Q_HEADS = 16
N_KV_HEADS = 2
Q_PER_KV = N_Q_HEADS // N_KV_HEADS
HEAD_DIM = 64
ATTN_W = N_Q_HEADS * HEAD_DIM
KV_W = N_KV_HEADS * HEAD_DIM
WINDOW = 128
BLOCK = 128
NUM_BUCKETS = 32
MAX_DISTANCE = 128

CONV_W = BRANCH_W
CONV_K = 31

ALPHA = (2 * DEPTH) ** 0.25
BETA = (8 * DEPTH) ** -0.25
LN_EPS = 1e-5

IN_SPLITS = (POOL_W, POOL_W,
             SGU_W, SGU_W, SGU_W,
             ATTN_W, KV_W, KV_W, ATTN_W,
             2 * CONV_W, CONV_W,
             N_BRANCH * D_MODEL)
D_IN = sum(IN_SPLITS)
SPLIT_POINTS = [int(p) for p in np.cumsum(IN_SPLITS)[:-1]]

kernel_name = "hybrid_pool_sgu_swa_conv_gated_deepnorm"


def layer_norm(x, g, b):
    xf = x.astype(jnp.float32)
    mu = jnp.mean(xf, axis=-1, keepdims=True)
    var = jnp.mean(jnp.square(xf - mu), axis=-1, keepdims=True)
    y = (xf - mu) * lax.rsqrt(var + LN_EPS) * g.astype(jnp.float32) + b.astype(jnp.float32)
    return y.astype(x.dtype)


def t5_bucket(n):
    max_exact = NUM_BUCKETS // 2
    nf = jnp.maximum(n, 1).astype(jnp.float32)
    large = max_exact + (jnp.log(nf / max_exact) / math.log(MAX_DISTANCE / max_exact)
                         * (NUM_BUCKETS - max_exact)).astype(jnp.int32)
    large = jnp.minimum(large, NUM_BUCKETS - 1)
    return jnp.where(n < max_exact, n, large)


def band_geometry():
    i = jnp.arange(BLOCK)[:, None]
    j = jnp.arange(2 * BLOCK)[None, :]
    return i + BLOCK - j


def relative_band_bias(rel_bias):
    d = jnp.clip(band_geometry(), 0, WINDOW - 1)
    bias = rel_bias[t5_bucket(d)]
    bias = jnp.transpose(bias, (2, 0, 1)).astype(jnp.float32)
    return bias.reshape(N_KV_HEADS, Q_PER_KV, BLOCK, 2 * BLOCK)


def pool_mixer(xa, w_grp, scale):
    b, s, _ = xa.shape
    xg = xa.reshape(b, s, POOL_GROUPS, POOL_GW)
    cs = jnp.cumsum(xg.astype(jnp.float32), axis=1)
    cs_pad = jnp.concatenate([jnp.zeros_like(cs[:, :1]), cs], axis=1)
    t = jnp.arange(s)[:, None]
    win = jnp.array(POOL_WINDOWS, dtype=jnp.int32)[None, :]
    lo = jnp.maximum(t + 1 - win, 0)
    cnt = jnp.minimum(t + 1, win).astype(jnp.float32)
    gidx = jnp.arange(POOL_GROUPS)[None, :]
    window_sum = cs - cs_pad[:, lo, gidx, :]
    pooled = (window_sum / cnt[None, :, :, None]).astype(xa.dtype)
    mix = pooled - xg
    y = jnp.einsum('bsgc,gcd->bsgd', mix, w_grp).reshape(b, s, POOL_W)
    return y * scale


def spatial_gating(u, v, ln_g, ln_b, w_s, b_s):
    b, s, _ = v.shape
    nc = s // CHUNK
    vn = layer_norm(v, ln_g, ln_b).reshape(b, nc, CHUNK, SGU_HEADS, SGU_HD)
    causal = jnp.tril(jnp.ones((CHUNK, CHUNK), dtype=bool))
    w = jnp.where(causal[None], w_s, jnp.zeros_like(w_s))
    sp = jnp.einsum('hts,bnshd->bnthd', w, vn) + jnp.transpose(b_s)[None, None, :, :, None]
    return u * sp.reshape(b, s, SGU_W)


def sliding_window_attention(q, k, v, sinks, band_bias):
    b, s, _ = q.shape
    nb = s // BLOCK
    qb = q.reshape(b, nb, BLOCK, N_KV_HEADS, Q_PER_KV, HEAD_DIM)
    kb = k.reshape(b, nb, BLOCK, N_KV_HEADS, HEAD_DIM)
    vb = v.reshape(b, nb, BLOCK, N_KV_HEADS, HEAD_DIM)

    def band(t):
        prev = jnp.concatenate([jnp.zeros_like(t[:, :1]), t[:, :-1]], axis=1)
        return jnp.concatenate([prev, t], axis=2)

    k_band, v_band = band(kb), band(vb)
    logits = jnp.einsum('bnqhgd,bnkhd->bnhgqk', qb, k_band).astype(jnp.float32)
    logits = logits * (HEAD_DIM ** -0.5) + band_bias[None, None]
    d = band_geometry()
    in_window = (d >= 0) & (d < WINDOW)
    has_prev = (jnp.arange(nb)[:, None] > 0) | (jnp.arange(2 * BLOCK)[None, :] >= BLOCK)
    mask = in_window[None] & has_prev[:, None, :]
    logits = jnp.where(mask[None, :, None, None], logits, jnp.float32(-1e30))
    sink = jnp.broadcast_to(sinks.astype(jnp.float32).reshape(1, 1, N_KV_HEADS, Q_PER_KV, 1, 1),
                            logits.shape[:-1] + (1,))
    probs = jax.nn.softmax(jnp.concatenate([logits, sink], axis=-1), axis=-1)[..., :-1]
    o = jnp.einsum('bnhgqk,bnkhd->bnqhgd', probs.astype(v.dtype), v_band)
    return o.reshape(b, s, ATTN_W)


def conformer_conv(d_in, conv_w, conv_b, ln_g, ln_b):
    val, gate = jnp.split(d_in, 2, axis=-1)
    glu = val * jax.nn.sigmoid(gate)
    y = lax.conv_general_dilated(glu, conv_w[:, None, :], window_strides=(1,),
                                 padding=[(CONV_K - 1, 0)],
                                 dimension_numbers=('NWC', 'WIO', 'NWC'),
                                 feature_group_count=CONV_W) + conv_b
    return jax.nn.silu(layer_norm(y, ln_g, ln_b))


def setup_inputs(seed: int = 0) -> dict:
    key = jax.random.key(seed)
    ks = jax.random.split(key, 20)
    nrm = lambda k, shape: jax.random.normal(k, shape, dtype=jnp.float32)
    return {
        "x": nrm(ks[0], (BATCH, SEQ, D_MODEL)),
        "w_in": nrm(ks[1], (DEPTH, D_MODEL, D_IN)) * D_MODEL ** -0.5,
        "pool_w": nrm(ks[2], (DEPTH, POOL_GROUPS, POOL_GW, POOL_GW)) * POOL_GW ** -0.5,
        "pool_scale": 1.0 + 0.1 * nrm(ks[3], (DEPTH, POOL_W)),
        "sgu_ln_g": 1.0 + 0.1 * nrm(ks[4], (DEPTH, SGU_W)),
        "sgu_ln_b": 0.1 * nrm(ks[5], (DEPTH, SGU_W)),
        "sgu_w": nrm(ks[6], (DEPTH, SGU_HEADS, CHUNK, CHUNK)) * 0.5 * CHUNK ** -0.5,
        "sgu_b": 1.0 + 0.1 * nrm(ks[7], (DEPTH, SGU_HEADS, CHUNK)),
        "attn_sinks": 0.5 * nrm(ks[8], (DEPTH, N_Q_HEADS)),
        "rel_bias": 0.5 * nrm(ks[9], (NUM_BUCKETS, N_Q_HEADS)),
        "conv_w": nrm(ks[10], (DEPTH, CONV_K, CONV_W)) * CONV_K ** -0.5,
        "conv_b": 0.02 * nrm(ks[11], (DEPTH, CONV_W)),
        "conv_ln_g": 1.0 + 0.1 * nrm(ks[12], (DEPTH, CONV_W)),
        "conv_ln_b": 0.1 * nrm(ks[13], (DEPTH, CONV_W)),
        "w_branch": nrm(ks[14], (DEPTH, N_BRANCH, BRANCH_W, D_MODEL)) * BRANCH_W ** -0.5 * BETA,
        "w_out": nrm(ks[15], (DEPTH, D_MODEL, D_MODEL)) * D_MODEL ** -0.5 * BETA,
        "ln_g": 1.0 + 0.1 * nrm(ks[16], (DEPTH, D_MODEL)),
        "ln_b": 0.1 * nrm(ks[17], (DEPTH, D_MODEL)),
    }


def reference(x, w_in, pool_w, pool_scale, sgu_ln_g, sgu_ln_b, sgu_w, sgu_b, attn_sinks,
              rel_bias, conv_w, conv_b, conv_ln_g, conv_ln_b, w_branch, w_out, ln_g, ln_b):
    b, s, _ = x.shape
    band_bias = relative_band_bias(rel_bias)
    for l in range(DEPTH):
        h = jnp.einsum('bsd,de->bse', x, w_in[l])
        (a_in, a_gate, u, v, b_gate, q, k, vv, c_gate,
         d_in, d_gate, g_logits) = jnp.split(h, SPLIT_POINTS, axis=-1)

        y_a = pool_mixer(a_in, pool_w[l], pool_scale[l]) * jax.nn.silu(a_gate)
        y_b = spatial_gating(u, v, sgu_ln_g[l], sgu_ln_b[l], sgu_w[l], sgu_b[l]) * jax.nn.silu(b_gate)
        y_c = sliding_window_attention(q, k, vv, attn_sinks[l], band_bias) * jax.nn.silu(c_gate)
        y_d = conformer_conv(d_in, conv_w[l], conv_b[l], conv_ln_g[l], conv_ln_b[l]) * jax.nn.silu(d_gate)

        gates = jax.nn.sigmoid(g_logits.reshape(b, s, N_BRANCH, D_MODEL))
        branches = (y_a, y_b, y_c, y_d)
        merged = gates[:, :, 0] * jnp.einsum('bsc,cd->bsd', branches[0], w_branch[l, 0])
        for i in range(1, N_BRANCH):
            merged = merged + gates[:, :, i] * jnp.einsum('bsc,cd->bsd', branches[i], w_branch[l, i])
        out = jnp.einsum('bsd,de->bse', merged, w_out[l])
        x = layer_norm(ALPHA * x + out, ln_g[l], ln_b[l])
    return x
```

```python
import contextlib
import numpy as np
import concourse.bass as bass
import concourse.mybir as mybir
from concourse.bass_utils import run_bass_kernel_spmd

F32 = mybir.dt.float32
BF16 = mybir.dt.bfloat16
AF = mybir.ActivationFunctionType
ALU = mybir.AluOpType

D = 2048
SEQ = 16384
NCORE = 8
TOK_CORE = SEQ // NCORE
NST = 2
ST_TOK = TOK_CORE // NST
KT = D // 128
ALPHA = (2 * 2) ** 0.25
LN_EPS = 1e-5
NEG = -30000.0
NW = 5
HALO = 32
PPC = 56 + 8 * 31

O_AIN, O_AG, O_U, O_V, O_BG, O_Q, O_K, O_VV, O_CG, O_DV, O_DG, O_DGATE, O_GL = (
    0, 1024, 2048, 3072, 4096, 5120, 6144, 6272, 6400, 7424, 8448, 9472, 10496)


def layer_units():
    u = []
    t8 = lambda base: [("in", base + 128 * t) for t in range(8)]
    def gates(i):
        r = []
        for d in range(16):
            if d % 2 == 0:
                r.append(("br", i, d))
            r.append(("in", O_GL + i * 2048 + d * 128))
        return r
    u += t8(O_AIN) + t8(O_AG) + gates(0)
    u += t8(O_V) + t8(O_BG) + t8(O_U) + gates(1)
    u += t8(O_CG) + t8(O_Q) + [("kd", 0), ("kd", 1), ("in", O_VV)] + gates(2)
    u += t8(O_DG) + t8(O_DV) + t8(O_DGATE) + gates(3)
    u += [("out", e) for e in range(16)]
    return u


UNITS = layer_units()
NU = len(UNITS)


def build_wstream(w_in, w_branch, w_out):
    ws = np.empty((NU, 128, 2048), np.float32)
    wk = w_in.reshape(KT, 128, -1)
    for n, un in enumerate(UNITS):
        if un[0] == "in":
            c = un[1]
            ws[n] = wk[:, :, c:c + 128].transpose(1, 0, 2).reshape(128, 2048)
        elif un[0] == "kd":
            c = O_K + 64 * un[1]
            blk = wk[:, :, c:c + 64]
            ws[n] = np.concatenate([blk, blk], axis=2).transpose(1, 0, 2).reshape(128, 2048)
        elif un[0] == "br":
            _, i, d = un
            wb = w_branch[i].reshape(8, 128, 2048)
            a = wb[:, :, d * 128:(d + 1) * 128].transpose(1, 0, 2).reshape(128, 1024)
            b = wb[:, :, (d + 1) * 128:(d + 2) * 128].transpose(1, 0, 2).reshape(128, 1024)
            ws[n] = np.concatenate([a, b], axis=1)
        else:
            e = un[1]
            wo = w_out.reshape(KT, 128, 2048)
            ws[n] = wo[:, :, e * 128:(e + 1) * 128].transpose(1, 0, 2).reshape(128, 2048)
    return ws


def t5_bucket_np(n):
    max_exact = 16
    nf = np.maximum(n, 1).astype(np.float32)
    large = max_exact + (np.log(nf / np.float32(max_exact)) / np.float32(np.log(128 / max_exact))
                         * np.float32(32 - max_exact)).astype(np.int32)
    large = np.minimum(large, 31)
    return np.where(n < max_exact, n, large)


def _bucket_table():
    return t5_bucket_np(np.arange(128))


def build_bias_table(rel_bias):
    bk = _bucket_table()
    kk = np.arange(128)[:, None]
    qq = np.arange(128)[None, :]
    bt = np.full((128, 2, 2, 2, 4, 128), NEG, np.float32)
    for kb in range(2):
        dist = qq - kk + (128 if kb == 0 else 0)
        valid = (dist >= 0) & (dist < 128)
        idx = bk[np.clip(dist, 0, 127)]
        for kv in range(2):
            for var in range(2):
                for j in range(4):
                    h = kv * 8 + 2 * j + var
                    vals = rel_bias[idx, h]
                    bt[:, kb, kv, var, j, :] = np.where(valid, vals, np.float32(NEG))
    return bt.reshape(128, 4096)


def build_pp(l, pool_scale, sgu_ln_g, sgu_ln_b, conv_b, conv_ln_g, conv_ln_b, attn_sinks, conv_w):
    pp = np.zeros((128, PPC), np.float32)
    col = lambda v: v.reshape(8, 128).T
    pp[:, 0:8] = col(pool_scale[l])
    pp[:, 8:16] = col(sgu_ln_g[l])
    pp[:, 16:24] = col(sgu_ln_b[l])
    pp[:, 24:32] = col(conv_b[l])
    pp[:, 32:40] = col(conv_ln_g[l])
    pp[:, 40:48] = col(conv_ln_b[l])
    for kv in range(2):
        for j in range(4):
            pp[0:64, 48 + kv * 4 + j] = attn_sinks[l, kv * 8 + 2 * j]
            pp[64:128, 48 + kv * 4 + j] = attn_sinks[l, kv * 8 + 2 * j + 1]
    cw = conv_w[l].reshape(31, 8, 128)
    pp[:, 56:] = cw.transpose(2, 1, 0).reshape(128, 8 * 31)
    return pp


class St:
    __slots__ = ("w", "r")

    def __init__(self):
        self.w = {}
        self.r = {}


def _merge(dst, src):
    for k, v in src.items():
        if dst.get(k, 0) < v:
            dst[k] = v


class Sync:
    def __init__(self, nc, es):
        self.nc = nc
        self.es = es
        self.engs = {"pe": nc.tensor, "act": nc.scalar, "dve": nc.vector, "pool": nc.gpsimd, "sp": nc.sync}
        self.sems = {}
        self.cnt = {}
        for e in ("pe", "act", "dve", "pool"):
            self.sems[e] = es.enter_context(nc.semaphore("s_" + e))
            self.cnt[e] = 0
        self.known = {e: {} for e in self.engs}
        self.ndma = 0

    def new_dma_sem(self, name):
        nm = "d_" + name
        self.sems[nm] = self.es.enter_context(self.nc.semaphore(nm))
        self.cnt[nm] = 0
        return nm

    def _wait(self, eng, toks):
        kn = self.known[eng]
        for s, v in toks.items():
            if eng == "pe" and s == "pe":
                continue
            if kn.get(s, 0) < v:
                self.engs[eng].wait_ge(self.sems[s], v)
                kn[s] = v

    def _deps(self, reads, writes):
        toks = {}
        for s in reads:
            _merge(toks, s.w)
        for s in writes:
            _merge(toks, s.w)
            _merge(toks, s.r)
        return toks

    def op(self, eng, reads, writes, fn):
        self._wait(eng, self._deps(reads, writes))
        inst = fn(self.engs[eng])
        self.cnt[eng] += 1
        inst.then_inc(self.sems[eng], 1)
        tok = {eng: self.cnt[eng]}
        for s in reads:
            _merge(s.r, tok)
        for s in writes:
            s.w = dict(tok)
            s.r = {}
        return tok

    def mm_group(self, reads, writes, fns):
        self._wait("pe", self._deps(reads, writes))
        for f in fns[:-1]:
            f(self.engs["pe"])
        inst = fns[-1](self.engs["pe"])
        self.cnt["pe"] += 1
        inst.then_inc(self.sems["pe"], 1)
        tok = {"pe": self.cnt["pe"]}
        for s in reads:
            _merge(s.r, tok)
        for s in writes:
            s.w = dict(tok)
            s.r = {}
        return tok

    def dma(self, q, dsem, reads, writes, out, in_):
        toks = self._deps(reads, writes)
        if self.cnt[dsem] > 0:
            _merge(toks, {dsem: self.cnt[dsem]})
        self._wait(q, toks)
        inst = self.engs[q].dma_start(out=out, in_=in_)
        self.cnt[dsem] += 16
        inst.then_inc(self.sems[dsem], 16)
        tok = {dsem: self.cnt[dsem]}
        for s in reads:
            _merge(s.r, tok)
        for s in writes:
            s.w = dict(tok)
            s.r = {}
        self.ndma += 1
        return tok

    def wait_all(self, eng, states):
        toks = {}
        for s in states:
            _merge(toks, s.w)
            _merge(toks, s.r)
        self._wait(eng, toks)


def inherit(dst, src):
    for d_ in dst:
        for s in src:
            _merge(d_.w, s.w)
            _merge(d_.w, s.r)


def chunks(lo, hi, mx=512):
    n = hi - lo
    k = -(-n // mx)
    base = -(-n // k)
    base = -(-base // 8) * 8
    out = []
    c = lo
    while c < hi:
        out.append((c, min(hi, c + base)))
        c += base
    return out


def build_program(layers, fused, branches=(0, 1, 2, 3)):
    nl = len(layers)
    NB1 = 9 if fused else 8
    XROWS = (NB1 + 1) * 128
    nc = bass.Bass("TRN2", target_bir_lowering=False)
    dt = nc.dram_tensor
    x_in = dt("x_in", [NST, XROWS, D], F32, kind="ExternalInput").ap()
    ws = dt("ws", [nl, NU, 128, 2048], F32, kind="ExternalInput").ap()
    pp_in = dt("pp", [128, nl, PPC], F32, kind="ExternalInput").ap()
    pw_in = dt("pw", [nl, 128, 2048], F32, kind="ExternalInput").ap()
    wst_in = dt("wst", [nl, 128, 1024], F32, kind="ExternalInput").ap()
    bsb_in = dt("bsb", [nl, 128, 1024], F32, kind="ExternalInput").ap()
    lng_in = dt("lng", [nl, 128, 2048], F32, kind="ExternalInput").ap()
    lnb_in = dt("lnb", [nl, 128, 2048], F32, kind="ExternalInput").ap()
    bt_in = dt("bt", [128, 4096], F32, kind="ExternalInput").ap()
    cst_in = dt("cst", [128, 384], F32, kind="ExternalInput").ap()
    y_out = dt("y_out", [NST, ST_TOK, D], F32, kind="ExternalOutput").ap()
    x1s = dt("x1s", [NST, 8 * 128, D], F32, kind="Internal").ap() if fused else None

    es = contextlib.ExitStack()
    with es:
        S = Sync(nc, es)
        off = [17536]

        def alloc(name, shape, dtype, at=None):
            nbytes = int(np.prod(shape[1:])) * (2 if dtype == BF16 else 4)
            if at is None:
                at = off[0]
                off[0] += (nbytes + 63) // 64 * 64
                assert off[0] <= 229344, (name, off[0])
            return nc.alloc_sbuf_tensor_at(name, list(shape), dtype, offset=at), at

        TW = HALO + NB1 * 128 if fused else HALO + 9 * 128
        TW = HALO + 9 * 128
        xT, _ = alloc("xT", [128, KT, 1280], BF16)
        MG, _ = alloc("MG", [128, KT, 1152], BF16)
        WR, _ = alloc("WR", [128, NW, 2048], BF16)
        Y, aY = alloc("Y", [128, 8, 1152], BF16)
        T0, aT0 = alloc("T0", [128, 8, TW], BF16)
        T1, aT1 = alloc("T1", [128, 8, TW], BF16)
        T2, aT2 = alloc("T2", [128, 9216], BF16)
        CB16, aCB16 = alloc("CB16", [128, 4096], BF16)
        CB16b, aCB16b = alloc("CB16b", [128, 2048], BF16)
        CF32, aCF32 = alloc("CF32", [128, 1024], F32)
        CF32b, aCF32b = alloc("CF32b", [128, 1024], F32)
        FT, _ = alloc("FT", [128, 4, 512], F32)
        SQ, _ = alloc("SQ", [128, 2, 512], BF16)
        PP, _ = alloc("PP", [128, nl, PPC], F32)
        IDB, _ = alloc("IDB", [128, 128], BF16)
        ONESV, _ = alloc("ONESV", [128, 2, 2, 128], BF16)
        ONES, _ = alloc("ONES", [128, 128], BF16)
        CST, _ = alloc("CST", [128, 384], F32)
        SC, _ = alloc("SC", [128, 32], F32)
        FX, _ = alloc("FX", [128, 2, 16], F32)
        R = [alloc("R0", [128, 2048], F32, at=aT0)[0], alloc("R1", [128, 2048], F32, at=aT0 + 8192)[0],
             alloc("R2", [128, 2048], F32, at=aT2)[0]]
        XB, _ = alloc("XB", [128, 2048], BF16, at=aT2 + 8192)
        XB2, _ = alloc("XB2", [128, 2048], BF16, at=aT2 + 12288)
        JK, _ = alloc("JK", [128, 2048], BF16, at=aY)
        XBS = [XB, XB2]
        VN, _ = alloc("VN", [128, 2, 1024], BF16, at=aT2 + 8192)
        LNG, _ = alloc("LNG", [128, 2048], F32, at=aT1)
        LNB, _ = alloc("LNB", [128, 2048], F32, at=aT1 + 8192)
        PQ, _ = alloc("PQ", [128, 2, TW], BF16, at=aT2)
        QQ, _ = alloc("QQ", [128, 2, TW], BF16, at=aY)
        KD, _ = alloc("KD", [128, 2, 1280], BF16, at=aT2)
        VT, _ = alloc("VT", [128, 1280], BF16, at=aT2 + 5120)
        VA, _ = alloc("VA", [128, 10, 2, 2, 128], BF16, at=aT2 + 7680)
        BT, _ = alloc("BT", [128, 2, 2, 2, 512], BF16, at=aCB16)
        DIAG, _ = alloc("DIAG", [128, 31, 128], BF16, at=aCB16)
        DIAG2, _ = alloc("DIAG2", [128, 31, 128], BF16, at=aT1)
        PW, _ = alloc("PW", [128, 4, 2, 256], BF16, at=aCB16b)
        WST, _ = alloc("WST", [128, 8, 128], BF16, at=aCB16b)
        PT, _ = alloc("PT", [128, 4, 512], BF16, at=aCB16b)
        BSB, _ = alloc("BSB", [128, 2, 512], F32, at=aCF32)
        SINKBC, _ = alloc("SINKBC", [128, 2, 4, 128], F32, at=aCF32)
        LT, _ = alloc("LT", [128, 2, 512], F32, at=aCF32b)
        STT, _ = alloc("STT", [128, 2, 512], F32, at=aCF32b)

        PBALL = es.enter_context(nc.psum_tensor("pball", [128, 8, 512], F32))
        PB = [PBALL[:, i, :] for i in range(6)]
        P16 = PBALL[:, 6:8, :].bitcast(BF16).rearrange("p a b -> p (a b)")
        P16B = PBALL[:, 4:6, :].bitcast(BF16).rearrange("p a b -> p (a b)")

        s_xT = [St() for _ in range(10)]
        s_MG = [St() for _ in range(16)]
        s_WR = [St() for _ in range(NW)]
        s_Y = [St() for _ in range(8)]
        s_T0 = [St() for _ in range(8)]
        s_T1 = [St() for _ in range(8)]
        s_T2 = St()
        s_CB16, s_CB16b, s_CF32, s_CF32b = St(), St(), St(), St()
        s_FT = [St(), St(), St(), St()]
        s_LT = [St(), St()]
        s_SQ = [St(), St()]
        s_PB = [St() for _ in range(6)]
        s_P16 = St()
        s_const = St()
        s_SC = St()
        s_SCP = [St(), St(), St()]
        s_FX = St()
        s_R = [St(), St(), St()]
        s_XB = St()
        s_XBS = [s_XB, St()]
        P16S = [(P16, None), (P16B, None)]
        s_LN = St()
        s_x1s = [St() for _ in range(NST)]
        d_w = [S.new_dma_sem("w%d" % i) for i in range(NW)]
        d_r = [S.new_dma_sem("r%d" % i) for i in range(3)]
        d_o = [S.new_dma_sem("o%d" % i) for i in range(3)]
        d_c = S.new_dma_sem("c")
        d_c2 = S.new_dma_sem("c2")
        d_cp = S.new_dma_sem("cp")
        d_ln = S.new_dma_sem("ln")

        bank_rr = [0]
        bank_lim = [6]

        def nbank():
            b = bank_rr[0] % bank_lim[0]
            bank_rr[0] = (b + 1) % bank_lim[0]
            return b

        wseq = []
        for _st in range(NST):
            for li in range(nl):
                for n in range(NU):
                    wseq.append((li, n))
        wstate = {"loaded": 0, "used": 0}
        wlive = set()

        def release(i):
            wlive.discard(i)
            prefetch()

        def prefetch():
            oldest = min(wlive) if wlive else wstate["used"]
            while wstate["loaded"] < len(wseq) and wstate["loaded"] < oldest + NW:
                i = wstate["loaded"]
                li, n = wseq[i]
                sl = i % NW
                S.dma("pool", d_w[sl], [], [s_WR[sl]], WR[:, sl, :], ws[li, n])
                wstate["loaded"] += 1

        def next_unit(li, kind):
            i = wstate["used"]
            assert wseq[i][0] == li and UNITS[wseq[i][1]][0] == kind, (wseq[i], UNITS[wseq[i][1]], kind)
            prefetch()
            wstate["used"] += 1
            wlive.add(i)
            return i

        S.dma("sp", d_c, [], [s_const], PP[:], pp_in)
        S.dma("sp", d_c2, [], [s_const], CST[:], cst_in)
        S.dma("pool", d_cp, [], [s_const], IDB[:], cst_in[:, 0:128])
        MASK = CST[:, 128:256]
        FLAG = CST[:, 256:257]
        POOLFIX = CST[:, 272:336]
        S.op("dve", [], [s_const], lambda e: e.memset(ONES[:], 1.0))
        S.op("dve", [], [s_const], lambda e: e.memset(ONESV[:], 0.0))
        S.op("dve", [], [s_const], lambda e: e.memset(ONESV[:, 1, 0, 0:64], 1.0))
        S.op("dve", [], [s_const], lambda e: e.memset(ONESV[:, 1, 1, 64:128], 1.0))
        S.op("dve", [s_const], [s_const], lambda e: e.tensor_scalar(
            out=ONESV[:, 0, 0, 0:64], in0=ONESV[:, 1, 0, 0:64], scalar1=FLAG, scalar2=None, op0=ALU.mult))
        S.op("dve", [s_const], [s_const], lambda e: e.tensor_scalar(
            out=ONESV[:, 0, 1, 64:128], in0=ONESV[:, 1, 1, 64:128], scalar1=FLAG, scalar2=None, op0=ALU.mult))
        prefetch()

        def xblocks(c0, c1):
            return s_xT[c0 // 128:(c1 - 1) // 128 + 1]

        alt = [0]

        def evac_copy(bank, src, dst, dst_states, scale=None):
            alt[0] ^= 1
            if alt[0]:
                if scale is None:
                    S.op("act", [s_PB[bank]], dst_states, lambda e: e.activation(out=dst, in_=src, func=AF.Copy))
                else:
                    S.op("act", [s_PB[bank]], dst_states,
                         lambda e: e.activation(out=dst, in_=src, func=AF.Copy, scale=scale))
            else:
                if scale is None:
                    S.op("dve", [s_PB[bank]], dst_states, lambda e: e.tensor_copy(out=dst, in_=src))
                else:
                    S.op("dve", [s_PB[bank]], dst_states, lambda e: e.tensor_scalar(
                        out=dst, in0=src, scalar1=scale, scalar2=None, op0=ALU.mult))

        def inproj(li, kind, c_lo, c_hi, evac):
            ui = next_unit(li, kind)
            sl = ui % NW
            for (c0, c1) in chunks(c_lo, c_hi):
                b = nbank()
                n = c1 - c0
                fns = []
                for k in range(KT):
                    fns.append(lambda e, k=k: e.matmul(PB[b][:, 0:n], lhsT=WR[:, sl, k * 128:(k + 1) * 128],
                                                       rhs=xT[:, k, c0:c1], start=(k == 0), stop=(k == KT - 1)))
                S.mm_group([s_WR[sl]] + xblocks(c0, c1), [s_PB[b]], fns)
                evac(b, PB[b][:, 0:n], c0, c1)
            release(ui)

        def inproj_gen(li, kind, c_lo, c_hi, evac):
            ui = next_unit(li, kind)
            sl = ui % NW
            for (c0, c1) in chunks(c_lo, c_hi):
                b = nbank()
                n = c1 - c0
                fns = []
                for k in range(KT):
                    fns.append(lambda e, k=k: e.matmul(PB[b][:, 0:n], lhsT=WR[:, sl, k * 128:(k + 1) * 128],
                                                       rhs=xT[:, k, c0:c1], start=(k == 0), stop=(k == KT - 1)))
                S.mm_group([s_WR[sl]] + xblocks(c0, c1), [s_PB[b]], fns)
                evac(b, PB[b][:, 0:n], c0, c1)
                yield
            release(ui)

        def chain(gens):
            for g in gens:
                for _ in g:
                    yield

        def ln_stats(Tb, s_T, c0, c1, nfeat):
            n = c1 - c0
            b1, b2 = nbank(), nbank()
            for t in range(8):
                j = t % 2
                S.op("act", [s_T[t]], [s_SQ[j]], lambda e: e.activation(
                    out=SQ[:, j, 0:n], in_=Tb[:, t, c0:c1], func=AF.Square))
                S.mm_group([s_T[t], s_const], [s_PB[b1]], [lambda e: e.matmul(
                    PB[b1][:, 0:n], lhsT=ONES[:], rhs=Tb[:, t, c0:c1], start=(t == 0), stop=(t == 7))])
                S.mm_group([s_SQ[j], s_const], [s_PB[b2]], [lambda e: e.matmul(
                    PB[b2][:, 0:n], lhsT=ONES[:], rhs=SQ[:, j, 0:n], start=(t == 0), stop=(t == 7))])
            inv = 1.0 / nfeat
            S.op("dve", [s_PB[b1]], [s_CF32b], lambda e: e.tensor_scalar(
                out=STT[:, 0, 0:n], in0=PB[b1][:, 0:n], scalar1=inv, scalar2=None, op0=ALU.mult))
            S.op("dve", [s_CF32b], [s_FT[0]], lambda e: e.tensor_tensor(
                out=FT[:, 0, 0:n], in0=STT[:, 0, 0:n], in1=STT[:, 0, 0:n], op=ALU.mult))
            S.op("dve", [s_PB[b2], s_FT[0]], [s_CF32b], lambda e: e.scalar_tensor_tensor(
                out=STT[:, 1, 0:n], in0=PB[b2][:, 0:n], scalar=inv, in1=FT[:, 0, 0:n],
                op0=ALU.mult, op1=ALU.subtract))
            S.op("dve", [s_CF32b], [s_CF32b], lambda e: e.tensor_scalar(
                out=STT[:, 1, 0:n], in0=STT[:, 1, 0:n], scalar1=LN_EPS, scalar2=None, op0=ALU.add))
            S.op("act", [s_CF32b], [s_CF32b], lambda e: e.activation(
                out=STT[:, 1, 0:n], in_=STT[:, 1, 0:n], func=AF.Sqrt))
            S.op("dve", [s_CF32b], [s_CF32b], lambda e: e.reciprocal(out=STT[:, 1, 0:n], in_=STT[:, 1, 0:n]))

        d_x = [S.new_dma_sem("x0"), S.new_dma_sem("x1")]

        def phase0_block(st, blk):
            xi = blk % 2
            xb, sxb = XBS[xi], s_XBS[xi]
            sp16 = [s_PB[4], s_PB[5]]
            S.dma("pool", d_x[xi], [], [sxb], xb[:], x_in[st, blk * 128:(blk + 1) * 128, :])
            fns = [lambda e, k=k: e.transpose(P16B[:, k * 128:(k + 1) * 128], xb[:, k * 128:(k + 1) * 128], IDB[:])
                   for k in range(KT)]
            S.mm_group([sxb, s_const], sp16, fns)
            S.op("dve", sp16, [s_xT[blk]], lambda e: e.tensor_copy(
                out=xT[:, :, blk * 128:(blk + 1) * 128], in_=P16B.rearrange("p (k c) -> p k c", k=KT)))

        def emit_layer(st, li, NB, first_in_prog, last_in_prog, skip_phase0=False, hook=None):
            l = li
            Tm = NB * 128
            W = HALO + Tm
            MAIN0 = 128
            tcol = lambda c: c - 96
            ppc = lambda c0, c1=None: PP[:, l, c0:(c0 + 1 if c1 is None else c1)]

            if first_in_prog and not skip_phase0:
                for blk in range(NB + 1):
                    phase0_block(st, blk)

            mchunks = chunks(MAIN0, MAIN0 + Tm)
            first_merge = [True]

            def phase2(i):
                for d in range(16):
                    if d % 2 == 0:
                        uib = next_unit(li, "br")
                        slb = uib % NW
                    uig = next_unit(li, "in")
                    slg = uig % NW
                    for (c0, c1) in mchunks:
                        n = c1 - c0
                        bg, bp = nbank(), nbank()
                        fns = [lambda e, k=k: e.matmul(PB[bg][:, 0:n], lhsT=WR[:, slg, k * 128:(k + 1) * 128],
                                                       rhs=xT[:, k, c0:c1], start=(k == 0), stop=(k == KT - 1))
                               for k in range(KT)]
                        S.mm_group([s_WR[slg]] + xblocks(c0, c1), [s_PB[bg]], fns)
                        o = (d % 2) * 1024
                        fns = [lambda e, k=k: e.matmul(PB[bp][:, 0:n], lhsT=WR[:, slb, o + k * 128:o + (k + 1) * 128],
                                                       rhs=Y[:, k, c0 - MAIN0:c1 - MAIN0], start=(k == 0), stop=(k == 7))
                               for k in range(8)]
                        S.mm_group([s_WR[slb]] + s_Y, [s_PB[bp]], fns)
                        j = d % 2
                        S.op("act", [s_PB[bg]], [s_SQ[j]], lambda e: e.activation(
                            out=SQ[:, j, 0:n], in_=PB[bg][:, 0:n], func=AF.Sigmoid))
                        mg = MG[:, d, c0 - MAIN0:c1 - MAIN0]
                        if first_merge[0]:
                            S.op("dve", [s_PB[bp], s_SQ[j]], [s_MG[d]], lambda e: e.tensor_tensor(
                                out=mg, in0=PB[bp][:, 0:n], in1=SQ[:, j, 0:n], op=ALU.mult))
                        else:
                            S.op("dve", [s_PB[bp], s_SQ[j]], [s_FT[j]], lambda e: e.tensor_tensor(
                                out=FT[:, j, 0:n], in0=PB[bp][:, 0:n], in1=SQ[:, j, 0:n], op=ALU.mult))
                            S.op("dve", [s_FT[j], s_MG[d]], [s_MG[d]], lambda e: e.tensor_tensor(
                                out=mg, in0=FT[:, j, 0:n], in1=mg, op=ALU.add))
                    release(uig)
                    if d % 2 == 1:
                        release(uib)
                first_merge[0] = False

            def skip_units(kinds):
                for k_ in kinds:
                    next_unit(li, k_)

            def branch_A():
                S.dma("pool", d_cp, [s_CB16b], [s_CB16b], PW[:].rearrange("p a b c -> p (a b c)"), pw_in[l])
                for t in range(8):
                    inproj(li, "in", 96, MAIN0 + Tm, lambda b, ps, c0, c1, t=t: evac_copy(
                        b, ps, T1[:, t, tcol(c0):tcol(c1)], [s_T1[t]]))
                for g in range(4):
                    win = 2 ** (g + 1)
                    X = T1[:, 2 * g:2 * g + 2, :]
                    sX = [s_T1[2 * g], s_T1[2 * g + 1]]
                    sP, sQ = [s_T2], s_Y
                    S.op("dve", sX, sP, lambda e: e.tensor_tensor(
                        out=PQ[:, :, 1:W], in0=X[:, :, 1:W], in1=X[:, :, 0:W - 1], op=ALU.add))
                    cur, scur = PQ, sP
                    if win >= 4:
                        S.op("dve", sP, sQ, lambda e: e.tensor_tensor(
                            out=QQ[:, :, 3:W], in0=PQ[:, :, 3:W], in1=PQ[:, :, 1:W - 2], op=ALU.add))
                        cur, scur = QQ, sQ
                    if win >= 8:
                        S.op("dve", sQ, sP, lambda e: e.tensor_tensor(
                            out=PQ[:, :, 7:W], in0=QQ[:, :, 7:W], in1=QQ[:, :, 3:W - 4], op=ALU.add))
                        cur, scur = PQ, sP
                    if win >= 16:
                        S.op("dve", sP, sQ, lambda e: e.tensor_tensor(
                            out=QQ[:, :, 15:W], in0=PQ[:, :, 15:W], in1=PQ[:, :, 7:W - 8], op=ALU.add))
                        cur, scur = QQ, sQ
                    fixc = None
                    if st == 0:
                        fixc = HALO + (128 if (fused and li == 0) else 0)
                        S.op("dve", scur + [s_const], [s_FX], lambda e: e.tensor_tensor(
                            out=FX[:], in0=cur[:, :, fixc:fixc + 16],
                            in1=POOLFIX[:, g * 16:(g + 1) * 16].unsqueeze(1).broadcast_to([128, 2, 16]), op=ALU.mult))
                        S.op("dve", sX + [s_FX], [s_FX], lambda e: e.tensor_tensor(
                            out=FX[:], in0=FX[:], in1=X[:, :, fixc:fixc + 16], op=ALU.subtract))
                    S.op("dve", scur + sX, sX, lambda e: e.scalar_tensor_tensor(
                        out=X[:, :, HALO:W], in0=cur[:, :, HALO:W], scalar=1.0 / win, in1=X[:, :, HALO:W],
                        op0=ALU.mult, op1=ALU.subtract))
                    if fixc is not None:
                        S.op("dve", [s_FX], sX, lambda e: e.tensor_copy(out=X[:, :, fixc:fixc + 16], in_=FX[:]))
                for t in range(8):
                    inproj(li, "in", MAIN0, MAIN0 + Tm, lambda b, ps, c0, c1, t=t: S.op(
                        "act", [s_PB[b]], [s_T0[t]], lambda e: e.activation(
                            out=T0[:, t, tcol(c0):tcol(c1)], in_=ps, func=AF.Silu)))
                for g in range(4):
                    for dtl in range(2):
                        t = 2 * g + dtl
                        for (c0, c1) in chunks(HALO, W):
                            n = c1 - c0
                            b = nbank()
                            fns = [lambda e, ct=ct: e.matmul(PB[b][:, 0:n], lhsT=PW[:, g, ct, dtl * 128:(dtl + 1) * 128],
                                                             rhs=T1[:, 2 * g + ct, c0:c1], start=(ct == 0), stop=(ct == 1))
                                   for ct in range(2)]
                            S.mm_group([s_T1[2 * g], s_T1[2 * g + 1], s_CB16b], [s_PB[b]], fns)
                            S.op("dve", [s_PB[b], s_T0[t], s_const], [s_Y[t]], lambda e: e.scalar_tensor_tensor(
                                out=Y[:, t, c0 - HALO:c1 - HALO], in0=PB[b][:, 0:n], scalar=ppc(t),
                                in1=T0[:, t, c0:c1], op0=ALU.mult, op1=ALU.mult))

            def branch_B():
                S.dma("sp", d_c2, [s_FT[0], s_FT[1]], [s_FT[0], s_FT[1]],
                      FT[:, 0:2, :].rearrange("p a b -> p (a b)"), wst_in[l])
                for h in range(8):
                    S.op("dve", [s_FT[0], s_FT[1], s_const], [s_CB16b], lambda e: e.tensor_tensor(
                        out=WST[:, h, :], in0=FT[:, 0:2, :].rearrange("p a b -> p (a b)")[:, h * 128:(h + 1) * 128],
                        in1=MASK, op=ALU.mult))
                S.dma("sp", d_c2, [s_CF32], [s_CF32], BSB[:].rearrange("p a b -> p (a b)"), bsb_in[l])
                for t in range(8):
                    inproj(li, "in", MAIN0, MAIN0 + Tm, lambda b, ps, c0, c1, t=t: evac_copy(
                        b, ps, T1[:, t, tcol(c0):tcol(c1)], [s_T1[t]]))
                def ev_bg(t):
                    return lambda b, ps, c0, c1: S.op(
                        "act", [s_PB[b]], [s_T0[t]], lambda e: e.activation(
                            out=T0[:, t, tcol(c0):tcol(c1)], in_=ps, func=AF.Silu))

                def ev_u(t):
                    return lambda b, ps, c0, c1: S.op(
                        "dve", [s_PB[b], s_T0[t]], [s_T0[t]], lambda e: e.tensor_tensor(
                            out=T0[:, t, tcol(c0):tcol(c1)], in0=ps, in1=T0[:, t, tcol(c0):tcol(c1)], op=ALU.mult))

                gq = chain([inproj_gen(li, "in", MAIN0, MAIN0 + Tm, ev_bg(t)) for t in range(8)] +
                           [inproj_gen(li, "in", MAIN0, MAIN0 + Tm, ev_u(t)) for t in range(8)])
                for (c0, c1) in chunks(HALO, W):
                    n = c1 - c0
                    ln_stats(T1, s_T1, c0, c1, 1024)
                    for t in range(8):
                        j = t % 4
                        S.op("dve", [s_T1[t], s_CF32b], [s_FT[j]], lambda e: e.tensor_tensor(
                            out=FT[:, j, 0:n], in0=T1[:, t, c0:c1], in1=STT[:, 0, 0:n], op=ALU.subtract))
                        S.op("dve", [s_FT[j], s_CF32b], [s_FT[j]], lambda e: e.tensor_tensor(
                            out=FT[:, j, 0:n], in0=FT[:, j, 0:n], in1=STT[:, 1, 0:n], op=ALU.mult))
                        S.op("act", [s_FT[j], s_const], [s_T1[t]], lambda e: e.activation(
                            out=T1[:, t, c0:c1], in_=FT[:, j, 0:n], func=AF.Identity,
                            scale=ppc(8 + t), bias=ppc(16 + t)))
                        next(gq, None)
                for _ in gq:
                    pass
                s_VN = [St(), St()]
                inherit(s_VN, [s_T2])
                bank_lim[0] = 4

                def stage_T(blk):
                    tc0 = HALO + blk * 128
                    vb = blk % 2
                    p16, sp = (P16, [s_P16]) if vb == 0 else (P16B, [s_PB[4], s_PB[5]])
                    fns = [lambda e, h=h: e.transpose(p16[:, h * 128:(h + 1) * 128], T1[:, h, tc0:tc0 + 128], IDB[:])
                           for h in range(8)]
                    S.mm_group(s_T1 + [s_const], sp, fns)
                    S.op("dve", sp, [s_VN[vb]], lambda e: e.tensor_copy(out=VN[:, vb, :], in_=p16[:, 0:1024]))

                def stage_S(blk):
                    tc0 = HALO + blk * 128
                    vb = blk % 2
                    for hh in range(2):
                        b = nbank()
                        for h4 in range(4):
                            h = hh * 4 + h4
                            S.mm_group([s_VN[vb], s_CB16b], [s_PB[b]], [lambda e: e.matmul(
                                PB[b][:, h4 * 128:(h4 + 1) * 128], lhsT=VN[:, vb, h * 128:(h + 1) * 128],
                                rhs=WST[:, h, :], start=True, stop=True)])
                        j = hh
                        S.op("dve", [s_PB[b], s_CF32], [s_FT[j]], lambda e: e.tensor_tensor(
                            out=FT[:, j, :], in0=PB[b][:], in1=BSB[:, hh, :], op=ALU.add))
                        S.op("dve", [s_FT[j]] + s_T0[hh * 4:hh * 4 + 4], s_Y[hh * 4:hh * 4 + 4], lambda e: e.tensor_tensor(
                            out=Y[:, hh * 4:hh * 4 + 4, blk * 128:(blk + 1) * 128],
                            in0=FT[:, j, :].rearrange("p (a b) -> p a b", a=4),
                            in1=T0[:, hh * 4:hh * 4 + 4, tc0:tc0 + 128], op=ALU.mult))

                stage_T(0)
                for blk in range(NB):
                    if blk + 1 < NB:
                        stage_T(blk + 1)
                    stage_S(blk)
                bank_lim[0] = 6
                inherit([s_T2], s_VN)

            def branch_C():
                inherit(s_LT, [s_CF32b])
                S.dma("pool", d_cp, [s_CB16], [s_CB16], BT[:].rearrange("p a b c d -> p (a b c d)"), bt_in)
                S.op("act", [s_const], [s_SC], lambda e: e.activation(out=SC[:, 0:8], in_=ppc(48, 56), func=AF.Exp))
                for i8 in range(8):
                    S.op("dve", [s_SC, s_const], [s_CF32], lambda e: e.tensor_scalar(
                        out=SINKBC[:, i8 // 4, i8 % 4, :], in0=CST[:, 0:128], scalar1=0.0, scalar2=SC[:, i8:i8 + 1],
                        op0=ALU.mult, op1=ALU.add))
                for t in range(8):
                    inproj(li, "in", MAIN0, MAIN0 + Tm, lambda b, ps, c0, c1, t=t: S.op(
                        "act", [s_PB[b]], [s_T1[t]], lambda e: e.activation(
                            out=T1[:, t, tcol(c0):tcol(c1)], in_=ps, func=AF.Silu)))
                for t in range(8):
                    inproj(li, "in", MAIN0, MAIN0 + Tm, lambda b, ps, c0, c1, t=t: evac_copy(
                        b, ps, T0[:, t, tcol(c0):tcol(c1)], [s_T0[t]], scale=0.125))
                Te = MAIN0 + Tm
                for kvi in range(2):
                    inproj(li, "kd", 0, Te, lambda b, ps, c0, c1, kvi=kvi: evac_copy(
                        b, ps, KD[:, kvi, c0:c1], [s_T2]))
                inproj(li, "in", 0, Te, lambda b, ps, c0, c1: evac_copy(b, ps, VT[:, c0:c1], [s_T2]))
                S.op("dve", [], [s_T2], lambda e: e.memset(VA[:].rearrange("p a b c d -> p (a b c d)"), 0.0))
                nbe = NB + 1
                fns = [lambda e, bb=bb: e.transpose(P16[:, bb * 128:(bb + 1) * 128], VT[:, bb * 128:(bb + 1) * 128], IDB[:])
                       for bb in range(nbe)]
                S.mm_group([s_T2, s_const], [s_P16], fns)
                pv = P16[:, 0:nbe * 128].rearrange("p (a b) -> p a b", a=nbe)
                for kvi in range(2):
                    for var in range(2):
                        S.op("dve", [s_P16], [s_T2], lambda e: e.tensor_copy(
                            out=VA[:, 0:nbe, kvi, var, var * 64:var * 64 + 64], in_=pv[:, :, kvi * 64:kvi * 64 + 64]))
                PTB = [PT, FT[:, 2:4, :].bitcast(BF16).rearrange("p a (b c) -> p (a b) c", b=2)]
                sPTB = [[s_CB16b], [s_FT[2], s_FT[3]]]
                its = [(blk, kvi) for blk in range(1, NB + 1) for kvi in range(2)]

                def stage_L(it):
                    blk, kvi = its[it]
                    tc0 = HALO + (blk - 1) * 128
                    ptb, sptb = PTB[it % 2], sPTB[it % 2]
                    for kbi, kb in enumerate((blk - 1, blk)):
                        b0, b1 = kbi * 2, kbi * 2 + 1
                        fns = []
                        for var in range(2):
                            fns.append(lambda e, var=var, bb=kbi * 2 + var: e.matmul(
                                PB[bb][:].rearrange("p (a b) -> p a b", a=4),
                                lhsT=KD[var * 64:var * 64 + 64, kvi, kb * 128:(kb + 1) * 128],
                                rhs=T0[var * 64:var * 64 + 64, 4 * kvi:4 * kvi + 4, tc0:tc0 + 128],
                                start=True, stop=False))
                        for var in range(2):
                            fns.append(lambda e, var=var, bb=kbi * 2 + var: e.matmul(
                                PB[bb][:], lhsT=IDB[:], rhs=BT[:, kbi, kvi, var, :], start=False, stop=True))
                        S.mm_group([s_T2, s_CB16, s_const] + s_T0[4 * kvi:4 * kvi + 4], [s_PB[b0], s_PB[b1]], fns)
                        for var in range(2):
                            pidx = kbi * 2 + var
                            S.op("act", [s_PB[pidx]], sptb, lambda e: e.activation(
                                out=ptb[:, pidx, :], in_=PB[pidx][:], func=AF.Exp))

                def stage_V(it):
                    blk, kvi = its[it]
                    tc0 = HALO + (blk - 1) * 128
                    ptb, sptb = PTB[it % 2], sPTB[it % 2]
                    bo, bd = 4, 5
                    fo, fd = [], []
                    for kbi, kb in enumerate((blk - 1, blk)):
                        for var in range(2):
                            pidx = kbi * 2 + var
                            first, last = (pidx == 0), (pidx == 3)
                            fo.append(lambda e, kb=kb, var=var, pidx=pidx, first=first, last=last: e.matmul(
                                PB[bo][:], lhsT=VA[:, kb, kvi, var, :], rhs=ptb[:, pidx, :], start=first, stop=last))
                            hsel = 0 if (st == 0 and (kb == 0 or (fused and li == 0 and kb == 1))) else 1
                            fd.append(lambda e, hsel=hsel, var=var, pidx=pidx, first=first, last=last: e.matmul(
                                PB[bd][:], lhsT=ONESV[:, hsel, var, :], rhs=ptb[:, pidx, :], start=first, stop=last))
                    S.mm_group([s_T2] + sptb, [s_PB[bo]], fo)
                    S.mm_group([s_const] + sptb, [s_PB[bd]], fd)
                    S.op("dve", [s_PB[bd], s_CF32], [s_FT[0]], lambda e: e.tensor_tensor(
                        out=FT[:, 0, :], in0=PB[bd][:], in1=SINKBC[:, kvi].rearrange("p a b -> p (a b)"), op=ALU.add))
                    S.op("dve", [s_FT[0]], [s_FT[0]], lambda e: e.reciprocal(out=FT[:, 0, :], in_=FT[:, 0, :]))
                    S.op("dve", [s_PB[bo], s_FT[0]], [s_FT[1]], lambda e: e.tensor_tensor(
                        out=FT[:, 1, :], in0=PB[bo][:], in1=FT[:, 0, :], op=ALU.mult))
                    S.op("dve", [s_FT[1]] + s_T1[4 * kvi:4 * kvi + 4], s_Y[4 * kvi:4 * kvi + 4], lambda e: e.tensor_tensor(
                        out=Y[:, 4 * kvi:4 * kvi + 4, (blk - 1) * 128:blk * 128],
                        in0=FT[:, 1, :].rearrange("p (a b) -> p a b", a=4),
                        in1=T1[:, 4 * kvi:4 * kvi + 4, tc0:tc0 + 128], op=ALU.mult))

                stage_L(0)
                for it in range(len(its)):
                    if it + 1 < len(its):
                        stage_L(it + 1)
                    stage_V(it)
                bank_rr[0] = 0
                inherit([s_CF32b], s_LT)

            def branch_D():
                for t in range(8):
                    inproj(li, "in", 96, MAIN0 + Tm, lambda b, ps, c0, c1, t=t: S.op(
                        "act", [s_PB[b]], [s_T0[t]], lambda e: e.activation(
                            out=T0[:, t, tcol(c0):tcol(c1)], in_=ps, func=AF.Sigmoid)))
                for t in range(8):
                    inproj(li, "in", 96, MAIN0 + Tm, lambda b, ps, c0, c1, t=t: S.op(
                        "dve", [s_PB[b], s_T0[t]], [s_T0[t]], lambda e: e.tensor_tensor(
                            out=T0[:, t, tcol(c0):tcol(c1)], in0=ps, in1=T0[:, t, tcol(c0):tcol(c1)], op=ALU.mult)))
                s_DG2 = St()
                inherit([s_DG2], s_T1)
                for t in range(8):
                    cw = PP[:, l, 56 + t * 31:56 + (t + 1) * 31]
                    DG, sdg = (DIAG, s_CB16) if t % 2 == 0 else (DIAG2, s_DG2)
                    S.op("dve", [s_const], [sdg], lambda e: e.tensor_tensor(
                        out=DG[:], in0=IDB[:].unsqueeze(1).broadcast_to([128, 31, 128]),
                        in1=cw.unsqueeze(2).broadcast_to([128, 31, 128]), op=ALU.mult))
                    for (c0, c1) in reversed(chunks(HALO, W)):
                        n = c1 - c0
                        b = nbank()
                        fns = [lambda e, j=j: e.matmul(PB[b][:, 0:n], lhsT=DG[:, j, :],
                                                       rhs=T0[:, t, c0 - 30 + j:c1 - 30 + j], start=(j == 0), stop=(j == 30))
                               for j in range(31)]
                        S.mm_group([s_T0[t], sdg], [s_PB[b]], fns)
                        S.op("act", [s_PB[b], s_const], [s_T0[t]], lambda e: e.activation(
                            out=T0[:, t, c0:c1], in_=PB[b][:, 0:n], func=AF.Identity, bias=ppc(24 + t)))
                inherit(s_T1, [s_DG2])

                def ev_dg(t):
                    return lambda b, ps, c0, c1: S.op(
                        "act", [s_PB[b]], [s_T1[t]], lambda e: e.activation(
                            out=T1[:, t, tcol(c0):tcol(c1)], in_=ps, func=AF.Silu))

                gq = chain([inproj_gen(li, "in", MAIN0, MAIN0 + Tm, ev_dg(t)) for t in range(8)])
                for (c0, c1) in chunks(HALO, W):
                    n = c1 - c0
                    ln_stats(T0, s_T0, c0, c1, 1024)
                    for t in range(8):
                        fa = t % 4
                        S.op("dve", [s_T0[t], s_CF32b], [s_FT[fa]], lambda e: e.tensor_tensor(
                            out=FT[:, fa, 0:n], in0=T0[:, t, c0:c1], in1=STT[:, 0, 0:n], op=ALU.subtract))
                        S.op("dve", [s_FT[fa], s_CF32b], [s_FT[fa]], lambda e: e.tensor_tensor(
                            out=FT[:, fa, 0:n], in0=FT[:, fa, 0:n], in1=STT[:, 1, 0:n], op=ALU.mult))
                        S.op("act", [s_FT[fa], s_const], [s_T0[t]], lambda e: e.activation(
                            out=T0[:, t, c0:c1], in_=FT[:, fa, 0:n], func=AF.Silu, scale=ppc(32 + t), bias=ppc(40 + t)))
                        next(gq, None)
                for _ in gq:
                    pass
                for t in range(8):
                    S.op("dve", [s_T0[t], s_T1[t]], [s_Y[t]], lambda e: e.tensor_tensor(
                        out=Y[:, t, 0:Tm], in0=T0[:, t, HALO:W], in1=T1[:, t, HALO:W], op=ALU.mult))

            inherit(s_T0, [s_R[0], s_R[1]])
            inherit(s_T1, [s_LN])
            inherit([s_T2], [s_R[2], s_XB, s_XBS[1]])
            bank_lim[0] = 6
            if len(branches) < 4:
                for d_ in range(16):
                    S.op("dve", [], [s_MG[d_]], lambda e: e.memset(MG[:, d_, :], 0.0))
                first_merge[0] = False
            brs = [branch_A, branch_B, branch_C, branch_D]
            nin = [16, 24, 19, 24]
            for i in range(4):
                if i in branches:
                    brs[i]()
                    phase2(i)
                else:
                    for _ in range(nin[i] + 24):
                        prefetch()
                        wstate["used"] += 1
                        prefetch()

            inherit([s_LN], s_T1)
            S.dma("sp", d_ln, [s_LN], [s_LN], LNG[:], lng_in[l])
            S.dma("sp", d_ln, [s_LN], [s_LN], LNB[:], lnb_in[l])
            inherit([s_R[0], s_R[1]], s_T0)
            inherit([s_R[2], s_XB, s_XBS[1]], [s_T2])
            bank_lim[0] = 4
            if li == 0 and first_in_prog:
                rsrc = lambda blk: x_in[st, blk * 128:(blk + 1) * 128, :]
                rst = None
            else:
                rsrc = lambda blk: x1s[st, (blk - 1) * 128:blk * 128, :]
                rst = s_x1s[st]

            def load_resid(blk):
                rb = blk % 3
                S.dma("sp", d_r[rb], [rst] if rst is not None else [], [s_R[rb]], R[rb][:], rsrc(blk))

            load_resid(1)
            load_resid(2)
            load_resid(3)
            for e_ in range(16):
                uio = next_unit(li, "out")
                sl = uio % NW
                for (c0, c1) in mchunks:
                    n = c1 - c0
                    b = nbank()
                    fns = [lambda e, k=k: e.matmul(PB[b][:, 0:n], lhsT=WR[:, sl, k * 128:(k + 1) * 128],
                                                   rhs=MG[:, k, c0 - MAIN0:c1 - MAIN0], start=(k == 0), stop=(k == KT - 1))
                           for k in range(KT)]
                    S.mm_group([s_WR[sl]] + s_MG, [s_PB[b]], fns)
                    evac_copy(b, PB[b][:, 0:n], xT[:, e_, c0:c1], xblocks(c0, c1))
                release(uio)
            sshift = 1 if (fused and li == 0 and st == 0) else 0

            def stage_A(blk):
                rb = blk % 3
                fns = [lambda e, k=k: e.transpose(P16[:, k * 128:(k + 1) * 128], xT[:, k, blk * 128:(blk + 1) * 128], IDB[:])
                       for k in range(KT)]
                S.mm_group([s_xT[blk], s_const], [s_P16], fns)
                S.op("dve", [s_P16, s_R[rb]], [s_R[rb]], lambda e: e.scalar_tensor_tensor(
                    out=R[rb][:], in0=R[rb][:], scalar=ALPHA, in1=P16[:], op0=ALU.mult, op1=ALU.add))
                o = 8 * (blk % 3)
                ssc = s_SCP[blk % 3]
                S.op("dve", [], [ssc], lambda e: e.memset(SC[:, o + 8:o + 10], 0.0))
                S.op("act", [s_R[rb]], s_Y + [ssc], lambda e: e.activation(
                    out=JK[:], in_=R[rb][:], func=AF.Identity, accum_out=SC[:, o + 8:o + 9]))
                S.op("act", [s_R[rb]], s_Y + [ssc], lambda e: e.activation(
                    out=JK[:], in_=R[rb][:], func=AF.Square, accum_out=SC[:, o + 9:o + 10]))

            def stage_A2(blk):
                o = 8 * (blk % 3)
                ssc = s_SCP[blk % 3]
                S.op("dve", [ssc], [ssc], lambda e: e.tensor_scalar(
                    out=SC[:, o + 10:o + 12], in0=SC[:, o + 8:o + 10], scalar1=1.0 / D, scalar2=None, op0=ALU.mult))
                S.op("dve", [ssc], [ssc], lambda e: e.tensor_tensor(
                    out=SC[:, o + 12:o + 13], in0=SC[:, o + 10:o + 11], in1=SC[:, o + 10:o + 11], op=ALU.mult))
                S.op("dve", [ssc], [ssc], lambda e: e.tensor_tensor(
                    out=SC[:, o + 13:o + 14], in0=SC[:, o + 11:o + 12], in1=SC[:, o + 12:o + 13], op=ALU.subtract))
                S.op("dve", [ssc], [ssc], lambda e: e.tensor_scalar(
                    out=SC[:, o + 14:o + 15], in0=SC[:, o + 13:o + 14], scalar1=LN_EPS, scalar2=None, op0=ALU.add))
                S.op("act", [ssc], [ssc], lambda e: e.activation(out=SC[:, o + 14:o + 15], in_=SC[:, o + 14:o + 15], func=AF.Sqrt))

            def stage_B(blk):
                rb = blk % 3
                o = 8 * (blk % 3)
                ssc = s_SCP[blk % 3]
                S.op("dve", [ssc], [ssc], lambda e: e.reciprocal(out=SC[:, o + 14:o + 15], in_=SC[:, o + 14:o + 15]))
                S.op("dve", [ssc, s_LN, s_R[rb]], [s_R[rb]], lambda e: e.scalar_tensor_tensor(
                    out=R[rb][:], in0=R[rb][:], scalar=SC[:, o + 10:o + 11], in1=LNG[:],
                    op0=ALU.subtract, op1=ALU.mult))
                S.op("dve", [ssc, s_LN, s_R[rb]], [s_R[rb]], lambda e: e.scalar_tensor_tensor(
                    out=R[rb][:], in0=R[rb][:], scalar=SC[:, o + 14:o + 15], in1=LNB[:],
                    op0=ALU.mult, op1=ALU.add))
                if last_in_prog:
                    S.dma("sp", d_o[rb], [s_R[rb]], [], y_out[st, (blk - 1) * 128:blk * 128, :], R[rb][:])
                else:
                    if blk - sshift >= 1:
                        S.dma("sp", d_o[rb], [s_R[rb]], [s_x1s[st]],
                              x1s[st, (blk - 1 - sshift) * 128:(blk - sshift) * 128, :], R[rb][:])
                    xi = blk % 2
                    xb, sxb = XBS[xi], s_XBS[xi]
                    if blk == 1 and st == 0:
                        S.op("act", [s_R[rb], s_const], [sxb], lambda e: e.activation(
                            out=xb[:], in_=R[rb][:], func=AF.Copy, scale=FLAG))
                    else:
                        S.op("act", [s_R[rb]], [sxb], lambda e: e.activation(out=xb[:], in_=R[rb][:], func=AF.Copy))

            def stage_C(blk):
                if last_in_prog:
                    return
                xi = blk % 2
                xb, sxb = XBS[xi], s_XBS[xi]
                sp16 = [s_PB[4], s_PB[5]]
                fns = [lambda e, k=k: e.transpose(P16B[:, k * 128:(k + 1) * 128], xb[:, k * 128:(k + 1) * 128], IDB[:])
                       for k in range(KT)]
                S.mm_group([sxb, s_const], sp16, fns)
                sl_ = blk - sshift
                if sl_ == 0 and sshift == 0:
                    return
                S.op("dve", sp16, [s_xT[sl_]], lambda e: e.tensor_copy(
                    out=xT[:, :, sl_ * 128:(sl_ + 1) * 128], in_=P16B.rearrange("p (k c) -> p k c", k=KT)))
                if sshift == 1 and blk == NB:
                    S.op("dve", sp16, [s_xT[9]], lambda e: e.tensor_copy(
                        out=xT[:, :, 9 * 128:10 * 128], in_=P16B.rearrange("p (k c) -> p k c", k=KT)))

            for i in range(1, NB + 4):
                if i <= NB:
                    stage_A(i)
                if 1 <= i - 1 <= NB:
                    stage_A2(i - 1)
                if 1 <= i - 2 <= NB:
                    stage_B(i - 2)
                    if (i - 2) + 3 <= NB:
                        load_resid((i - 2) + 3)
                if 1 <= i - 3 <= NB:
                    stage_C(i - 3)
                if hook is not None:
                    hook(i)

        for st in range(NST):
            if fused:
                emit_layer(st, 0, 9 if st == 0 else 8, True, False, skip_phase0=(st == 1))
                if st == 1:
                    S.op("dve", [s_xT[9]], [s_xT[0]], lambda e: e.tensor_copy(
                        out=xT[:, :, 0:128], in_=xT[:, :, 9 * 128:10 * 128]))
                hk = None
                if st == 0:
                    def hk(i):
                        if i == 1:
                            phase0_block(1, 0)
                            phase0_block(1, 1)
                        elif 2 <= i <= 8:
                            phase0_block(1, i)
                emit_layer(st, 1, 8, False, True, hook=hk)
            else:
                emit_layer(st, 0, 8, True, True)
        allst = s_R + s_x1s + s_xT + s_MG + s_Y + s_T0 + s_T1 + [s_T2, s_P16] + s_PB + s_WR
        S.wait_all("sp", allst)
    return nc


def _consts(core, fused):
    cst = np.zeros((128, 384), np.float32)
    cst[:, 0:128] = np.eye(128, dtype=np.float32)
    s = np.arange(128)[:, None]
    t = np.arange(128)[None, :]
    cst[:, 128:256] = (s <= t).astype(np.float32)
    cst[:, 256] = 0.0 if core == 0 else 1.0
    for g in range(4):
        win = 2 ** (g + 1)
        tt = np.arange(16)
        if core == 0:
            cst[:, 272 + g * 16:272 + (g + 1) * 16] = (1.0 / np.minimum(tt + 1, win)).astype(np.float32)[None, :]
        else:
            cst[:, 272 + g * 16:272 + (g + 1) * 16] = np.float32(1.0 / win)
    return cst


def _layer_inputs(ls, w_in, pool_w, pool_scale, sgu_ln_g, sgu_ln_b, sgu_w, sgu_b, attn_sinks, rel_bias,
                  conv_w, conv_b, conv_ln_g, conv_ln_b, w_branch, w_out, ln_g, ln_b):
    nl = len(ls)
    ws = np.stack([build_wstream(w_in[l], w_branch[l], w_out[l]) for l in ls])
    pp = np.stack([build_pp(l, pool_scale, sgu_ln_g, sgu_ln_b, conv_b, conv_ln_g, conv_ln_b, attn_sinks, conv_w)
                   for l in ls], axis=1)
    pw = np.stack([pool_w[l].reshape(4, 2, 128, 256).transpose(2, 0, 1, 3).reshape(128, 2048) for l in ls])
    wst = np.stack([sgu_w[l].transpose(2, 0, 1).reshape(128, 1024) for l in ls])
    bsb = np.stack([np.broadcast_to(sgu_b[l].reshape(1, 1024), (128, 1024)) for l in ls])
    lng = np.stack([np.broadcast_to(ln_g[l][None, :], (128, D)) for l in ls])
    lnb = np.stack([np.broadcast_to(ln_b[l][None, :], (128, D)) for l in ls])
    return dict(ws=np.ascontiguousarray(ws), pp=np.ascontiguousarray(pp), pw=np.ascontiguousarray(pw),
                wst=np.ascontiguousarray(wst), bsb=np.ascontiguousarray(bsb), lng=np.ascontiguousarray(lng),
                lnb=np.ascontiguousarray(lnb), bt=build_bias_table(rel_bias))


def _shard_x(x2d, nblk_front):
    pad = nblk_front * 128
    xp = np.concatenate([np.zeros((pad, D), np.float32), x2d], axis=0)
    out = []
    for c in range(NCORE):
        sts = []
        for st in range(NST):
            t0 = c * TOK_CORE + st * ST_TOK
            if nblk_front == 2 and st == 1:
                blk = np.zeros((pad + ST_TOK, D), np.float32)
                blk[0:128 + ST_TOK] = xp[t0 + 128:t0 + pad + ST_TOK]
                sts.append(blk)
            else:
                sts.append(xp[t0:t0 + pad + ST_TOK])
        out.append(np.ascontiguousarray(np.stack(sts)))
    return out


_PROG = {}


def _get_prog(key, *a):
    if key not in _PROG:
        _PROG[key] = build_program(*a)
    return _PROG[key]


FUSED = True


def kernel(x, w_in, pool_w, pool_scale, sgu_ln_g, sgu_ln_b, sgu_w, sgu_b, attn_sinks, rel_bias,
           conv_w, conv_b, conv_ln_g, conv_ln_b, w_branch, w_out, ln_g, ln_b):
    args = [np.asarray(a, np.float32) for a in (w_in, pool_w, pool_scale, sgu_ln_g, sgu_ln_b, sgu_w, sgu_b,
                                                 attn_sinks, rel_bias, conv_w, conv_b, conv_ln_g, conv_ln_b,
                                                 w_branch, w_out, ln_g, ln_b)]
    x2d = np.asarray(x, np.float32).reshape(SEQ, D)
    if FUSED:
        nc = _get_prog("fused", [0, 1], True)
        li = _layer_inputs([0, 1], *args)
        xs = _shard_x(x2d, 2)
        in_maps = [dict(li, x_in=xs[c], cst=_consts(c, True)) for c in range(NCORE)]
        res = run_bass_kernel_spmd(nc, in_maps, core_ids=list(range(NCORE)))
        out = np.concatenate([r["y_out"].reshape(TOK_CORE, D) for r in res.results], axis=0)
        return out.reshape(1, SEQ, D)
    cur = x2d
    for l in range(2):
        nc = _get_prog("single", [0], False)
        li = _layer_inputs([l], *args)
        xs = _shard_x(cur, 1)
        in_maps = [dict(li, x_in=xs[c], cst=_consts(c, False)) for c in range(NCORE)]
        res = run_bass_kernel_spmd(nc, in_maps, core_ids=list(range(NCORE)))
        cur = np.concatenate([r["y_out"].reshape(TOK_CORE, D) for r in res.results], axis=0)
    return cur.reshape(1, SEQ, D)
```

```python
import contextlib
import numpy as np
import concourse.bass as bass
import concourse.mybir as mybir
from concourse.bass_utils import run_bass_kernel_spmd

F32 = mybir.dt.float32
BF16 = mybir.dt.bfloat16
AF = mybir.ActivationFunctionType
ALU = mybir.AluOpType

D = 2048
SEQ = 16384
NCORE = 8
TOK_CORE = SEQ // NCORE
NST = 2
ST_TOK = TOK_CORE // NST
KT = D // 128
ALPHA = (2 * 2) ** 0.25
LN_EPS = 1e-5
NEG = -30000.0
NW = 5
HALO = 32
PPC = 56 + 8 * 31

O_AIN, O_AG, O_U, O_V, O_BG, O_Q, O_K, O_VV, O_CG, O_DV, O_DG, O_DGATE, O_GL = (
    0, 1024, 2048, 3072, 4096, 5120, 6144, 6272, 6400, 7424, 8448, 9472, 10496)


def layer_units():
    u = []
    t8 = lambda base: [("in", base + 128 * t) for t in range(8)]
    def gates(i):
        r = []
        for d in range(16):
            if d % 2 == 0:
                r.append(("br", i, d))
            r.append(("in", O_GL + i * 2048 + d * 128))
        return r
    u += t8(O_AIN) + t8(O_AG) + gates(0)
    u += t8(O_V) + t8(O_BG) + t8(O_U) + gates(1)
    u += t8(O_CG) + t8(O_Q) + [("kd", 0), ("kd", 1), ("in", O_VV)] + gates(2)
    u += t8(O_DG) + t8(O_DV) + t8(O_DGATE) + gates(3)
    u += [("out", e) for e in range(16)]
    return u


UNITS = layer_units()
NU = len(UNITS)


def build_wstream(w_in, w_branch, w_out):
    ws = np.empty((NU, 128, 2048), np.float32)
    wk = w_in.reshape(KT, 128, -1)
    for n, un in enumerate(UNITS):
        if un[0] == "in":
            c = un[1]
            ws[n] = wk[:, :, c:c + 128].transpose(1, 0, 2).reshape(128, 2048)
        elif un[0] == "kd":
            c = O_K + 64 * un[1]
            blk = wk[:, :, c:c + 64]
            ws[n] = np.concatenate([blk, blk], axis=2).transpose(1, 0, 2).reshape(128, 2048)
        elif un[0] == "br":
            _, i, d = un
            wb = w_branch[i].reshape(8, 128, 2048)
            a = wb[:, :, d * 128:(d + 1) * 128].transpose(1, 0, 2).reshape(128, 1024)
            b = wb[:, :, (d + 1) * 128:(d + 2) * 128].transpose(1, 0, 2).reshape(128, 1024)
            ws[n] = np.concatenate([a, b], axis=1)
        else:
            e = un[1]
            wo = w_out.reshape(KT, 128, 2048)
            ws[n] = wo[:, :, e * 128:(e + 1) * 128].transpose(1, 0, 2).reshape(128, 2048)
    return ws


def t5_bucket_np(n):
    max_exact = 16
    nf = np.maximum(n, 1).astype(np.float32)
    large = max_exact + (np.log(nf / np.float32(max_exact)) / np.float32(np.log(128 / max_exact))
                         * np.float32(32 - max_exact)).astype(np.int32)
    large = np.minimum(large, 31)
    return np.where(n < max_exact, n, large)


def _bucket_table():
    return t5_bucket_np(np.arange(128))


def build_bias_table(rel_bias):
    bk = _bucket_table()
    kk = np.arange(128)[:, None]
    qq = np.arange(128)[None, :]
    bt = np.full((128, 2, 2, 2, 4, 128), NEG, np.float32)
    for kb in range(2):
        dist = qq - kk + (128 if kb == 0 else 0)
        valid = (dist >= 0) & (dist < 128)
        idx = bk[np.clip(dist, 0, 127)]
        for kv in range(2):
            for var in range(2):
                for j in range(4):
                    h = kv * 8 + 2 * j + var
                    vals = rel_bias[idx, h]
                    bt[:, kb, kv, var, j, :] = np.where(valid, vals, np.float32(NEG))
    return bt.reshape(128, 4096)


def build_pp(l, pool_scale, sgu_ln_g, sgu_ln_b, conv_b, conv_ln_g, conv_ln_b, attn_sinks, conv_w):
    pp = np.zeros((128, PPC), np.float32)
    col = lambda v: v.reshape(8, 128).T
    pp[:, 0:8] = col(pool_scale[l])
    pp[:, 8:16] = col(sgu_ln_g[l])
    pp[:, 16:24] = col(sgu_ln_b[l])
    pp[:, 24:32] = col(conv_b[l])
    pp[:, 32:40] = col(conv_ln_g[l])
    pp[:, 40:48] = col(conv_ln_b[l])
    for kv in range(2):
        for j in range(4):
            pp[0:64, 48 + kv * 4 + j] = attn_sinks[l, kv * 8 + 2 * j]
            pp[64:128, 48 + kv * 4 + j] = attn_sinks[l, kv * 8 + 2 * j + 1]
    cw = conv_w[l].reshape(31, 8, 128)
    pp[:, 56:] = cw.transpose(2, 1, 0).reshape(128, 8 * 31)
    return pp


class St:
    __slots__ = ("w", "r")

    def __init__(self):
        self.w = {}
        self.r = {}


def _merge(dst, src):
    for k, v in src.items():
        if dst.get(k, 0) < v:
            dst[k] = v


class Sync:
    def __init__(self, nc, es):
        self.nc = nc
        self.es = es
        self.engs = {"pe": nc.tensor, "act": nc.scalar, "dve": nc.vector, "pool": nc.gpsimd, "sp": nc.sync}
        self.sems = {}
        self.cnt = {}
        for e in ("pe", "act", "dve", "pool"):
            self.sems[e] = es.enter_context(nc.semaphore("s_" + e))
            self.cnt[e] = 0
        self.known = {e: {} for e in self.engs}
        self.ndma = 0

    def new_dma_sem(self, name):
        nm = "d_" + name
        self.sems[nm] = self.es.enter_context(self.nc.semaphore(nm))
        self.cnt[nm] = 0
        return nm

    def _wait(self, eng, toks):
        kn = self.known[eng]
        for s, v in toks.items():
            if eng == "pe" and s == "pe":
                continue
            if kn.get(s, 0) < v:
                self.engs[eng].wait_ge(self.sems[s], v)
                kn[s] = v

    def _deps(self, reads, writes):
        toks = {}
        for s in reads:
            _merge(toks, s.w)
        for s in writes:
            _merge(toks, s.w)
            _merge(toks, s.r)
        return toks

    def op(self, eng, reads, writes, fn):
        self._wait(eng, self._deps(reads, writes))
        inst = fn(self.engs[eng])
        self.cnt[eng] += 1
        inst.then_inc(self.sems[eng], 1)
        tok = {eng: self.cnt[eng]}
        for s in reads:
            _merge(s.r, tok)
        for s in writes:
            s.w = dict(tok)
            s.r = {}
        return tok

    def mm_group(self, reads, writes, fns):
        self._wait("pe", self._deps(reads, writes))
        for f in fns[:-1]:
            f(self.engs["pe"])
        inst = fns[-1](self.engs["pe"])
        self.cnt["pe"] += 1
        inst.then_inc(self.sems["pe"], 1)
        tok = {"pe": self.cnt["pe"]}
        for s in reads:
            _merge(s.r, tok)
        for s in writes:
            s.w = dict(tok)
            s.r = {}
        return tok

    def dma(self, q, dsem, reads, writes, out, in_):
        toks = self._deps(reads, writes)
        if self.cnt[dsem] > 0:
            _merge(toks, {dsem: self.cnt[dsem]})
        self._wait(q, toks)
        inst = self.engs[q].dma_start(out=out, in_=in_)
        self.cnt[dsem] += 16
        inst.then_inc(self.sems[dsem], 16)
        tok = {dsem: self.cnt[dsem]}
        for s in reads:
            _merge(s.r, tok)
        for s in writes:
            s.w = dict(tok)
            s.r = {}
        self.ndma += 1
        return tok

    def wait_all(self, eng, states):
        toks = {}
        for s in states:
            _merge(toks, s.w)
            _merge(toks, s.r)
        self._wait(eng, toks)


def inherit(dst, src):
    for d_ in dst:
        for s in src:
            _merge(d_.w, s.w)
            _merge(d_.w, s.r)


def chunks(lo, hi, mx=512):
    n = hi - lo
    k = -(-n // mx)
    base = -(-n // k)
    base = -(-base // 8) * 8
    out = []
    c = lo
    while c < hi:
        out.append((c, min(hi, c + base)))
        c += base
    return out


def build_program(layers, fused, branches=(0, 1, 2, 3)):
    nl = len(layers)
    NB1 = 9 if fused else 8
    XROWS = (NB1 + 1) * 128
    nc = bass.Bass("TRN2", target_bir_lowering=False)
    dt = nc.dram_tensor
    x_in = dt("x_in", [NST, XROWS, D], F32, kind="ExternalInput").ap()
    ws = dt("ws", [nl, NU, 128, 2048], F32, kind="ExternalInput").ap()
    pp_in = dt("pp", [128, nl, PPC], F32, kind="ExternalInput").ap()
    pw_in = dt("pw", [nl, 128, 2048], F32, kind="ExternalInput").ap()
    wst_in = dt("wst", [nl, 128, 1024], F32, kind="ExternalInput").ap()
    bsb_in = dt("bsb", [nl, 128, 1024], F32, kind="ExternalInput").ap()
    lng_in = dt("lng", [nl, 128, 2048], F32, kind="ExternalInput").ap()
    lnb_in = dt("lnb", [nl, 128, 2048], F32, kind="ExternalInput").ap()
    bt_in = dt("bt", [128, 4096], F32, kind="ExternalInput").ap()
    cst_in = dt("cst", [128, 384], F32, kind="ExternalInput").ap()
    y_out = dt("y_out", [NST, ST_TOK, D], F32, kind="ExternalOutput").ap()
    x1s = dt("x1s", [NST, 8 * 128, D], F32, kind="Internal").ap() if fused else None

    es = contextlib.ExitStack()
    with es:
        S = Sync(nc, es)
        off = [17536]

        def alloc(name, shape, dtype, at=None):
            nbytes = int(np.prod(shape[1:])) * (2 if dtype == BF16 else 4)
            if at is None:
                at = off[0]
                off[0] += (nbytes + 63) // 64 * 64
                assert off[0] <= 229344, (name, off[0])
            return nc.alloc_sbuf_tensor_at(name, list(shape), dtype, offset=at), at

        TW = HALO + NB1 * 128 if fused else HALO + 9 * 128
        TW = HALO + 9 * 128
        xT, _ = alloc("xT", [128, KT, 1280], BF16)
        MG, _ = alloc("MG", [128, KT, 1152], BF16)
        WR, _ = alloc("WR", [128, NW, 2048], BF16)
        Y, aY = alloc("Y", [128, 8, 1152], BF16)
        T0, aT0 = alloc("T0", [128, 8, TW], BF16)
        T1, aT1 = alloc("T1", [128, 8, TW], BF16)
        T2, aT2 = alloc("T2", [128, 9216], BF16)
        CB16, aCB16 = alloc("CB16", [128, 4096], BF16)
        CB16b, aCB16b = alloc("CB16b", [128, 2048], BF16)
        CF32, aCF32 = alloc("CF32", [128, 1024], F32)
        CF32b, aCF32b = alloc("CF32b", [128, 1024], F32)
        FT, _ = alloc("FT", [128, 4, 512], F32)
        SQ, _ = alloc("SQ", [128, 2, 512], BF16)
        PP, _ = alloc("PP", [128, nl, PPC], F32)
        IDB, _ = alloc("IDB", [128, 128], BF16)
        ONESV, _ = alloc("ONESV", [128, 2, 2, 128], BF16)
        ONES, _ = alloc("ONES", [128, 128], BF16)
        CST, _ = alloc("CST", [128, 384], F32)
        SC, _ = alloc("SC", [128, 32], F32)
        FX, _ = alloc("FX", [128, 2, 16], F32)
        R = [alloc("R0", [128, 2048], F32, at=aT0)[0], alloc("R1", [128, 2048], F32, at=aT0 + 8192)[0],
             alloc("R2", [128, 2048], F32, at=aT2)[0]]
        XB, _ = alloc("XB", [128, 2048], BF16, at=aT2 + 8192)
        XB2, _ = alloc("XB2", [128, 2048], BF16, at=aT2 + 12288)
        JK, _ = alloc("JK", [128, 2048], BF16, at=aY)
        XBS = [XB, XB2]
        VN, _ = alloc("VN", [128, 2, 1024], BF16, at=aT2 + 8192)
        LNG, _ = alloc("LNG", [128, 2048], F32, at=aT1)
        LNB, _ = alloc("LNB", [128, 2048], F32, at=aT1 + 8192)
        PQ, _ = alloc("PQ", [128, 2, TW], BF16, at=aT2)
        QQ, _ = alloc("QQ", [128, 2, TW], BF16, at=aY)
        KD, _ = alloc("KD", [128, 2, 1280], BF16, at=aT2)
        VT, _ = alloc("VT", [128, 1280], BF16, at=aT2 + 5120)
        VA, _ = alloc("VA", [128, 10, 2, 2, 128], BF16, at=aT2 + 7680)
        BT, _ = alloc("BT", [128, 2, 2, 2, 512], BF16, at=aCB16)
        DIAG, _ = alloc("DIAG", [128, 31, 128], BF16, at=aCB16)
        DIAG2, _ = alloc("DIAG2", [128, 31, 128], BF16, at=aT1)
        PW, _ = alloc("PW", [128, 4, 2, 256], BF16, at=aCB16b)
        WST, _ = alloc("WST", [128, 8, 128], BF16, at=aCB16b)
        PT, _ = alloc("PT", [128, 4, 512], BF16, at=aCB16b)
        BSB, _ = alloc("BSB", [128, 2, 512], F32, at=aCF32)
        SINKBC, _ = alloc("SINKBC", [128, 2, 4, 128], F32, at=aCF32)
        LT, _ = alloc("LT", [128, 2, 512], F32, at=aCF32b)
        STT, _ = alloc("STT", [128, 2, 512], F32, at=aCF32b)

        PBALL = es.enter_context(nc.psum_tensor("pball", [128, 8, 512], F32))
        PB = [PBALL[:, i, :] for i in range(6)]
        P16 = PBALL[:, 6:8, :].bitcast(BF16).rearrange("p a b -> p (a b)")
        P16B = PBALL[:, 4:6, :].bitcast(BF16).rearrange("p a b -> p (a b)")

        s_xT = [St() for _ in range(10)]
        s_MG = [St() for _ in range(16)]
        s_WR = [St() for _ in range(NW)]
        s_Y = [St() for _ in range(8)]
        s_T0 = [St() for _ in range(8)]
        s_T1 = [St() for _ in range(8)]
        s_T2 = St()
        s_CB16, s_CB16b, s_CF32, s_CF32b = St(), St(), St(), St()
        s_FT = [St(), St(), St(), St()]
        s_LT = [St(), St()]
        s_SQ = [St(), St()]
        s_PB = [St() for _ in range(6)]
        s_P16 = St()
        s_const = St()
        s_SC = St()
        s_SCP = [St(), St(), St()]
        s_FX = St()
        s_R = [St(), St(), St()]
        s_XB = St()
        s_XBS = [s_XB, St()]
        P16S = [(P16, None), (P16B, None)]
        s_LN = St()
        s_x1s = [St() for _ in range(NST)]
        d_w = [S.new_dma_sem("w%d" % i) for i in range(NW)]
        d_r = [S.new_dma_sem("r%d" % i) for i in range(3)]
        d_o = [S.new_dma_sem("o%d" % i) for i in range(3)]
        d_c = S.new_dma_sem("c")
        d_c2 = S.new_dma_sem("c2")
        d_cp = S.new_dma_sem("cp")
        d_ln = S.new_dma_sem("ln")

        bank_rr = [0]
        bank_lim = [6]

        def nbank():
            b = bank_rr[0] % bank_lim[0]
            bank_rr[0] = (b + 1) % bank_lim[0]
            return b

        wseq = []
        for _st in range(NST):
            for li in range(nl):
                for n in range(NU):
                    wseq.append((li, n))
        wstate = {"loaded": 0, "used": 0}
        wlive = set()

        def release(i):
            wlive.discard(i)
            prefetch()

        def prefetch():
            oldest = min(wlive) if wlive else wstate["used"]
            while wstate["loaded"] < len(wseq) and wstate["loaded"] < oldest + NW:
                i = wstate["loaded"]
                li, n = wseq[i]
                sl = i % NW
                S.dma("pool", d_w[sl], [], [s_WR[sl]], WR[:, sl, :], ws[li, n])
                wstate["loaded"] += 1

        def next_unit(li, kind):
            i = wstate["used"]
            assert wseq[i][0] == li and UNITS[wseq[i][1]][0] == kind, (wseq[i], UNITS[wseq[i][1]], kind)
            prefetch()
            wstate["used"] += 1
            wlive.add(i)
            return i

        S.dma("sp", d_c, [], [s_const], PP[:], pp_in)
        S.dma("sp", d_c2, [], [s_const], CST[:], cst_in)
        S.dma("pool", d_cp, [], [s_const], IDB[:], cst_in[:, 0:128])
        MASK = CST[:, 128:256]
        FLAG = CST[:, 256:257]
        POOLFIX = CST[:, 272:336]
        S.op("dve", [], [s_const], lambda e: e.memset(ONES[:], 1.0))
        S.op("dve", [], [s_const], lambda e: e.memset(ONESV[:], 0.0))
        S.op("dve", [], [s_const], lambda e: e.memset(ONESV[:, 1, 0, 0:64], 1.0))
        S.op("dve", [], [s_const], lambda e: e.memset(ONESV[:, 1, 1, 64:128], 1.0))
        S.op("dve", [s_const], [s_const], lambda e: e.tensor_scalar(
            out=ONESV[:, 0, 0, 0:64], in0=ONESV[:, 1, 0, 0:64], scalar1=FLAG, scalar2=None, op0=ALU.mult))
        S.op("dve", [s_const], [s_const], lambda e: e.tensor_scalar(
            out=ONESV[:, 0, 1, 64:128], in0=ONESV[:, 1, 1, 64:128], scalar1=FLAG, scalar2=None, op0=ALU.mult))
        prefetch()

        def xblocks(c0, c1):
            return s_xT[c0 // 128:(c1 - 1) // 128 + 1]

        alt = [0]

        def evac_copy(bank, src, dst, dst_states, scale=None):
            alt[0] ^= 1
            if alt[0]:
                if scale is None:
                    S.op("act", [s_PB[bank]], dst_states, lambda e: e.activation(out=dst, in_=src, func=AF.Copy))
                else:
                    S.op("act", [s_PB[bank]], dst_states,
                         lambda e: e.activation(out=dst, in_=src, func=AF.Copy, scale=scale))
            else:
                if scale is None:
                    S.op("dve", [s_PB[bank]], dst_states, lambda e: e.tensor_copy(out=dst, in_=src))
                else:
                    S.op("dve", [s_PB[bank]], dst_states, lambda e: e.tensor_scalar(
                        out=dst, in0=src, scalar1=scale, scalar2=None, op0=ALU.mult))

        def inproj(li, kind, c_lo, c_hi, evac):
            ui = next_unit(li, kind)
            sl = ui % NW
            for (c0, c1) in chunks(c_lo, c_hi):
                b = nbank()
                n = c1 - c0
                fns = []
                for k in range(KT):
                    fns.append(lambda e, k=k: e.matmul(PB[b][:, 0:n], lhsT=WR[:, sl, k * 128:(k + 1) * 128],
                                                       rhs=xT[:, k, c0:c1], start=(k == 0), stop=(k == KT - 1)))
                S.mm_group([s_WR[sl]] + xblocks(c0, c1), [s_PB[b]], fns)
                evac(b, PB[b][:, 0:n], c0, c1)
            release(ui)

        def inproj_gen(li, kind, c_lo, c_hi, evac):
            ui = next_unit(li, kind)
            sl = ui % NW
            for (c0, c1) in chunks(c_lo, c_hi):
                b = nbank()
                n = c1 - c0
                fns = []
                for k in range(KT):
                    fns.append(lambda e, k=k: e.matmul(PB[b][:, 0:n], lhsT=WR[:, sl, k * 128:(k + 1) * 128],
                                                       rhs=xT[:, k, c0:c1], start=(k == 0), stop=(k == KT - 1)))
                S.mm_group([s_WR[sl]] + xblocks(c0, c1), [s_PB[b]], fns)
                evac(b, PB[b][:, 0:n], c0, c1)
                yield
            release(ui)

        def chain(gens):
            for g in gens:
                for _ in g:
                    yield

        def ln_stats(Tb, s_T, c0, c1, nfeat):
            n = c1 - c0
            b1, b2 = nbank(), nbank()
            for t in range(8):
                j = t % 2
                S.op("act", [s_T[t]], [s_SQ[j]], lambda e: e.activation(
                    out=SQ[:, j, 0:n], in_=Tb[:, t, c0:c1], func=AF.Square))
                S.mm_group([s_T[t], s_const], [s_PB[b1]], [lambda e: e.matmul(
                    PB[b1][:, 0:n], lhsT=ONES[:], rhs=Tb[:, t, c0:c1], start=(t == 0), stop=(t == 7))])
                S.mm_group([s_SQ[j], s_const], [s_PB[b2]], [lambda e: e.matmul(
                    PB[b2][:, 0:n], lhsT=ONES[:], rhs=SQ[:, j, 0:n], start=(t == 0), stop=(t == 7))])
            inv = 1.0 / nfeat
            S.op("dve", [s_PB[b1]], [s_CF32b], lambda e: e.tensor_scalar(
                out=STT[:, 0, 0:n], in0=PB[b1][:, 0:n], scalar1=inv, scalar2=None, op0=ALU.mult))
            S.op("dve", [s_CF32b], [s_FT[0]], lambda e: e.tensor_tensor(
                out=FT[:, 0, 0:n], in0=STT[:, 0, 0:n], in1=STT[:, 0, 0:n], op=ALU.mult))
            S.op("dve", [s_PB[b2], s_FT[0]], [s_CF32b], lambda e: e.scalar_tensor_tensor(
                out=STT[:, 1, 0:n], in0=PB[b2][:, 0:n], scalar=inv, in1=FT[:, 0, 0:n],
                op0=ALU.mult, op1=ALU.subtract))
            S.op("dve", [s_CF32b], [s_CF32b], lambda e: e.tensor_scalar(
                out=STT[:, 1, 0:n], in0=STT[:, 1, 0:n], scalar1=LN_EPS, scalar2=None, op0=ALU.add))
            S.op("act", [s_CF32b], [s_CF32b], lambda e: e.activation(
                out=STT[:, 1, 0:n], in_=STT[:, 1, 0:n], func=AF.Sqrt))
            S.op("dve", [s_CF32b], [s_CF32b], lambda e: e.reciprocal(out=STT[:, 1, 0:n], in_=STT[:, 1, 0:n]))

        d_x = [S.new_dma_sem("x0"), S.new_dma_sem("x1")]

        def phase0_block(st, blk):
            xi = blk % 2
            xb, sxb = XBS[xi], s_XBS[xi]
            sp16 = [s_PB[4], s_PB[5]]
            S.dma("pool", d_x[xi], [], [sxb], xb[:], x_in[st, blk * 128:(blk + 1) * 128, :])
            fns = [lambda e, k=k: e.transpose(P16B[:, k * 128:(k + 1) * 128], xb[:, k * 128:(k + 1) * 128], IDB[:])
                   for k in range(KT)]
            S.mm_group([sxb, s_const], sp16, fns)
            S.op("dve", sp16, [s_xT[blk]], lambda e: e.tensor_copy(
                out=xT[:, :, blk * 128:(blk + 1) * 128], in_=P16B.rearrange("p (k c) -> p k c", k=KT)))

        def emit_layer(st, li, NB, first_in_prog, last_in_prog, skip_phase0=False, hook=None):
            l = li
            Tm = NB * 128
            W = HALO + Tm
            MAIN0 = 128
            tcol = lambda c: c - 96
            ppc = lambda c0, c1=None: PP[:, l, c0:(c0 + 1 if c1 is None else c1)]

            if first_in_prog and not skip_phase0:
                for blk in range(NB + 1):
                    rb = blk % 3
                    S.dma("sp", d_r[rb], [], [s_R[rb]], R[rb][:], x_in[st, blk * 128:(blk + 1) * 128, :])
                    xi = blk % 2
                    xb, sxb = XBS[xi], s_XBS[xi]
                    pp16 = P16 if xi == 0 else P16B
                    sp16 = [s_P16] if xi == 0 else [s_PB[4], s_PB[5]]
                    S.op("act", [s_R[rb]], [sxb], lambda e: e.activation(out=xb[:], in_=R[rb][:], func=AF.Copy))
                    fns = [lambda e, k=k: e.transpose(pp16[:, k * 128:(k + 1) * 128], xb[:, k * 128:(k + 1) * 128], IDB[:])
                           for k in range(KT)]
                    S.mm_group([sxb, s_const], sp16, fns)
                    S.op("dve", sp16, [s_xT[blk]], lambda e: e.tensor_copy(
                        out=xT[:, :, blk * 128:(blk + 1) * 128], in_=pp16.rearrange("p (k c) -> p k c", k=KT)))

            mchunks = chunks(MAIN0, MAIN0 + Tm)
            first_merge = [True]

            def phase2(i):
                for d in range(16):
                    if d % 2 == 0:
                        uib = next_unit(li, "br")
                        slb = uib % NW
                    uig = next_unit(li, "in")
                    slg = uig % NW
                    for (c0, c1) in mchunks:
                        n = c1 - c0
                        bg, bp = nbank(), nbank()
                        fns = [lambda e, k=k: e.matmul(PB[bg][:, 0:n], lhsT=WR[:, slg, k * 128:(k + 1) * 128],
                                                       rhs=xT[:, k, c0:c1], start=(k == 0), stop=(k == KT - 1))
                               for k in range(KT)]
                        S.mm_group([s_WR[slg]] + xblocks(c0, c1), [s_PB[bg]], fns)
                        o = (d % 2) * 1024
                        fns = [lambda e, k=k: e.matmul(PB[bp][:, 0:n], lhsT=WR[:, slb, o + k * 128:o + (k + 1) * 128],
                                                       rhs=Y[:, k, c0 - MAIN0:c1 - MAIN0], start=(k == 0), stop=(k == 7))
                               for k in range(8)]
                        S.mm_group([s_WR[slb]] + s_Y, [s_PB[bp]], fns)
                        j = d % 2
                        S.op("act", [s_PB[bg]], [s_SQ[j]], lambda e: e.activation(
                            out=SQ[:, j, 0:n], in_=PB[bg][:, 0:n], func=AF.Sigmoid))
                        mg = MG[:, d, c0 - MAIN0:c1 - MAIN0]
                        if first_merge[0]:
                            S.op("dve", [s_PB[bp], s_SQ[j]], [s_MG[d]], lambda e: e.tensor_tensor(
                                out=mg, in0=PB[bp][:, 0:n], in1=SQ[:, j, 0:n], op=ALU.mult))
                        else:
                            S.op("dve", [s_PB[bp], s_SQ[j]], [s_FT[j]], lambda e: e.tensor_tensor(
                                out=FT[:, j, 0:n], in0=PB[bp][:, 0:n], in1=SQ[:, j, 0:n], op=ALU.mult))
                            S.op("dve", [s_FT[j], s_MG[d]], [s_MG[d]], lambda e: e.tensor_tensor(
                                out=mg, in0=FT[:, j, 0:n], in1=mg, op=ALU.add))
                    release(uig)
                    if d % 2 == 1:
                        release(uib)
                first_merge[0] = False

            def skip_units(kinds):
                for k_ in kinds:
                    next_unit(li, k_)

            def branch_A():
                S.dma("pool", d_cp, [s_CB16b], [s_CB16b], PW[:].rearrange("p a b c -> p (a b c)"), pw_in[l])
                for t in range(8):
                    inproj(li, "in", 96, MAIN0 + Tm, lambda b, ps, c0, c1, t=t: evac_copy(
                        b, ps, T1[:, t, tcol(c0):tcol(c1)], [s_T1[t]]))
                for g in range(4):
                    win = 2 ** (g + 1)
                    X = T1[:, 2 * g:2 * g + 2, :]
                    sX = [s_T1[2 * g], s_T1[2 * g + 1]]
                    sP, sQ = [s_T2], s_Y
                    S.op("dve", sX, sP, lambda e: e.tensor_tensor(
                        out=PQ[:, :, 1:W], in0=X[:, :, 1:W], in1=X[:, :, 0:W - 1], op=ALU.add))
                    cur, scur = PQ, sP
                    if win >= 4:
                        S.op("dve", sP, sQ, lambda e: e.tensor_tensor(
                            out=QQ[:, :, 3:W], in0=PQ[:, :, 3:W], in1=PQ[:, :, 1:W - 2], op=ALU.add))
                        cur, scur = QQ, sQ
                    if win >= 8:
                        S.op("dve", sQ, sP, lambda e: e.tensor_tensor(
                            out=PQ[:, :, 7:W], in0=QQ[:, :, 7:W], in1=QQ[:, :, 3:W - 4], op=ALU.add))
                        cur, scur = PQ, sP
                    if win >= 16:
                        S.op("dve", sP, sQ, lambda e: e.tensor_tensor(
                            out=QQ[:, :, 15:W], in0=PQ[:, :, 15:W], in1=PQ[:, :, 7:W - 8], op=ALU.add))
                        cur, scur = QQ, sQ
                    fixc = None
                    if st == 0:
                        fixc = HALO + (128 if (fused and li == 0) else 0)
                        S.op("dve", scur + [s_const], [s_FX], lambda e: e.tensor_tensor(
                            out=FX[:], in0=cur[:, :, fixc:fixc + 16],
                            in1=POOLFIX[:, g * 16:(g + 1) * 16].unsqueeze(1).broadcast_to([128, 2, 16]), op=ALU.mult))
                        S.op("dve", sX + [s_FX], [s_FX], lambda e: e.tensor_tensor(
                            out=FX[:], in0=FX[:], in1=X[:, :, fixc:fixc + 16], op=ALU.subtract))
                    S.op("dve", scur + sX, sX, lambda e: e.scalar_tensor_tensor(
                        out=X[:, :, HALO:W], in0=cur[:, :, HALO:W], scalar=1.0 / win, in1=X[:, :, HALO:W],
                        op0=ALU.mult, op1=ALU.subtract))
                    if fixc is not None:
                        S.op("dve", [s_FX], sX, lambda e: e.tensor_copy(out=X[:, :, fixc:fixc + 16], in_=FX[:]))
                for t in range(8):
                    inproj(li, "in", MAIN0, MAIN0 + Tm, lambda b, ps, c0, c1, t=t: S.op(
                        "act", [s_PB[b]], [s_T0[t]], lambda e: e.activation(
                            out=T0[:, t, tcol(c0):tcol(c1)], in_=ps, func=AF.Silu)))
                for g in range(4):
                    for dtl in range(2):
                        t = 2 * g + dtl
                        for (c0, c1) in chunks(HALO, W):
                            n = c1 - c0
                            b = nbank()
                            fns = [lambda e, ct=ct: e.matmul(PB[b][:, 0:n], lhsT=PW[:, g, ct, dtl * 128:(dtl + 1) * 128],
                                                             rhs=T1[:, 2 * g + ct, c0:c1], start=(ct == 0), stop=(ct == 1))
                                   for ct in range(2)]
                            S.mm_group([s_T1[2 * g], s_T1[2 * g + 1], s_CB16b], [s_PB[b]], fns)
                            S.op("dve", [s_PB[b], s_T0[t], s_const], [s_Y[t]], lambda e: e.scalar_tensor_tensor(
                                out=Y[:, t, c0 - HALO:c1 - HALO], in0=PB[b][:, 0:n], scalar=ppc(t),
                                in1=T0[:, t, c0:c1], op0=ALU.mult, op1=ALU.mult))

            def branch_B():
                S.dma("sp", d_c2, [s_FT[0], s_FT[1]], [s_FT[0], s_FT[1]],
                      FT[:, 0:2, :].rearrange("p a b -> p (a b)"), wst_in[l])
                for h in range(8):
                    S.op("dve", [s_FT[0], s_FT[1], s_const], [s_CB16b], lambda e: e.tensor_tensor(
                        out=WST[:, h, :], in0=FT[:, 0:2, :].rearrange("p a b -> p (a b)")[:, h * 128:(h + 1) * 128],
                        in1=MASK, op=ALU.mult))
                S.dma("sp", d_c2, [s_CF32], [s_CF32], BSB[:].rearrange("p a b -> p (a b)"), bsb_in[l])
                for t in range(8):
                    inproj(li, "in", MAIN0, MAIN0 + Tm, lambda b, ps, c0, c1, t=t: evac_copy(
                        b, ps, T1[:, t, tcol(c0):tcol(c1)], [s_T1[t]]))
                def ev_bg(t):
                    return lambda b, ps, c0, c1: S.op(
                        "act", [s_PB[b]], [s_T0[t]], lambda e: e.activation(
                            out=T0[:, t, tcol(c0):tcol(c1)], in_=ps, func=AF.Silu))

                def ev_u(t):
                    return lambda b, ps, c0, c1: S.op(
                        "dve", [s_PB[b], s_T0[t]], [s_T0[t]], lambda e: e.tensor_tensor(
                            out=T0[:, t, tcol(c0):tcol(c1)], in0=ps, in1=T0[:, t, tcol(c0):tcol(c1)], op=ALU.mult))

                gq = chain([inproj_gen(li, "in", MAIN0, MAIN0 + Tm, ev_bg(t)) for t in range(8)] +
                           [inproj_gen(li, "in", MAIN0, MAIN0 + Tm, ev_u(t)) for t in range(8)])
                for (c0, c1) in chunks(HALO, W):
                    n = c1 - c0
                    ln_stats(T1, s_T1, c0, c1, 1024)
                    for t in range(8):
                        j = t % 4
                        S.op("dve", [s_T1[t], s_CF32b], [s_FT[j]], lambda e: e.tensor_tensor(
                            out=FT[:, j, 0:n], in0=T1[:, t, c0:c1], in1=STT[:, 0, 0:n], op=ALU.subtract))
                        S.op("dve", [s_FT[j], s_CF32b], [s_FT[j]], lambda e: e.tensor_tensor(
                            out=FT[:, j, 0:n], in0=FT[:, j, 0:n], in1=STT[:, 1, 0:n], op=ALU.mult))
                        S.op("act", [s_FT[j], s_const], [s_T1[t]], lambda e: e.activation(
                            out=T1[:, t, c0:c1], in_=FT[:, j, 0:n], func=AF.Identity,
                            scale=ppc(8 + t), bias=ppc(16 + t)))
                        next(gq, None)
                for _ in gq:
                    pass
                s_VN = [St(), St()]
                inherit(s_VN, [s_T2])
                bank_lim[0] = 4

                def stage_T(blk):
                    tc0 = HALO + blk * 128
                    vb = blk % 2
                    p16, sp = (P16, [s_P16]) if vb == 0 else (P16B, [s_PB[4], s_PB[5]])
                    fns = [lambda e, h=h: e.transpose(p16[:, h * 128:(h + 1) * 128], T1[:, h, tc0:tc0 + 128], IDB[:])
                           for h in range(8)]
                    S.mm_group(s_T1 + [s_const], sp, fns)
                    S.op("dve", sp, [s_VN[vb]], lambda e: e.tensor_copy(out=VN[:, vb, :], in_=p16[:, 0:1024]))

                def stage_S(blk):
                    tc0 = HALO + blk * 128
                    vb = blk % 2
                    for hh in range(2):
                        b = nbank()
                        for h4 in range(4):
                            h = hh * 4 + h4
                            S.mm_group([s_VN[vb], s_CB16b], [s_PB[b]], [lambda e: e.matmul(
                                PB[b][:, h4 * 128:(h4 + 1) * 128], lhsT=VN[:, vb, h * 128:(h + 1) * 128],
                                rhs=WST[:, h, :], start=True, stop=True)])
                        j = hh
                        S.op("dve", [s_PB[b], s_CF32], [s_FT[j]], lambda e: e.tensor_tensor(
                            out=FT[:, j, :], in0=PB[b][:], in1=BSB[:, hh, :], op=ALU.add))
                        S.op("dve", [s_FT[j]] + s_T0[hh * 4:hh * 4 + 4], s_Y[hh * 4:hh * 4 + 4], lambda e: e.tensor_tensor(
                            out=Y[:, hh * 4:hh * 4 + 4, blk * 128:(blk + 1) * 128],
                            in0=FT[:, j, :].rearrange("p (a b) -> p a b", a=4),
                            in1=T0[:, hh * 4:hh * 4 + 4, tc0:tc0 + 128], op=ALU.mult))

                stage_T(0)
                for blk in range(NB):
                    if blk + 1 < NB:
                        stage_T(blk + 1)
                    stage_S(blk)
                bank_lim[0] = 6
                inherit([s_T2], s_VN)

            def branch_C():
                inherit(s_LT, [s_CF32b])
                S.dma("pool", d_cp, [s_CB16], [s_CB16], BT[:].rearrange("p a b c d -> p (a b c d)"), bt_in)
                S.op("act", [s_const], [s_SC], lambda e: e.activation(out=SC[:, 0:8], in_=ppc(48, 56), func=AF.Exp))
                for i8 in range(8):
                    S.op("dve", [s_SC, s_const], [s_CF32], lambda e: e.tensor_scalar(
                        out=SINKBC[:, i8 // 4, i8 % 4, :], in0=CST[:, 0:128], scalar1=0.0, scalar2=SC[:, i8:i8 + 1],
                        op0=ALU.mult, op1=ALU.add))
                for t in range(8):
                    inproj(li, "in", MAIN0, MAIN0 + Tm, lambda b, ps, c0, c1, t=t: S.op(
                        "act", [s_PB[b]], [s_T1[t]], lambda e: e.activation(
                            out=T1[:, t, tcol(c0):tcol(c1)], in_=ps, func=AF.Silu)))
                for t in range(8):
                    inproj(li, "in", MAIN0, MAIN0 + Tm, lambda b, ps, c0, c1, t=t: evac_copy(
                        b, ps, T0[:, t, tcol(c0):tcol(c1)], [s_T0[t]], scale=0.125))
                Te = MAIN0 + Tm
                for kvi in range(2):
                    inproj(li, "kd", 0, Te, lambda b, ps, c0, c1, kvi=kvi: evac_copy(
                        b, ps, KD[:, kvi, c0:c1], [s_T2]))
                inproj(li, "in", 0, Te, lambda b, ps, c0, c1: evac_copy(b, ps, VT[:, c0:c1], [s_T2]))
                S.op("dve", [], [s_T2], lambda e: e.memset(VA[:].rearrange("p a b c d -> p (a b c d)"), 0.0))
                nbe = NB + 1
                fns = [lambda e, bb=bb: e.transpose(P16[:, bb * 128:(bb + 1) * 128], VT[:, bb * 128:(bb + 1) * 128], IDB[:])
                       for bb in range(nbe)]
                S.mm_group([s_T2, s_const], [s_P16], fns)
                pv = P16[:, 0:nbe * 128].rearrange("p (a b) -> p a b", a=nbe)
                for kvi in range(2):
                    for var in range(2):
                        S.op("dve", [s_P16], [s_T2], lambda e: e.tensor_copy(
                            out=VA[:, 0:nbe, kvi, var, var * 64:var * 64 + 64], in_=pv[:, :, kvi * 64:kvi * 64 + 64]))
                PTB = [PT, FT[:, 2:4, :].bitcast(BF16).rearrange("p a (b c) -> p (a b) c", b=2)]
                sPTB = [[s_CB16b], [s_FT[2], s_FT[3]]]
                its = [(blk, kvi) for blk in range(1, NB + 1) for kvi in range(2)]

                def stage_L(it):
                    blk, kvi = its[it]
                    tc0 = HALO + (blk - 1) * 128
                    ptb, sptb = PTB[it % 2], sPTB[it % 2]
                    for kbi, kb in enumerate((blk - 1, blk)):
                        b0, b1 = kbi * 2, kbi * 2 + 1
                        fns = []
                        for var in range(2):
                            fns.append(lambda e, var=var, bb=kbi * 2 + var: e.matmul(
                                PB[bb][:].rearrange("p (a b) -> p a b", a=4),
                                lhsT=KD[var * 64:var * 64 + 64, kvi, kb * 128:(kb + 1) * 128],
                                rhs=T0[var * 64:var * 64 + 64, 4 * kvi:4 * kvi + 4, tc0:tc0 + 128],
                                start=True, stop=False))
                        for var in range(2):
                            fns.append(lambda e, var=var, bb=kbi * 2 + var: e.matmul(
                                PB[bb][:], lhsT=IDB[:], rhs=BT[:, kbi, kvi, var, :], start=False, stop=True))
                        S.mm_group([s_T2, s_CB16, s_const] + s_T0[4 * kvi:4 * kvi + 4], [s_PB[b0], s_PB[b1]], fns)
                        for var in range(2):
                            pidx = kbi * 2 + var
                            S.op("act", [s_PB[pidx]], sptb, lambda e: e.activation(
                                out=ptb[:, pidx, :], in_=PB[pidx][:], func=AF.Exp))

                def stage_V(it):
                    blk, kvi = its[it]
                    tc0 = HALO + (blk - 1) * 128
                    ptb, sptb = PTB[it % 2], sPTB[it % 2]
                    bo, bd = 4, 5
                    fo, fd = [], []
                    for kbi, kb in enumerate((blk - 1, blk)):
                        for var in range(2):
                            pidx = kbi * 2 + var
                            first, last = (pidx == 0), (pidx == 3)
                            fo.append(lambda e, kb=kb, var=var, pidx=pidx, first=first, last=last: e.matmul(
                                PB[bo][:], lhsT=VA[:, kb, kvi, var, :], rhs=ptb[:, pidx, :], start=first, stop=last))
                            hsel = 0 if (st == 0 and (kb == 0 or (fused and li == 0 and kb == 1))) else 1
                            fd.append(lambda e, hsel=hsel, var=var, pidx=pidx, first=first, last=last: e.matmul(
                                PB[bd][:], lhsT=ONESV[:, hsel, var, :], rhs=ptb[:, pidx, :], start=first, stop=last))
                    S.mm_group([s_T2] + sptb, [s_PB[bo]], fo)
                    S.mm_group([s_const] + sptb, [s_PB[bd]], fd)
                    S.op("dve", [s_PB[bd], s_CF32], [s_FT[0]], lambda e: e.tensor_tensor(
                        out=FT[:, 0, :], in0=PB[bd][:], in1=SINKBC[:, kvi].rearrange("p a b -> p (a b)"), op=ALU.add))
                    S.op("dve", [s_FT[0]], [s_FT[0]], lambda e: e.reciprocal(out=FT[:, 0, :], in_=FT[:, 0, :]))
                    S.op("dve", [s_PB[bo], s_FT[0]], [s_FT[1]], lambda e: e.tensor_tensor(
                        out=FT[:, 1, :], in0=PB[bo][:], in1=FT[:, 0, :], op=ALU.mult))
                    S.op("dve", [s_FT[1]] + s_T1[4 * kvi:4 * kvi + 4], s_Y[4 * kvi:4 * kvi + 4], lambda e: e.tensor_tensor(
                        out=Y[:, 4 * kvi:4 * kvi + 4, (blk - 1) * 128:blk * 128],
                        in0=FT[:, 1, :].rearrange("p (a b) -> p a b", a=4),
                        in1=T1[:, 4 * kvi:4 * kvi + 4, tc0:tc0 + 128], op=ALU.mult))

                stage_L(0)
                for it in range(len(its)):
                    if it + 1 < len(its):
                        stage_L(it + 1)
                    stage_V(it)
                bank_rr[0] = 0
                inherit([s_CF32b], s_LT)

            def branch_D():
                for t in range(8):
                    inproj(li, "in", 96, MAIN0 + Tm, lambda b, ps, c0, c1, t=t: S.op(
                        "act", [s_PB[b]], [s_T0[t]], lambda e: e.activation(
                            out=T0[:, t, tcol(c0):tcol(c1)], in_=ps, func=AF.Sigmoid)))
                for t in range(8):
                    inproj(li, "in", 96, MAIN0 + Tm, lambda b, ps, c0, c1, t=t: S.op(
                        "dve", [s_PB[b], s_T0[t]], [s_T0[t]], lambda e: e.tensor_tensor(
                            out=T0[:, t, tcol(c0):tcol(c1)], in0=ps, in1=T0[:, t, tcol(c0):tcol(c1)], op=ALU.mult)))
                s_DG2 = St()
                inherit([s_DG2], s_T1)
                for t in range(8):
                    cw = PP[:, l, 56 + t * 31:56 + (t + 1) * 31]
                    DG, sdg = (DIAG, s_CB16) if t % 2 == 0 else (DIAG2, s_DG2)
                    S.op("dve", [s_const], [sdg], lambda e: e.tensor_tensor(
                        out=DG[:], in0=IDB[:].unsqueeze(1).broadcast_to([128, 31, 128]),
                        in1=cw.unsqueeze(2).broadcast_to([128, 31, 128]), op=ALU.mult))
                    for (c0, c1) in reversed(chunks(HALO, W)):
                        n = c1 - c0
                        b = nbank()
                        fns = [lambda e, j=j: e.matmul(PB[b][:, 0:n], lhsT=DG[:, j, :],
                                                       rhs=T0[:, t, c0 - 30 + j:c1 - 30 + j], start=(j == 0), stop=(j == 30))
                               for j in range(31)]
                        S.mm_group([s_T0[t], sdg], [s_PB[b]], fns)
                        S.op("act", [s_PB[b], s_const], [s_T0[t]], lambda e: e.activation(
                            out=T0[:, t, c0:c1], in_=PB[b][:, 0:n], func=AF.Identity, bias=ppc(24 + t)))
                inherit(s_T1, [s_DG2])

                def ev_dg(t):
                    return lambda b, ps, c0, c1: S.op(
                        "act", [s_PB[b]], [s_T1[t]], lambda e: e.activation(
                            out=T1[:, t, tcol(c0):tcol(c1)], in_=ps, func=AF.Silu))

                gq = chain([inproj_gen(li, "in", MAIN0, MAIN0 + Tm, ev_dg(t)) for t in range(8)])
                for (c0, c1) in chunks(HALO, W):
                    n = c1 - c0
                    ln_stats(T0, s_T0, c0, c1, 1024)
                    for t in range(8):
                        fa = t % 4
                        S.op("dve", [s_T0[t], s_CF32b], [s_FT[fa]], lambda e: e.tensor_tensor(
                            out=FT[:, fa, 0:n], in0=T0[:, t, c0:c1], in1=STT[:, 0, 0:n], op=ALU.subtract))
                        S.op("dve", [s_FT[fa], s_CF32b], [s_FT[fa]], lambda e: e.tensor_tensor(
                            out=FT[:, fa, 0:n], in0=FT[:, fa, 0:n], in1=STT[:, 1, 0:n], op=ALU.mult))
                        S.op("act", [s_FT[fa], s_const], [s_T0[t]], lambda e: e.activation(
                            out=T0[:, t, c0:c1], in_=FT[:, fa, 0:n], func=AF.Silu, scale=ppc(32 + t), bias=ppc(40 + t)))
                        next(gq, None)
                for _ in gq:
                    pass
                for t in range(8):
                    S.op("dve", [s_T0[t], s_T1[t]], [s_Y[t]], lambda e: e.tensor_tensor(
                        out=Y[:, t, 0:Tm], in0=T0[:, t, HALO:W], in1=T1[:, t, HALO:W], op=ALU.mult))

            inherit(s_T0, [s_R[0], s_R[1]])
            inherit(s_T1, [s_LN])
            inherit([s_T2], [s_R[2], s_XB, s_XBS[1]])
            bank_lim[0] = 6
            if len(branches) < 4:
                for d_ in range(16):
                    S.op("dve", [], [s_MG[d_]], lambda e: e.memset(MG[:, d_, :], 0.0))
                first_merge[0] = False
            brs = [branch_A, branch_B, branch_C, branch_D]
            nin = [16, 24, 19, 24]
            for i in range(4):
                if i in branches:
                    brs[i]()
                    phase2(i)
                else:
                    for _ in range(nin[i] + 24):
                        prefetch()
                        wstate["used"] += 1
                        prefetch()

            inherit([s_LN], s_T1)
            S.dma("sp", d_ln, [s_LN], [s_LN], LNG[:], lng_in[l])
            S.dma("sp", d_ln, [s_LN], [s_LN], LNB[:], lnb_in[l])
            inherit([s_R[0], s_R[1]], s_T0)
            inherit([s_R[2], s_XB, s_XBS[1]], [s_T2])
            bank_lim[0] = 4
            if li == 0 and first_in_prog:
                rsrc = lambda blk: x_in[st, blk * 128:(blk + 1) * 128, :]
                rst = None
            else:
                rsrc = lambda blk: x1s[st, (blk - 1) * 128:blk * 128, :]
                rst = s_x1s[st]

            def load_resid(blk):
                rb = blk % 3
                S.dma("sp", d_r[rb], [rst] if rst is not None else [], [s_R[rb]], R[rb][:], rsrc(blk))

            load_resid(1)
            load_resid(2)
            load_resid(3)
            for e_ in range(16):
                uio = next_unit(li, "out")
                sl = uio % NW
                for (c0, c1) in mchunks:
                    n = c1 - c0
                    b = nbank()
                    fns = [lambda e, k=k: e.matmul(PB[b][:, 0:n], lhsT=WR[:, sl, k * 128:(k + 1) * 128],
                                                   rhs=MG[:, k, c0 - MAIN0:c1 - MAIN0], start=(k == 0), stop=(k == KT - 1))
                           for k in range(KT)]
                    S.mm_group([s_WR[sl]] + s_MG, [s_PB[b]], fns)
                    evac_copy(b, PB[b][:, 0:n], xT[:, e_, c0:c1], xblocks(c0, c1))
                release(uio)
            sshift = 1 if (fused and li == 0 and st == 0) else 0

            def stage_A(blk):
                rb = blk % 3
                fns = [lambda e, k=k: e.transpose(P16[:, k * 128:(k + 1) * 128], xT[:, k, blk * 128:(blk + 1) * 128], IDB[:])
                       for k in range(KT)]
                S.mm_group([s_xT[blk], s_const], [s_P16], fns)
                S.op("dve", [s_P16, s_R[rb]], [s_R[rb]], lambda e: e.scalar_tensor_tensor(
                    out=R[rb][:], in0=R[rb][:], scalar=ALPHA, in1=P16[:], op0=ALU.mult, op1=ALU.add))
                o = 8 * (blk % 3)
                ssc = s_SCP[blk % 3]
                S.op("dve", [], [ssc], lambda e: e.memset(SC[:, o + 8:o + 10], 0.0))
                S.op("act", [s_R[rb]], s_Y + [ssc], lambda e: e.activation(
                    out=JK[:], in_=R[rb][:], func=AF.Identity, accum_out=SC[:, o + 8:o + 9]))
                S.op("act", [s_R[rb]], s_Y + [ssc], lambda e: e.activation(
                    out=JK[:], in_=R[rb][:], func=AF.Square, accum_out=SC[:, o + 9:o + 10]))

            def stage_A2(blk):
                o = 8 * (blk % 3)
                ssc = s_SCP[blk % 3]
                S.op("dve", [ssc], [ssc], lambda e: e.tensor_scalar(
                    out=SC[:, o + 10:o + 12], in0=SC[:, o + 8:o + 10], scalar1=1.0 / D, scalar2=None, op0=ALU.mult))
                S.op("dve", [ssc], [ssc], lambda e: e.tensor_tensor(
                    out=SC[:, o + 12:o + 13], in0=SC[:, o + 10:o + 11], in1=SC[:, o + 10:o + 11], op=ALU.mult))
                S.op("dve", [ssc], [ssc], lambda e: e.tensor_tensor(
                    out=SC[:, o + 13:o + 14], in0=SC[:, o + 11:o + 12], in1=SC[:, o + 12:o + 13], op=ALU.subtract))
                S.op("dve", [ssc], [ssc], lambda e: e.tensor_scalar(
                    out=SC[:, o + 14:o + 15], in0=SC[:, o + 13:o + 14], scalar1=LN_EPS, scalar2=None, op0=ALU.add))
                S.op("act", [ssc], [ssc], lambda e: e.activation(out=SC[:, o + 14:o + 15], in_=SC[:, o + 14:o + 15], func=AF.Sqrt))

            def stage_B(blk):
                rb = blk % 3
                o = 8 * (blk % 3)
                ssc = s_SCP[blk % 3]
                S.op("dve", [ssc], [ssc], lambda e: e.reciprocal(out=SC[:, o + 14:o + 15], in_=SC[:, o + 14:o + 15]))
                S.op("dve", [ssc], [ssc], lambda e: e.scalar_tensor_tensor(
                    out=SC[:, o + 15:o + 16], in0=SC[:, o + 10:o + 11], scalar=-1.0, in1=SC[:, o + 14:o + 15],
                    op0=ALU.mult, op1=ALU.mult))
                S.op("act", [ssc, s_R[rb]], [s_R[rb]], lambda e: e.activation(
                    out=R[rb][:], in_=R[rb][:], func=AF.Identity,
                    scale=SC[:, o + 14:o + 15], bias=SC[:, o + 15:o + 16]))
                S.op("dve", [s_LN, s_R[rb]], [s_R[rb]], lambda e: e.tensor_tensor(
                    out=R[rb][:], in0=R[rb][:], in1=LNG[:], op=ALU.mult))
                S.op("pool", [s_LN, s_R[rb]], [s_R[rb]], lambda e: e.tensor_tensor(
                    out=R[rb][:], in0=R[rb][:], in1=LNB[:], op=ALU.add))
                if last_in_prog:
                    S.dma("sp", d_o[rb], [s_R[rb]], [], y_out[st, (blk - 1) * 128:blk * 128, :], R[rb][:])
                else:
                    if blk - sshift >= 1:
                        S.dma("sp", d_o[rb], [s_R[rb]], [s_x1s[st]],
                              x1s[st, (blk - 1 - sshift) * 128:(blk - sshift) * 128, :], R[rb][:])
                    xi = blk % 2
                    xb, sxb = XBS[xi], s_XBS[xi]
                    if blk == 1 and st == 0:
                        S.op("act", [s_R[rb], s_const], [sxb], lambda e: e.activation(
                            out=xb[:], in_=R[rb][:], func=AF.Copy, scale=FLAG))
                    else:
                        S.op("act", [s_R[rb]], [sxb], lambda e: e.activation(out=xb[:], in_=R[rb][:], func=AF.Copy))

            def stage_C(blk):
                if last_in_prog:
                    return
                xi = blk % 2
                xb, sxb = XBS[xi], s_XBS[xi]
                sp16 = [s_PB[4], s_PB[5]]
                fns = [lambda e, k=k: e.transpose(P16B[:, k * 128:(k + 1) * 128], xb[:, k * 128:(k + 1) * 128], IDB[:])
                       for k in range(KT)]
                S.mm_group([sxb, s_const], sp16, fns)
                sl_ = blk - sshift
                if sl_ == 0 and sshift == 0:
                    return
                S.op("dve", sp16, [s_xT[sl_]], lambda e: e.tensor_copy(
                    out=xT[:, :, sl_ * 128:(sl_ + 1) * 128], in_=P16B.rearrange("p (k c) -> p k c", k=KT)))
                if sshift == 1 and blk == NB:
                    S.op("dve", sp16, [s_xT[9]], lambda e: e.tensor_copy(
                        out=xT[:, :, 9 * 128:10 * 128], in_=P16B.rearrange("p (k c) -> p k c", k=KT)))

            for i in range(1, NB + 4):
                if i <= NB:
                    stage_A(i)
                if 1 <= i - 1 <= NB:
                    stage_A2(i - 1)
                if 1 <= i - 2 <= NB:
                    stage_B(i - 2)
                    if (i - 2) + 3 <= NB:
                        load_resid((i - 2) + 3)
                if 1 <= i - 3 <= NB:
                    stage_C(i - 3)
                if hook is not None:
                    hook(i)

        for st in range(NST):
            if fused:
                emit_layer(st, 0, 9 if st == 0 else 8, True, False, skip_phase0=(st == 1))
                if st == 1:
                    S.op("dve", [s_xT[9]], [s_xT[0]], lambda e: e.tensor_copy(
                        out=xT[:, :, 0:128], in_=xT[:, :, 9 * 128:10 * 128]))
                hk = None
                if st == 0:
                    def hk(i):
                        if i == 1:
                            phase0_block(1, 0)
                            phase0_block(1, 1)
                        elif 2 <= i <= 8:
                            phase0_block(1, i)
                emit_layer(st, 1, 8, False, True, hook=hk)
            else:
                emit_layer(st, 0, 8, True, True)
        allst = s_R + s_x1s + s_xT + s_MG + s_Y + s_T0 + s_T1 + [s_T2, s_P16] + s_PB + s_WR
        S.wait_all("sp", allst)
    return nc


def _consts(core, fused):
    cst = np.zeros((128, 384), np.float32)
    cst[:, 0:128] = np.eye(128, dtype=np.float32)
    s = np.arange(128)[:, None]
    t = np.arange(128)[None, :]
    cst[:, 128:256] = (s <= t).astype(np.float32)
    cst[:, 256] = 0.0 if core == 0 else 1.0
    for g in range(4):
        win = 2 ** (g + 1)
        tt = np.arange(16)
        if core == 0:
            cst[:, 272 + g * 16:272 + (g + 1) * 16] = (1.0 / np.minimum(tt + 1, win)).astype(np.float32)[None, :]
        else:
            cst[:, 272 + g * 16:272 + (g + 1) * 16] = np.float32(1.0 / win)
    return cst


def _layer_inputs(ls, w_in, pool_w, pool_scale, sgu_ln_g, sgu_ln_b, sgu_w, sgu_b, attn_sinks, rel_bias,
                  conv_w, conv_b, conv_ln_g, conv_ln_b, w_branch, w_out, ln_g, ln_b):
    nl = len(ls)
    ws = np.stack([build_wstream(w_in[l], w_branch[l], w_out[l]) for l in ls])
    pp = np.stack([build_pp(l, pool_scale, sgu_ln_g, sgu_ln_b, conv_b, conv_ln_g, conv_ln_b, attn_sinks, conv_w)
                   for l in ls], axis=1)
    pw = np.stack([pool_w[l].reshape(4, 2, 128, 256).transpose(2, 0, 1, 3).reshape(128, 2048) for l in ls])
    wst = np.stack([sgu_w[l].transpose(2, 0, 1).reshape(128, 1024) for l in ls])
    bsb = np.stack([np.broadcast_to(sgu_b[l].reshape(1, 1024), (128, 1024)) for l in ls])
    lng = np.stack([np.broadcast_to(ln_g[l][None, :], (128, D)) for l in ls])
    lnb = np.stack([np.broadcast_to(ln_b[l][None, :], (128, D)) for l in ls])
    return dict(ws=np.ascontiguousarray(ws), pp=np.ascontiguousarray(pp), pw=np.ascontiguousarray(pw),
                wst=np.ascontiguousarray(wst), bsb=np.ascontiguousarray(bsb), lng=np.ascontiguousarray(lng),
                lnb=np.ascontiguousarray(lnb), bt=build_bias_table(rel_bias))


def _shard_x(x2d, nblk_front):
    pad = nblk_front * 128
    xp = np.concatenate([np.zeros((pad, D), np.float32), x2d], axis=0)
    out = []
    for c in range(NCORE):
        sts = []
        for st in range(NST):
            t0 = c * TOK_CORE + st * ST_TOK
            if nblk_front == 2 and st == 1:
                blk = np.zeros((pad + ST_TOK, D), np.float32)
                blk[0:128 + ST_TOK] = xp[t0 + 128:t0 + pad + ST_TOK]
                sts.append(blk)
            else:
                sts.append(xp[t0:t0 + pad + ST_TOK])
        out.append(np.ascontiguousarray(np.stack(sts)))
    return out


_PROG = {}


def _get_prog(key, *a):
    if key not in _PROG:
        _PROG[key] = build_program(*a)
    return _PROG[key]


FUSED = True


def kernel(x, w_in, pool_w, pool_scale, sgu_ln_g, sgu_ln_b, sgu_w, sgu_b, attn_sinks, rel_bias,
           conv_w, conv_b, conv_ln_g, conv_ln_b, w_branch, w_out, ln_g, ln_b):
    args = [np.asarray(a, np.float32) for a in (w_in, pool_w, pool_scale, sgu_ln_g, sgu_ln_b, sgu_w, sgu_b,
                                                 attn_sinks, rel_bias, conv_w, conv_b, conv_ln_g, conv_ln_b,
                                                 w_branch, w_out, ln_g, ln_b)]
    x2d = np.asarray(x, np.float32).reshape(SEQ, D)
    if FUSED:
        nc = _get_prog("fused", [0, 1], True)
        li = _layer_inputs([0, 1], *args)
        xs = _shard_x(x2d, 2)
        in_maps = [dict(li, x_in=xs[c], cst=_consts(c, True)) for c in range(NCORE)]
        res = run_bass_kernel_spmd(nc, in_maps, core_ids=list(range(NCORE)))
        out = np.concatenate([r["y_out"].reshape(TOK_CORE, D) for r in res.results], axis=0)
        return out.reshape(1, SEQ, D)
    cur = x2d
    for l in range(2):
        nc = _get_prog("single", [0], False)
        li = _layer_inputs([l], *args)
        xs = _shard_x(cur, 1)
        in_maps = [dict(li, x_in=xs[c], cst=_consts(c, False)) for c in range(NCORE)]
        res = run_bass_kernel_spmd(nc, in_maps, core_ids=list(range(NCORE)))
        cur = np.concatenate([r["y_out"].reshape(TOK_CORE, D) for r in res.results], axis=0)
    return cur.reshape(1, SEQ, D)
```

```python
import contextlib
import numpy as np
import concourse.bass as bass
import concourse.mybir as mybir
from concourse.bass_utils import run_bass_kernel_spmd

F32 = mybir.dt.float32
BF16 = mybir.dt.bfloat16
AF = mybir.ActivationFunctionType
ALU = mybir.AluOpType

D = 2048
SEQ = 16384
NCORE = 8
TOK_CORE = SEQ // NCORE
NST = 2
ST_TOK = TOK_CORE // NST
KT = D // 128
ALPHA = (2 * 2) ** 0.25
LN_EPS = 1e-5
NEG = -30000.0
NW = 5
HALO = 32
PPC = 56 + 8 * 31

O_AIN, O_AG, O_U, O_V, O_BG, O_Q, O_K, O_VV, O_CG, O_DV, O_DG, O_DGATE, O_GL = (
    0, 1024, 2048, 3072, 4096, 5120, 6144, 6272, 6400, 7424, 8448, 9472, 10496)


def layer_units():
    u = []
    t8 = lambda base: [("in", base + 128 * t) for t in range(8)]
    def gates(i):
        r = []
        for d in range(16):
            if d % 2 == 0:
                r.append(("br", i, d))
            r.append(("in", O_GL + i * 2048 + d * 128))
        return r
    u += t8(O_AIN) + t8(O_AG) + gates(0)
    u += t8(O_V) + t8(O_BG) + t8(O_U) + gates(1)
    u += t8(O_CG) + t8(O_Q) + [("kd", 0), ("kd", 1), ("in", O_VV)] + gates(2)
    u += t8(O_DG) + t8(O_DV) + t8(O_DGATE) + gates(3)
    u += [("out", e) for e in range(16)]
    return u


UNITS = layer_units()
NU = len(UNITS)


def build_wstream(w_in, w_branch, w_out):
    ws = np.empty((NU, 128, 2048), np.float32)
    wk = w_in.reshape(KT, 128, -1)
    for n, un in enumerate(UNITS):
        if un[0] == "in":
            c = un[1]
            ws[n] = wk[:, :, c:c + 128].transpose(1, 0, 2).reshape(128, 2048)
        elif un[0] == "kd":
            c = O_K + 64 * un[1]
            blk = wk[:, :, c:c + 64]
            ws[n] = np.concatenate([blk, blk], axis=2).transpose(1, 0, 2).reshape(128, 2048)
        elif un[0] == "br":
            _, i, d = un
            wb = w_branch[i].reshape(8, 128, 2048)
            a = wb[:, :, d * 128:(d + 1) * 128].transpose(1, 0, 2).reshape(128, 1024)
            b = wb[:, :, (d + 1) * 128:(d + 2) * 128].transpose(1, 0, 2).reshape(128, 1024)
            ws[n] = np.concatenate([a, b], axis=1)
        else:
            e = un[1]
            wo = w_out.reshape(KT, 128, 2048)
            ws[n] = wo[:, :, e * 128:(e + 1) * 128].transpose(1, 0, 2).reshape(128, 2048)
    return ws


def t5_bucket_np(n):
    max_exact = 16
    nf = np.maximum(n, 1).astype(np.float32)
    large = max_exact + (np.log(nf / np.float32(max_exact)) / np.float32(np.log(128 / max_exact))
                         * np.float32(32 - max_exact)).astype(np.int32)
    large = np.minimum(large, 31)
    return np.where(n < max_exact, n, large)


def _bucket_table():
    return t5_bucket_np(np.arange(128))


def build_bias_table(rel_bias):
    bk = _bucket_table()
    kk = np.arange(128)[:, None]
    qq = np.arange(128)[None, :]
    bt = np.full((128, 2, 2, 2, 4, 128), NEG, np.float32)
    for kb in range(2):
        dist = qq - kk + (128 if kb == 0 else 0)
        valid = (dist >= 0) & (dist < 128)
        idx = bk[np.clip(dist, 0, 127)]
        for kv in range(2):
            for var in range(2):
                for j in range(4):
                    h = kv * 8 + 2 * j + var
                    vals = rel_bias[idx, h]
                    bt[:, kb, kv, var, j, :] = np.where(valid, vals, np.float32(NEG))
    return bt.reshape(128, 4096)


def build_pp(l, pool_scale, sgu_ln_g, sgu_ln_b, conv_b, conv_ln_g, conv_ln_b, attn_sinks, conv_w):
    pp = np.zeros((128, PPC), np.float32)
    col = lambda v: v.reshape(8, 128).T
    pp[:, 0:8] = col(pool_scale[l])
    pp[:, 8:16] = col(sgu_ln_g[l])
    pp[:, 16:24] = col(sgu_ln_b[l])
    pp[:, 24:32] = col(conv_b[l])
    pp[:, 32:40] = col(conv_ln_g[l])
    pp[:, 40:48] = col(conv_ln_b[l])
    for kv in range(2):
        for j in range(4):
            pp[0:64, 48 + kv * 4 + j] = attn_sinks[l, kv * 8 + 2 * j]
            pp[64:128, 48 + kv * 4 + j] = attn_sinks[l, kv * 8 + 2 * j + 1]
    cw = conv_w[l].reshape(31, 8, 128)
    pp[:, 56:] = cw.transpose(2, 1, 0).reshape(128, 8 * 31)
    return pp


class St:
    __slots__ = ("w", "r")

    def __init__(self):
        self.w = {}
        self.r = {}


def _merge(dst, src):
    for k, v in src.items():
        if dst.get(k, 0) < v:
            dst[k] = v


class Sync:
    def __init__(self, nc, es):
        self.nc = nc
        self.es = es
        self.engs = {"pe": nc.tensor, "act": nc.scalar, "dve": nc.vector, "pool": nc.gpsimd, "sp": nc.sync}
        self.sems = {}
        self.cnt = {}
        for e in ("pe", "act", "dve", "pool"):
            self.sems[e] = es.enter_context(nc.semaphore("s_" + e))
            self.cnt[e] = 0
        self.known = {e: {} for e in self.engs}
        self.ndma = 0

    def new_dma_sem(self, name):
        nm = "d_" + name
        self.sems[nm] = self.es.enter_context(self.nc.semaphore(nm))
        self.cnt[nm] = 0
        return nm

    def _wait(self, eng, toks):
        kn = self.known[eng]
        for s, v in toks.items():
            if eng == "pe" and s == "pe":
                continue
            if kn.get(s, 0) < v:
                self.engs[eng].wait_ge(self.sems[s], v)
                kn[s] = v

    def _deps(self, reads, writes):
        toks = {}
        for s in reads:
            _merge(toks, s.w)
        for s in writes:
            _merge(toks, s.w)
            _merge(toks, s.r)
        return toks

    def op(self, eng, reads, writes, fn):
        self._wait(eng, self._deps(reads, writes))
        inst = fn(self.engs[eng])
        self.cnt[eng] += 1
        inst.then_inc(self.sems[eng], 1)
        tok = {eng: self.cnt[eng]}
        for s in reads:
            _merge(s.r, tok)
        for s in writes:
            s.w = dict(tok)
            s.r = {}
        return tok

    def mm_group(self, reads, writes, fns):
        self._wait("pe", self._deps(reads, writes))
        for f in fns[:-1]:
            f(self.engs["pe"])
        inst = fns[-1](self.engs["pe"])
        self.cnt["pe"] += 1
        inst.then_inc(self.sems["pe"], 1)
        tok = {"pe": self.cnt["pe"]}
        for s in reads:
            _merge(s.r, tok)
        for s in writes:
            s.w = dict(tok)
            s.r = {}
        return tok

    def dma(self, q, dsem, reads, writes, out, in_):
        toks = self._deps(reads, writes)
        if self.cnt[dsem] > 0:
            _merge(toks, {dsem: self.cnt[dsem]})
        self._wait(q, toks)
        inst = self.engs[q].dma_start(out=out, in_=in_)
        self.cnt[dsem] += 16
        inst.then_inc(self.sems[dsem], 16)
        tok = {dsem: self.cnt[dsem]}
        for s in reads:
            _merge(s.r, tok)
        for s in writes:
            s.w = dict(tok)
            s.r = {}
        self.ndma += 1
        return tok

    def wait_all(self, eng, states):
        toks = {}
        for s in states:
            _merge(toks, s.w)
            _merge(toks, s.r)
        self._wait(eng, toks)


def inherit(dst, src):
    for d_ in dst:
        for s in src:
            _merge(d_.w, s.w)
            _merge(d_.w, s.r)


def chunks(lo, hi, mx=512):
    n = hi - lo
    k = -(-n // mx)
    base = -(-n // k)
    base = -(-base // 8) * 8
    out = []
    c = lo
    while c < hi:
        out.append((c, min(hi, c + base)))
        c += base
    return out


def build_program(layers, fused, branches=(0, 1, 2, 3)):
    nl = len(layers)
    NB1 = 9 if fused else 8
    XROWS = (NB1 + 1) * 128
    nc = bass.Bass("TRN2", target_bir_lowering=False)
    dt = nc.dram_tensor
    x_in = dt("x_in", [NST, XROWS, D], F32, kind="ExternalInput").ap()
    ws = dt("ws", [nl, NU, 128, 2048], F32, kind="ExternalInput").ap()
    pp_in = dt("pp", [128, nl, PPC], F32, kind="ExternalInput").ap()
    pw_in = dt("pw", [nl, 128, 2048], F32, kind="ExternalInput").ap()
    wst_in = dt("wst", [nl, 128, 1024], F32, kind="ExternalInput").ap()
    bsb_in = dt("bsb", [nl, 128, 1024], F32, kind="ExternalInput").ap()
    lng_in = dt("lng", [nl, 128, 2048], F32, kind="ExternalInput").ap()
    lnb_in = dt("lnb", [nl, 128, 2048], F32, kind="ExternalInput").ap()
    bt_in = dt("bt", [128, 4096], F32, kind="ExternalInput").ap()
    cst_in = dt("cst", [128, 384], F32, kind="ExternalInput").ap()
    y_out = dt("y_out", [NST, ST_TOK, D], F32, kind="ExternalOutput").ap()
    x1s = dt("x1s", [NST, 8 * 128, D], F32, kind="Internal").ap() if fused else None

    es = contextlib.ExitStack()
    with es:
        S = Sync(nc, es)
        off = [17536]

        def alloc(name, shape, dtype, at=None):
            nbytes = int(np.prod(shape[1:])) * (2 if dtype == BF16 else 4)
            if at is None:
                at = off[0]
                off[0] += (nbytes + 63) // 64 * 64
                assert off[0] <= 229344, (name, off[0])
            return nc.alloc_sbuf_tensor_at(name, list(shape), dtype, offset=at), at

        TW = HALO + NB1 * 128 if fused else HALO + 9 * 128
        TW = HALO + 9 * 128
        xT, _ = alloc("xT", [128, KT, 1280], BF16)
        MG, _ = alloc("MG", [128, KT, 1152], BF16)
        WR, _ = alloc("WR", [128, NW, 2048], BF16)
        Y, aY = alloc("Y", [128, 8, 1152], BF16)
        T0, aT0 = alloc("T0", [128, 8, TW], BF16)
        T1, aT1 = alloc("T1", [128, 8, TW], BF16)
        T2, aT2 = alloc("T2", [128, 9216], BF16)
        CB16, aCB16 = alloc("CB16", [128, 4096], BF16)
        CB16b, aCB16b = alloc("CB16b", [128, 2048], BF16)
        CF32, aCF32 = alloc("CF32", [128, 1024], F32)
        CF32b, aCF32b = alloc("CF32b", [128, 1024], F32)
        FT, _ = alloc("FT", [128, 4, 512], F32)
        SQ, _ = alloc("SQ", [128, 2, 512], BF16)
        PP, _ = alloc("PP", [128, nl, PPC], F32)
        IDB, _ = alloc("IDB", [128, 128], BF16)
        ONESV, _ = alloc("ONESV", [128, 2, 2, 128], BF16)
        ONES, _ = alloc("ONES", [128, 128], BF16)
        CST, _ = alloc("CST", [128, 384], F32)
        SC, _ = alloc("SC", [128, 32], F32)
        FX, _ = alloc("FX", [128, 2, 16], F32)
        R = [alloc("R0", [128, 2048], F32, at=aT0)[0], alloc("R1", [128, 2048], F32, at=aT0 + 8192)[0],
             alloc("R2", [128, 2048], F32, at=aT2)[0]]
        XB, _ = alloc("XB", [128, 2048], BF16, at=aT2 + 8192)
        XB2, _ = alloc("XB2", [128, 2048], BF16, at=aT2 + 12288)
        JK, _ = alloc("JK", [128, 2048], BF16, at=aY)
        XBS = [XB, XB2]
        VN, _ = alloc("VN", [128, 2, 1024], BF16, at=aT2 + 8192)
        LNG, _ = alloc("LNG", [128, 2048], F32, at=aT1)
        LNB, _ = alloc("LNB", [128, 2048], F32, at=aT1 + 8192)
        PQ, _ = alloc("PQ", [128, 2, TW], BF16, at=aT2)
        QQ, _ = alloc("QQ", [128, 2, TW], BF16, at=aY)
        KD, _ = alloc("KD", [128, 2, 1280], BF16, at=aT2)
        VT, _ = alloc("VT", [128, 1280], BF16, at=aT2 + 5120)
        VA, _ = alloc("VA", [128, 10, 2, 2, 128], BF16, at=aT2 + 7680)
        BT, _ = alloc("BT", [128, 2, 2, 2, 512], BF16, at=aCB16)
        DIAG, _ = alloc("DIAG", [128, 31, 128], BF16, at=aCB16)
        DIAG2, _ = alloc("DIAG2", [128, 31, 128], BF16, at=aT1)
        PW, _ = alloc("PW", [128, 4, 2, 256], BF16, at=aCB16b)
        WST, _ = alloc("WST", [128, 8, 128], BF16, at=aCB16b)
        PT, _ = alloc("PT", [128, 4, 512], BF16, at=aCB16b)
        BSB, _ = alloc("BSB", [128, 2, 512], F32, at=aCF32)
        SINKBC, _ = alloc("SINKBC", [128, 2, 4, 128], F32, at=aCF32)
        LT, _ = alloc("LT", [128, 2, 512], F32, at=aCF32b)
        STT, _ = alloc("STT", [128, 2, 512], F32, at=aCF32b)

        PBALL = es.enter_context(nc.psum_tensor("pball", [128, 8, 512], F32))
        PB = [PBALL[:, i, :] for i in range(6)]
        P16 = PBALL[:, 6:8, :].bitcast(BF16).rearrange("p a b -> p (a b)")
        P16B = PBALL[:, 4:6, :].bitcast(BF16).rearrange("p a b -> p (a b)")

        s_xT = [St() for _ in range(10)]
        s_MG = [St() for _ in range(16)]
        s_WR = [St() for _ in range(NW)]
        s_Y = [St() for _ in range(8)]
        s_T0 = [St() for _ in range(8)]
        s_T1 = [St() for _ in range(8)]
        s_T2 = St()
        s_CB16, s_CB16b, s_CF32, s_CF32b = St(), St(), St(), St()
        s_FT = [St(), St(), St(), St()]
        s_LT = [St(), St()]
        s_SQ = [St(), St()]
        s_PB = [St() for _ in range(6)]
        s_P16 = St()
        s_const = St()
        s_SC = St()
        s_SCP = [St(), St(), St()]
        s_FX = St()
        s_R = [St(), St(), St()]
        s_XB = St()
        s_XBS = [s_XB, St()]
        P16S = [(P16, None), (P16B, None)]
        s_LN = St()
        s_x1s = [St() for _ in range(NST)]
        d_w = [S.new_dma_sem("w%d" % i) for i in range(NW)]
        d_r = [S.new_dma_sem("r%d" % i) for i in range(3)]
        d_o = [S.new_dma_sem("o%d" % i) for i in range(3)]
        d_c = S.new_dma_sem("c")
        d_c2 = S.new_dma_sem("c2")
        d_cp = S.new_dma_sem("cp")
        d_ln = S.new_dma_sem("ln")

        bank_rr = [0]
        bank_lim = [6]

        def nbank():
            b = bank_rr[0] % bank_lim[0]
            bank_rr[0] = (b + 1) % bank_lim[0]
            return b

        wseq = []
        for _st in range(NST):
            for li in range(nl):
                for n in range(NU):
                    wseq.append((li, n))
        wstate = {"loaded": 0, "used": 0}
        wlive = set()

        def release(i):
            wlive.discard(i)
            prefetch()

        def prefetch():
            oldest = min(wlive) if wlive else wstate["used"]
            while wstate["loaded"] < len(wseq) and wstate["loaded"] < oldest + NW:
                i = wstate["loaded"]
                li, n = wseq[i]
                sl = i % NW
                S.dma("pool", d_w[sl], [], [s_WR[sl]], WR[:, sl, :], ws[li, n])
                wstate["loaded"] += 1

        def next_unit(li, kind):
            i = wstate["used"]
            assert wseq[i][0] == li and UNITS[wseq[i][1]][0] == kind, (wseq[i], UNITS[wseq[i][1]], kind)
            prefetch()
            wstate["used"] += 1
            wlive.add(i)
            return i

        S.dma("sp", d_c, [], [s_const], PP[:], pp_in)
        S.dma("sp", d_c2, [], [s_const], CST[:], cst_in)
        S.dma("pool", d_cp, [], [s_const], IDB[:], cst_in[:, 0:128])
        MASK = CST[:, 128:256]
        FLAG = CST[:, 256:257]
        POOLFIX = CST[:, 272:336]
        S.op("dve", [], [s_const], lambda e: e.memset(ONES[:], 1.0))
        S.op("dve", [], [s_const], lambda e: e.memset(ONESV[:], 0.0))
        S.op("dve", [], [s_const], lambda e: e.memset(ONESV[:, 1, 0, 0:64], 1.0))
        S.op("dve", [], [s_const], lambda e: e.memset(ONESV[:, 1, 1, 64:128], 1.0))
        S.op("dve", [s_const], [s_const], lambda e: e.tensor_scalar(
            out=ONESV[:, 0, 0, 0:64], in0=ONESV[:, 1, 0, 0:64], scalar1=FLAG, scalar2=None, op0=ALU.mult))
        S.op("dve", [s_const], [s_const], lambda e: e.tensor_scalar(
            out=ONESV[:, 0, 1, 64:128], in0=ONESV[:, 1, 1, 64:128], scalar1=FLAG, scalar2=None, op0=ALU.mult))
        prefetch()

        def xblocks(c0, c1):
            return s_xT[c0 // 128:(c1 - 1) // 128 + 1]

        alt = [0]

        def evac_copy(bank, src, dst, dst_states, scale=None):
            alt[0] ^= 1
            if alt[0]:
                if scale is None:
                    S.op("act", [s_PB[bank]], dst_states, lambda e: e.activation(out=dst, in_=src, func=AF.Copy))
                else:
                    S.op("act", [s_PB[bank]], dst_states,
                         lambda e: e.activation(out=dst, in_=src, func=AF.Copy, scale=scale))
            else:
                if scale is None:
                    S.op("dve", [s_PB[bank]], dst_states, lambda e: e.tensor_copy(out=dst, in_=src))
                else:
                    S.op("dve", [s_PB[bank]], dst_states, lambda e: e.tensor_scalar(
                        out=dst, in0=src, scalar1=scale, scalar2=None, op0=ALU.mult))

        def inproj(li, kind, c_lo, c_hi, evac):
            ui = next_unit(li, kind)
            sl = ui % NW
            for (c0, c1) in chunks(c_lo, c_hi):
                b = nbank()
                n = c1 - c0
                fns = []
                for k in range(KT):
                    fns.append(lambda e, k=k: e.matmul(PB[b][:, 0:n], lhsT=WR[:, sl, k * 128:(k + 1) * 128],
                                                       rhs=xT[:, k, c0:c1], start=(k == 0), stop=(k == KT - 1)))
                S.mm_group([s_WR[sl]] + xblocks(c0, c1), [s_PB[b]], fns)
                evac(b, PB[b][:, 0:n], c0, c1)
            release(ui)

        def inproj_gen(li, kind, c_lo, c_hi, evac):
            ui = next_unit(li, kind)
            sl = ui % NW
            for (c0, c1) in chunks(c_lo, c_hi):
                b = nbank()
                n = c1 - c0
                fns = []
                for k in range(KT):
                    fns.append(lambda e, k=k: e.matmul(PB[b][:, 0:n], lhsT=WR[:, sl, k * 128:(k + 1) * 128],
                                                       rhs=xT[:, k, c0:c1], start=(k == 0), stop=(k == KT - 1)))
                S.mm_group([s_WR[sl]] + xblocks(c0, c1), [s_PB[b]], fns)
                evac(b, PB[b][:, 0:n], c0, c1)
                yield
            release(ui)

        def chain(gens):
            for g in gens:
                for _ in g:
                    yield

        def ln_stats(Tb, s_T, c0, c1, nfeat):
            n = c1 - c0
            b1, b2 = nbank(), nbank()
            for t in range(8):
                j = t % 2
                S.op("act", [s_T[t]], [s_SQ[j]], lambda e: e.activation(
                    out=SQ[:, j, 0:n], in_=Tb[:, t, c0:c1], func=AF.Square))
                S.mm_group([s_T[t], s_const], [s_PB[b1]], [lambda e: e.matmul(
                    PB[b1][:, 0:n], lhsT=ONES[:], rhs=Tb[:, t, c0:c1], start=(t == 0), stop=(t == 7))])
                S.mm_group([s_SQ[j], s_const], [s_PB[b2]], [lambda e: e.matmul(
                    PB[b2][:, 0:n], lhsT=ONES[:], rhs=SQ[:, j, 0:n], start=(t == 0), stop=(t == 7))])
            inv = 1.0 / nfeat
            S.op("dve", [s_PB[b1]], [s_CF32b], lambda e: e.tensor_scalar(
                out=STT[:, 0, 0:n], in0=PB[b1][:, 0:n], scalar1=inv, scalar2=None, op0=ALU.mult))
            S.op("dve", [s_CF32b], [s_FT[0]], lambda e: e.tensor_tensor(
                out=FT[:, 0, 0:n], in0=STT[:, 0, 0:n], in1=STT[:, 0, 0:n], op=ALU.mult))
            S.op("dve", [s_PB[b2], s_FT[0]], [s_CF32b], lambda e: e.scalar_tensor_tensor(
                out=STT[:, 1, 0:n], in0=PB[b2][:, 0:n], scalar=inv, in1=FT[:, 0, 0:n],
                op0=ALU.mult, op1=ALU.subtract))
            S.op("dve", [s_CF32b], [s_CF32b], lambda e: e.tensor_scalar(
                out=STT[:, 1, 0:n], in0=STT[:, 1, 0:n], scalar1=LN_EPS, scalar2=None, op0=ALU.add))
            S.op("act", [s_CF32b], [s_CF32b], lambda e: e.activation(
                out=STT[:, 1, 0:n], in_=STT[:, 1, 0:n], func=AF.Sqrt))
            S.op("dve", [s_CF32b], [s_CF32b], lambda e: e.reciprocal(out=STT[:, 1, 0:n], in_=STT[:, 1, 0:n]))

        d_x = [S.new_dma_sem("x0"), S.new_dma_sem("x1")]

        def phase0_block(st, blk):
            xi = blk % 2
            xb, sxb = XBS[xi], s_XBS[xi]
            sp16 = [s_PB[4], s_PB[5]]
            S.dma("pool", d_x[xi], [], [sxb], xb[:], x_in[st, blk * 128:(blk + 1) * 128, :])
            fns = [lambda e, k=k: e.transpose(P16B[:, k * 128:(k + 1) * 128], xb[:, k * 128:(k + 1) * 128], IDB[:])
                   for k in range(KT)]
            S.mm_group([sxb, s_const], sp16, fns)
            S.op("dve", sp16, [s_xT[blk]], lambda e: e.tensor_copy(
                out=xT[:, :, blk * 128:(blk + 1) * 128], in_=P16B.rearrange("p (k c) -> p k c", k=KT)))

        def emit_layer(st, li, NB, first_in_prog, last_in_prog, skip_phase0=False, hook=None):
            l = li
            Tm = NB * 128
            W = HALO + Tm
            MAIN0 = 128
            tcol = lambda c: c - 96
            ppc = lambda c0, c1=None: PP[:, l, c0:(c0 + 1 if c1 is None else c1)]

            if first_in_prog and not skip_phase0:
                for blk in range(NB + 1):
                    rb = blk % 3
                    S.dma("sp", d_r[rb], [], [s_R[rb]], R[rb][:], x_in[st, blk * 128:(blk + 1) * 128, :])
                    xi = blk % 2
                    xb, sxb = XBS[xi], s_XBS[xi]
                    pp16 = P16 if xi == 0 else P16B
                    sp16 = [s_P16] if xi == 0 else [s_PB[4], s_PB[5]]
                    S.op("act", [s_R[rb]], [sxb], lambda e: e.activation(out=xb[:], in_=R[rb][:], func=AF.Copy))
                    fns = [lambda e, k=k: e.transpose(pp16[:, k * 128:(k + 1) * 128], xb[:, k * 128:(k + 1) * 128], IDB[:])
                           for k in range(KT)]
                    S.mm_group([sxb, s_const], sp16, fns)
                    S.op("dve", sp16, [s_xT[blk]], lambda e: e.tensor_copy(
                        out=xT[:, :, blk * 128:(blk + 1) * 128], in_=pp16.rearrange("p (k c) -> p k c", k=KT)))

            mchunks = chunks(MAIN0, MAIN0 + Tm)
            first_merge = [True]

            def phase2(i):
                for d in range(16):
                    if d % 2 == 0:
                        uib = next_unit(li, "br")
                        slb = uib % NW
                    uig = next_unit(li, "in")
                    slg = uig % NW
                    for (c0, c1) in mchunks:
                        n = c1 - c0
                        bg, bp = nbank(), nbank()
                        fns = [lambda e, k=k: e.matmul(PB[bg][:, 0:n], lhsT=WR[:, slg, k * 128:(k + 1) * 128],
                                                       rhs=xT[:, k, c0:c1], start=(k == 0), stop=(k == KT - 1))
                               for k in range(KT)]
                        S.mm_group([s_WR[slg]] + xblocks(c0, c1), [s_PB[bg]], fns)
                        o = (d % 2) * 1024
                        fns = [lambda e, k=k: e.matmul(PB[bp][:, 0:n], lhsT=WR[:, slb, o + k * 128:o + (k + 1) * 128],
                                                       rhs=Y[:, k, c0 - MAIN0:c1 - MAIN0], start=(k == 0), stop=(k == 7))
                               for k in range(8)]
                        S.mm_group([s_WR[slb]] + s_Y, [s_PB[bp]], fns)
                        j = d % 2
                        S.op("act", [s_PB[bg]], [s_SQ[j]], lambda e: e.activation(
                            out=SQ[:, j, 0:n], in_=PB[bg][:, 0:n], func=AF.Sigmoid))
                        mg = MG[:, d, c0 - MAIN0:c1 - MAIN0]
                        if first_merge[0]:
                            S.op("dve", [s_PB[bp], s_SQ[j]], [s_MG[d]], lambda e: e.tensor_tensor(
                                out=mg, in0=PB[bp][:, 0:n], in1=SQ[:, j, 0:n], op=ALU.mult))
                        else:
                            S.op("dve", [s_PB[bp], s_SQ[j]], [s_FT[j]], lambda e: e.tensor_tensor(
                                out=FT[:, j, 0:n], in0=PB[bp][:, 0:n], in1=SQ[:, j, 0:n], op=ALU.mult))
                            S.op("dve", [s_FT[j], s_MG[d]], [s_MG[d]], lambda e: e.tensor_tensor(
                                out=mg, in0=FT[:, j, 0:n], in1=mg, op=ALU.add))
                    release(uig)
                    if d % 2 == 1:
                        release(uib)
                first_merge[0] = False

            def skip_units(kinds):
                for k_ in kinds:
                    next_unit(li, k_)

            def branch_A():
                S.dma("pool", d_cp, [s_CB16b], [s_CB16b], PW[:].rearrange("p a b c -> p (a b c)"), pw_in[l])
                for t in range(8):
                    inproj(li, "in", 96, MAIN0 + Tm, lambda b, ps, c0, c1, t=t: evac_copy(
                        b, ps, T1[:, t, tcol(c0):tcol(c1)], [s_T1[t]]))
                for g in range(4):
                    win = 2 ** (g + 1)
                    X = T1[:, 2 * g:2 * g + 2, :]
                    sX = [s_T1[2 * g], s_T1[2 * g + 1]]
                    sP, sQ = [s_T2], s_Y
                    S.op("dve", sX, sP, lambda e: e.tensor_tensor(
                        out=PQ[:, :, 1:W], in0=X[:, :, 1:W], in1=X[:, :, 0:W - 1], op=ALU.add))
                    cur, scur = PQ, sP
                    if win >= 4:
                        S.op("dve", sP, sQ, lambda e: e.tensor_tensor(
                            out=QQ[:, :, 3:W], in0=PQ[:, :, 3:W], in1=PQ[:, :, 1:W - 2], op=ALU.add))
                        cur, scur = QQ, sQ
                    if win >= 8:
                        S.op("dve", sQ, sP, lambda e: e.tensor_tensor(
                            out=PQ[:, :, 7:W], in0=QQ[:, :, 7:W], in1=QQ[:, :, 3:W - 4], op=ALU.add))
                        cur, scur = PQ, sP
                    if win >= 16:
                        S.op("dve", sP, sQ, lambda e: e.tensor_tensor(
                            out=QQ[:, :, 15:W], in0=PQ[:, :, 15:W], in1=PQ[:, :, 7:W - 8], op=ALU.add))
                        cur, scur = QQ, sQ
                    fixc = None
                    if st == 0:
                        fixc = HALO + (128 if (fused and li == 0) else 0)
                        S.op("dve", scur + [s_const], [s_FX], lambda e: e.tensor_tensor(
                            out=FX[:], in0=cur[:, :, fixc:fixc + 16],
                            in1=POOLFIX[:, g * 16:(g + 1) * 16].unsqueeze(1).broadcast_to([128, 2, 16]), op=ALU.mult))
                        S.op("dve", sX + [s_FX], [s_FX], lambda e: e.tensor_tensor(
                            out=FX[:], in0=FX[:], in1=X[:, :, fixc:fixc + 16], op=ALU.subtract))
                    S.op("dve", scur + sX, sX, lambda e: e.scalar_tensor_tensor(
                        out=X[:, :, HALO:W], in0=cur[:, :, HALO:W], scalar=1.0 / win, in1=X[:, :, HALO:W],
                        op0=ALU.mult, op1=ALU.subtract))
                    if fixc is not None:
                        S.op("dve", [s_FX], sX, lambda e: e.tensor_copy(out=X[:, :, fixc:fixc + 16], in_=FX[:]))
                for t in range(8):
                    inproj(li, "in", MAIN0, MAIN0 + Tm, lambda b, ps, c0, c1, t=t: S.op(
                        "act", [s_PB[b]], [s_T0[t]], lambda e: e.activation(
                            out=T0[:, t, tcol(c0):tcol(c1)], in_=ps, func=AF.Silu)))
                for g in range(4):
                    for dtl in range(2):
                        t = 2 * g + dtl
                        for (c0, c1) in chunks(HALO, W):
                            n = c1 - c0
                            b = nbank()
                            fns = [lambda e, ct=ct: e.matmul(PB[b][:, 0:n], lhsT=PW[:, g, ct, dtl * 128:(dtl + 1) * 128],
                                                             rhs=T1[:, 2 * g + ct, c0:c1], start=(ct == 0), stop=(ct == 1))
                                   for ct in range(2)]
                            S.mm_group([s_T1[2 * g], s_T1[2 * g + 1], s_CB16b], [s_PB[b]], fns)
                            S.op("dve", [s_PB[b], s_T0[t], s_const], [s_Y[t]], lambda e: e.scalar_tensor_tensor(
                                out=Y[:, t, c0 - HALO:c1 - HALO], in0=PB[b][:, 0:n], scalar=ppc(t),
                                in1=T0[:, t, c0:c1], op0=ALU.mult, op1=ALU.mult))

            def branch_B():
                S.dma("sp", d_c2, [s_FT[0], s_FT[1]], [s_FT[0], s_FT[1]],
                      FT[:, 0:2, :].rearrange("p a b -> p (a b)"), wst_in[l])
                for h in range(8):
                    S.op("dve", [s_FT[0], s_FT[1], s_const], [s_CB16b], lambda e: e.tensor_tensor(
                        out=WST[:, h, :], in0=FT[:, 0:2, :].rearrange("p a b -> p (a b)")[:, h * 128:(h + 1) * 128],
                        in1=MASK, op=ALU.mult))
                S.dma("sp", d_c2, [s_CF32], [s_CF32], BSB[:].rearrange("p a b -> p (a b)"), bsb_in[l])
                for t in range(8):
                    inproj(li, "in", MAIN0, MAIN0 + Tm, lambda b, ps, c0, c1, t=t: evac_copy(
                        b, ps, T1[:, t, tcol(c0):tcol(c1)], [s_T1[t]]))
                def ev_bg(t):
                    return lambda b, ps, c0, c1: S.op(
                        "act", [s_PB[b]], [s_T0[t]], lambda e: e.activation(
                            out=T0[:, t, tcol(c0):tcol(c1)], in_=ps, func=AF.Silu))

                def ev_u(t):
                    return lambda b, ps, c0, c1: S.op(
                        "dve", [s_PB[b], s_T0[t]], [s_T0[t]], lambda e: e.tensor_tensor(
                            out=T0[:, t, tcol(c0):tcol(c1)], in0=ps, in1=T0[:, t, tcol(c0):tcol(c1)], op=ALU.mult))

                gq = chain([inproj_gen(li, "in", MAIN0, MAIN0 + Tm, ev_bg(t)) for t in range(8)] +
                           [inproj_gen(li, "in", MAIN0, MAIN0 + Tm, ev_u(t)) for t in range(8)])
                for (c0, c1) in chunks(HALO, W):
                    n = c1 - c0
                    ln_stats(T1, s_T1, c0, c1, 1024)
                    for t in range(8):
                        j = t % 4
                        S.op("dve", [s_T1[t], s_CF32b], [s_FT[j]], lambda e: e.tensor_tensor(
                            out=FT[:, j, 0:n], in0=T1[:, t, c0:c1], in1=STT[:, 0, 0:n], op=ALU.subtract))
                        S.op("dve", [s_FT[j], s_CF32b], [s_FT[j]], lambda e: e.tensor_tensor(
                            out=FT[:, j, 0:n], in0=FT[:, j, 0:n], in1=STT[:, 1, 0:n], op=ALU.mult))
                        S.op("act", [s_FT[j], s_const], [s_T1[t]], lambda e: e.activation(
                            out=T1[:, t, c0:c1], in_=FT[:, j, 0:n], func=AF.Identity,
                            scale=ppc(8 + t), bias=ppc(16 + t)))
                        next(gq, None)
                for _ in gq:
                    pass
                s_VN = [St(), St()]
                inherit(s_VN, [s_T2])
                bank_lim[0] = 4

                def stage_T(blk):
                    tc0 = HALO + blk * 128
                    vb = blk % 2
                    p16, sp = (P16, [s_P16]) if vb == 0 else (P16B, [s_PB[4], s_PB[5]])
                    fns = [lambda e, h=h: e.transpose(p16[:, h * 128:(h + 1) * 128], T1[:, h, tc0:tc0 + 128], IDB[:])
                           for h in range(8)]
                    S.mm_group(s_T1 + [s_const], sp, fns)
                    S.op("dve", sp, [s_VN[vb]], lambda e: e.tensor_copy(out=VN[:, vb, :], in_=p16[:, 0:1024]))

                def stage_S(blk):
                    tc0 = HALO + blk * 128
                    vb = blk % 2
                    for hh in range(2):
                        b = nbank()
                        for h4 in range(4):
                            h = hh * 4 + h4
                            S.mm_group([s_VN[vb], s_CB16b], [s_PB[b]], [lambda e: e.matmul(
                                PB[b][:, h4 * 128:(h4 + 1) * 128], lhsT=VN[:, vb, h * 128:(h + 1) * 128],
                                rhs=WST[:, h, :], start=True, stop=True)])
                        j = hh
                        S.op("dve", [s_PB[b], s_CF32], [s_FT[j]], lambda e: e.tensor_tensor(
                            out=FT[:, j, :], in0=PB[b][:], in1=BSB[:, hh, :], op=ALU.add))
                        S.op("dve", [s_FT[j]] + s_T0[hh * 4:hh * 4 + 4], s_Y[hh * 4:hh * 4 + 4], lambda e: e.tensor_tensor(
                            out=Y[:, hh * 4:hh * 4 + 4, blk * 128:(blk + 1) * 128],
                            in0=FT[:, j, :].rearrange("p (a b) -> p a b", a=4),
                            in1=T0[:, hh * 4:hh * 4 + 4, tc0:tc0 + 128], op=ALU.mult))

                stage_T(0)
                for blk in range(NB):
                    if blk + 1 < NB:
                        stage_T(blk + 1)
                    stage_S(blk)
                bank_lim[0] = 6
                inherit([s_T2], s_VN)

            def branch_C():
                inherit(s_LT, [s_CF32b])
                S.dma("pool", d_cp, [s_CB16], [s_CB16], BT[:].rearrange("p a b c d -> p (a b c d)"), bt_in)
                S.op("act", [s_const], [s_SC], lambda e: e.activation(out=SC[:, 0:8], in_=ppc(48, 56), func=AF.Exp))
                for i8 in range(8):
                    S.op("dve", [s_SC, s_const], [s_CF32], lambda e: e.tensor_scalar(
                        out=SINKBC[:, i8 // 4, i8 % 4, :], in0=CST[:, 0:128], scalar1=0.0, scalar2=SC[:, i8:i8 + 1],
                        op0=ALU.mult, op1=ALU.add))
                for t in range(8):
                    inproj(li, "in", MAIN0, MAIN0 + Tm, lambda b, ps, c0, c1, t=t: S.op(
                        "act", [s_PB[b]], [s_T1[t]], lambda e: e.activation(
                            out=T1[:, t, tcol(c0):tcol(c1)], in_=ps, func=AF.Silu)))
                for t in range(8):
                    inproj(li, "in", MAIN0, MAIN0 + Tm, lambda b, ps, c0, c1, t=t: evac_copy(
                        b, ps, T0[:, t, tcol(c0):tcol(c1)], [s_T0[t]], scale=0.125))
                Te = MAIN0 + Tm
                for kvi in range(2):
                    inproj(li, "kd", 0, Te, lambda b, ps, c0, c1, kvi=kvi: evac_copy(
                        b, ps, KD[:, kvi, c0:c1], [s_T2]))
                inproj(li, "in", 0, Te, lambda b, ps, c0, c1: evac_copy(b, ps, VT[:, c0:c1], [s_T2]))
                S.op("dve", [], [s_T2], lambda e: e.memset(VA[:].rearrange("p a b c d -> p (a b c d)"), 0.0))
                nbe = NB + 1
                fns = [lambda e, bb=bb: e.transpose(P16[:, bb * 128:(bb + 1) * 128], VT[:, bb * 128:(bb + 1) * 128], IDB[:])
                       for bb in range(nbe)]
                S.mm_group([s_T2, s_const], [s_P16], fns)
                pv = P16[:, 0:nbe * 128].rearrange("p (a b) -> p a b", a=nbe)
                for kvi in range(2):
                    for var in range(2):
                        S.op("dve", [s_P16], [s_T2], lambda e: e.tensor_copy(
                            out=VA[:, 0:nbe, kvi, var, var * 64:var * 64 + 64], in_=pv[:, :, kvi * 64:kvi * 64 + 64]))
                PTB = [PT, FT[:, 2:4, :].bitcast(BF16).rearrange("p a (b c) -> p (a b) c", b=2)]
                sPTB = [[s_CB16b], [s_FT[2], s_FT[3]]]
                its = [(blk, kvi) for blk in range(1, NB + 1) for kvi in range(2)]

                def stage_L(it):
                    blk, kvi = its[it]
                    tc0 = HALO + (blk - 1) * 128
                    ptb, sptb = PTB[it % 2], sPTB[it % 2]
                    for kbi, kb in enumerate((blk - 1, blk)):
                        b0, b1 = kbi * 2, kbi * 2 + 1
                        fns = []
                        for var in range(2):
                            fns.append(lambda e, var=var, bb=kbi * 2 + var: e.matmul(
                                PB[bb][:].rearrange("p (a b) -> p a b", a=4),
                                lhsT=KD[var * 64:var * 64 + 64, kvi, kb * 128:(kb + 1) * 128],
                                rhs=T0[var * 64:var * 64 + 64, 4 * kvi:4 * kvi + 4, tc0:tc0 + 128],
                                start=True, stop=False))
                        for var in range(2):
                            fns.append(lambda e, var=var, bb=kbi * 2 + var: e.matmul(
                                PB[bb][:], lhsT=IDB[:], rhs=BT[:, kbi, kvi, var, :], start=False, stop=True))
                        S.mm_group([s_T2, s_CB16, s_const] + s_T0[4 * kvi:4 * kvi + 4], [s_PB[b0], s_PB[b1]], fns)
                        for var in range(2):
                            pidx = kbi * 2 + var
                            S.op("act", [s_PB[pidx]], sptb, lambda e: e.activation(
                                out=ptb[:, pidx, :], in_=PB[pidx][:], func=AF.Exp))

                def stage_V(it):
                    blk, kvi = its[it]
                    tc0 = HALO + (blk - 1) * 128
                    ptb, sptb = PTB[it % 2], sPTB[it % 2]
                    bo, bd = 4, 5
                    fo, fd = [], []
                    for kbi, kb in enumerate((blk - 1, blk)):
                        for var in range(2):
                            pidx = kbi * 2 + var
                            first, last = (pidx == 0), (pidx == 3)
                            fo.append(lambda e, kb=kb, var=var, pidx=pidx, first=first, last=last: e.matmul(
                                PB[bo][:], lhsT=VA[:, kb, kvi, var, :], rhs=ptb[:, pidx, :], start=first, stop=last))
                            hsel = 0 if (st == 0 and (kb == 0 or (fused and li == 0 and kb == 1))) else 1
                            fd.append(lambda e, hsel=hsel, var=var, pidx=pidx, first=first, last=last: e.matmul(
                                PB[bd][:], lhsT=ONESV[:, hsel, var, :], rhs=ptb[:, pidx, :], start=first, stop=last))
                    S.mm_group([s_T2] + sptb, [s_PB[bo]], fo)
                    S.mm_group([s_const] + sptb, [s_PB[bd]], fd)
                    S.op("dve", [s_PB[bd], s_CF32], [s_FT[0]], lambda e: e.tensor_tensor(
                        out=FT[:, 0, :], in0=PB[bd][:], in1=SINKBC[:, kvi].rearrange("p a b -> p (a b)"), op=ALU.add))
                    S.op("act", [s_FT[0]], [s_FT[0]], lambda e: e.activation(out=FT[:, 0, :], in_=FT[:, 0, :], func=AF.Ln))
                    S.op("act", [s_FT[0]], [s_FT[0]], lambda e: e.activation(
                        out=FT[:, 0, :], in_=FT[:, 0, :], func=AF.Exp, scale=-1.0))
                    S.op("dve", [s_PB[bo], s_FT[0]], [s_FT[1]], lambda e: e.tensor_tensor(
                        out=FT[:, 1, :], in0=PB[bo][:], in1=FT[:, 0, :], op=ALU.mult))
                    S.op("dve", [s_FT[1]] + s_T1[4 * kvi:4 * kvi + 4], s_Y[4 * kvi:4 * kvi + 4], lambda e: e.tensor_tensor(
                        out=Y[:, 4 * kvi:4 * kvi + 4, (blk - 1) * 128:blk * 128],
                        in0=FT[:, 1, :].rearrange("p (a b) -> p a b", a=4),
                        in1=T1[:, 4 * kvi:4 * kvi + 4, tc0:tc0 + 128], op=ALU.mult))

                stage_L(0)
                for it in range(len(its)):
                    if it + 1 < len(its):
                        stage_L(it + 1)
                    stage_V(it)
                bank_rr[0] = 0
                inherit([s_CF32b], s_LT)

            def branch_D():
                for t in range(8):
                    inproj(li, "in", 96, MAIN0 + Tm, lambda b, ps, c0, c1, t=t: S.op(
                        "act", [s_PB[b]], [s_T0[t]], lambda e: e.activation(
                            out=T0[:, t, tcol(c0):tcol(c1)], in_=ps, func=AF.Sigmoid)))
                for t in range(8):
                    inproj(li, "in", 96, MAIN0 + Tm, lambda b, ps, c0, c1, t=t: S.op(
                        "dve", [s_PB[b], s_T0[t]], [s_T0[t]], lambda e: e.tensor_tensor(
                            out=T0[:, t, tcol(c0):tcol(c1)], in0=ps, in1=T0[:, t, tcol(c0):tcol(c1)], op=ALU.mult)))
                s_DG2 = St()
                inherit([s_DG2], s_T1)
                for t in range(8):
                    cw = PP[:, l, 56 + t * 31:56 + (t + 1) * 31]
                    DG, sdg = (DIAG, s_CB16) if t % 2 == 0 else (DIAG2, s_DG2)
                    S.op("dve", [s_const], [sdg], lambda e: e.tensor_tensor(
                        out=DG[:], in0=IDB[:].unsqueeze(1).broadcast_to([128, 31, 128]),
                        in1=cw.unsqueeze(2).broadcast_to([128, 31, 128]), op=ALU.mult))
                    for (c0, c1) in reversed(chunks(HALO, W)):
                        n = c1 - c0
                        b = nbank()
                        fns = [lambda e, j=j: e.matmul(PB[b][:, 0:n], lhsT=DG[:, j, :],
                                                       rhs=T0[:, t, c0 - 30 + j:c1 - 30 + j], start=(j == 0), stop=(j == 30))
                               for j in range(31)]
                        S.mm_group([s_T0[t], sdg], [s_PB[b]], fns)
                        S.op("act", [s_PB[b], s_const], [s_T0[t]], lambda e: e.activation(
                            out=T0[:, t, c0:c1], in_=PB[b][:, 0:n], func=AF.Identity, bias=ppc(24 + t)))
                inherit(s_T1, [s_DG2])

                def ev_dg(t):
                    return lambda b, ps, c0, c1: S.op(
                        "act", [s_PB[b]], [s_T1[t]], lambda e: e.activation(
                            out=T1[:, t, tcol(c0):tcol(c1)], in_=ps, func=AF.Silu))

                gq = chain([inproj_gen(li, "in", MAIN0, MAIN0 + Tm, ev_dg(t)) for t in range(8)])
                for (c0, c1) in chunks(HALO, W):
                    n = c1 - c0
                    ln_stats(T0, s_T0, c0, c1, 1024)
                    for t in range(8):
                        fa = t % 4
                        S.op("dve", [s_T0[t], s_CF32b], [s_FT[fa]], lambda e: e.tensor_tensor(
                            out=FT[:, fa, 0:n], in0=T0[:, t, c0:c1], in1=STT[:, 0, 0:n], op=ALU.subtract))
                        S.op("dve", [s_FT[fa], s_CF32b], [s_FT[fa]], lambda e: e.tensor_tensor(
                            out=FT[:, fa, 0:n], in0=FT[:, fa, 0:n], in1=STT[:, 1, 0:n], op=ALU.mult))
                        S.op("act", [s_FT[fa], s_const], [s_T0[t]], lambda e: e.activation(
                            out=T0[:, t, c0:c1], in_=FT[:, fa, 0:n], func=AF.Silu, scale=ppc(32 + t), bias=ppc(40 + t)))
                        next(gq, None)
                for _ in gq:
                    pass
                for t in range(8):
                    S.op("dve", [s_T0[t], s_T1[t]], [s_Y[t]], lambda e: e.tensor_tensor(
                        out=Y[:, t, 0:Tm], in0=T0[:, t, HALO:W], in1=T1[:, t, HALO:W], op=ALU.mult))

            inherit(s_T0, [s_R[0], s_R[1]])
            inherit(s_T1, [s_LN])
            inherit([s_T2], [s_R[2], s_XB, s_XBS[1]])
            bank_lim[0] = 6
            if len(branches) < 4:
                for d_ in range(16):
                    S.op("dve", [], [s_MG[d_]], lambda e: e.memset(MG[:, d_, :], 0.0))
                first_merge[0] = False
            brs = [branch_A, branch_B, branch_C, branch_D]
            nin = [16, 24, 19, 24]
            for i in range(4):
                if i in branches:
                    brs[i]()
                    phase2(i)
                else:
                    for _ in range(nin[i] + 24):
                        prefetch()
                        wstate["used"] += 1
                        prefetch()

            inherit([s_LN], s_T1)
            S.dma("sp", d_ln, [s_LN], [s_LN], LNG[:], lng_in[l])
            S.dma("sp", d_ln, [s_LN], [s_LN], LNB[:], lnb_in[l])
            inherit([s_R[0], s_R[1]], s_T0)
            inherit([s_R[2], s_XB, s_XBS[1]], [s_T2])
            bank_lim[0] = 4
            if li == 0 and first_in_prog:
                rsrc = lambda blk: x_in[st, blk * 128:(blk + 1) * 128, :]
                rst = None
            else:
                rsrc = lambda blk: x1s[st, (blk - 1) * 128:blk * 128, :]
                rst = s_x1s[st]

            def load_resid(blk):
                rb = blk % 3
                S.dma("sp", d_r[rb], [rst] if rst is not None else [], [s_R[rb]], R[rb][:], rsrc(blk))

            load_resid(1)
            load_resid(2)
            load_resid(3)
            for e_ in range(16):
                uio = next_unit(li, "out")
                sl = uio % NW
                for (c0, c1) in mchunks:
                    n = c1 - c0
                    b = nbank()
                    fns = [lambda e, k=k: e.matmul(PB[b][:, 0:n], lhsT=WR[:, sl, k * 128:(k + 1) * 128],
                                                   rhs=MG[:, k, c0 - MAIN0:c1 - MAIN0], start=(k == 0), stop=(k == KT - 1))
                           for k in range(KT)]
                    S.mm_group([s_WR[sl]] + s_MG, [s_PB[b]], fns)
                    evac_copy(b, PB[b][:, 0:n], xT[:, e_, c0:c1], xblocks(c0, c1))
                release(uio)
            sshift = 1 if (fused and li == 0 and st == 0) else 0

            def stage_A(blk):
                rb = blk % 3
                fns = [lambda e, k=k: e.transpose(P16[:, k * 128:(k + 1) * 128], xT[:, k, blk * 128:(blk + 1) * 128], IDB[:])
                       for k in range(KT)]
                S.mm_group([s_xT[blk], s_const], [s_P16], fns)
                S.op("dve", [s_P16, s_R[rb]], [s_R[rb]], lambda e: e.scalar_tensor_tensor(
                    out=R[rb][:], in0=R[rb][:], scalar=ALPHA, in1=P16[:], op0=ALU.mult, op1=ALU.add))
                o = 8 * (blk % 3)
                ssc = s_SCP[blk % 3]
                S.op("dve", [], [ssc], lambda e: e.memset(SC[:, o + 8:o + 10], 0.0))
                S.op("act", [s_R[rb]], s_Y + [ssc], lambda e: e.activation(
                    out=JK[:], in_=R[rb][:], func=AF.Identity, accum_out=SC[:, o + 8:o + 9]))
                S.op("act", [s_R[rb]], s_Y + [ssc], lambda e: e.activation(
                    out=JK[:], in_=R[rb][:], func=AF.Square, accum_out=SC[:, o + 9:o + 10]))

            def stage_A2(blk):
                o = 8 * (blk % 3)
                ssc = s_SCP[blk % 3]
                S.op("dve", [ssc], [ssc], lambda e: e.tensor_scalar(
                    out=SC[:, o + 10:o + 12], in0=SC[:, o + 8:o + 10], scalar1=1.0 / D, scalar2=None, op0=ALU.mult))
                S.op("dve", [ssc], [ssc], lambda e: e.tensor_tensor(
                    out=SC[:, o + 12:o + 13], in0=SC[:, o + 10:o + 11], in1=SC[:, o + 10:o + 11], op=ALU.mult))
                S.op("dve", [ssc], [ssc], lambda e: e.tensor_tensor(
                    out=SC[:, o + 13:o + 14], in0=SC[:, o + 11:o + 12], in1=SC[:, o + 12:o + 13], op=ALU.subtract))
                S.op("dve", [ssc], [ssc], lambda e: e.tensor_scalar(
                    out=SC[:, o + 14:o + 15], in0=SC[:, o + 13:o + 14], scalar1=LN_EPS, scalar2=None, op0=ALU.add))
                S.op("act", [ssc], [ssc], lambda e: e.activation(out=SC[:, o + 14:o + 15], in_=SC[:, o + 14:o + 15], func=AF.Sqrt))

            def stage_B(blk):
                rb = blk % 3
                o = 8 * (blk % 3)
                ssc = s_SCP[blk % 3]
                S.op("dve", [ssc], [ssc], lambda e: e.reciprocal(out=SC[:, o + 14:o + 15], in_=SC[:, o + 14:o + 15]))
                S.op("dve", [ssc, s_LN, s_R[rb]], [s_R[rb]], lambda e: e.scalar_tensor_tensor(
                    out=R[rb][:], in0=R[rb][:], scalar=SC[:, o + 10:o + 11], in1=LNG[:],
                    op0=ALU.subtract, op1=ALU.mult))
                S.op("dve", [ssc, s_LN, s_R[rb]], [s_R[rb]], lambda e: e.scalar_tensor_tensor(
                    out=R[rb][:], in0=R[rb][:], scalar=SC[:, o + 14:o + 15], in1=LNB[:],
                    op0=ALU.mult, op1=ALU.add))
                if last_in_prog:
                    S.dma("sp", d_o[rb], [s_R[rb]], [], y_out[st, (blk - 1) * 128:blk * 128, :], R[rb][:])
                else:
                    if blk - sshift >= 1:
                        S.dma("sp", d_o[rb], [s_R[rb]], [s_x1s[st]],
                              x1s[st, (blk - 1 - sshift) * 128:(blk - sshift) * 128, :], R[rb][:])
                    xi = blk % 2
                    xb, sxb = XBS[xi], s_XBS[xi]
                    if blk == 1 and st == 0:
                        S.op("act", [s_R[rb], s_const], [sxb], lambda e: e.activation(
                            out=xb[:], in_=R[rb][:], func=AF.Copy, scale=FLAG))
                    else:
                        S.op("act", [s_R[rb]], [sxb], lambda e: e.activation(out=xb[:], in_=R[rb][:], func=AF.Copy))

            def stage_C(blk):
                if last_in_prog:
                    return
                xi = blk % 2
                xb, sxb = XBS[xi], s_XBS[xi]
                sp16 = [s_PB[4], s_PB[5]]
                fns = [lambda e, k=k: e.transpose(P16B[:, k * 128:(k + 1) * 128], xb[:, k * 128:(k + 1) * 128], IDB[:])
                       for k in range(KT)]
                S.mm_group([sxb, s_const], sp16, fns)
                sl_ = blk - sshift
                if sl_ == 0 and sshift == 0:
                    return
                S.op("dve", sp16, [s_xT[sl_]], lambda e: e.tensor_copy(
                    out=xT[:, :, sl_ * 128:(sl_ + 1) * 128], in_=P16B.rearrange("p (k c) -> p k c", k=KT)))
                if sshift == 1 and blk == NB:
                    S.op("dve", sp16, [s_xT[9]], lambda e: e.tensor_copy(
                        out=xT[:, :, 9 * 128:10 * 128], in_=P16B.rearrange("p (k c) -> p k c", k=KT)))

            for i in range(1, NB + 4):
                if i <= NB:
                    stage_A(i)
                if 1 <= i - 1 <= NB:
                    stage_A2(i - 1)
                if 1 <= i - 2 <= NB:
                    stage_B(i - 2)
                    if (i - 2) + 3 <= NB:
                        load_resid((i - 2) + 3)
                if 1 <= i - 3 <= NB:
                    stage_C(i - 3)
                if hook is not None:
                    hook(i)

        for st in range(NST):
            if fused:
                emit_layer(st, 0, 9 if st == 0 else 8, True, False, skip_phase0=(st == 1))
                if st == 1:
                    S.op("dve", [s_xT[9]], [s_xT[0]], lambda e: e.tensor_copy(
                        out=xT[:, :, 0:128], in_=xT[:, :, 9 * 128:10 * 128]))
                hk = None
                if st == 0:
                    def hk(i):
                        if i == 1:
                            phase0_block(1, 0)
                            phase0_block(1, 1)
                        elif 2 <= i <= 8:
                            phase0_block(1, i)
                emit_layer(st, 1, 8, False, True, hook=hk)
            else:
                emit_layer(st, 0, 8, True, True)
        allst = s_R + s_x1s + s_xT + s_MG + s_Y + s_T0 + s_T1 + [s_T2, s_P16] + s_PB + s_WR
        S.wait_all("sp", allst)
    return nc


def _consts(core, fused):
    cst = np.zeros((128, 384), np.float32)
    cst[:, 0:128] = np.eye(128, dtype=np.float32)
    s = np.arange(128)[:, None]
    t = np.arange(128)[None, :]
    cst[:, 128:256] = (s <= t).astype(np.float32)
    cst[:, 256] = 0.0 if core == 0 else 1.0
    for g in range(4):
        win = 2 ** (g + 1)
        tt = np.arange(16)
        if core == 0:
            cst[:, 272 + g * 16:272 + (g + 1) * 16] = (1.0 / np.minimum(tt + 1, win)).astype(np.float32)[None, :]
        else:
            cst[:, 272 + g * 16:272 + (g + 1) * 16] = np.float32(1.0 / win)
    return cst


def _layer_inputs(ls, w_in, pool_w, pool_scale, sgu_ln_g, sgu_ln_b, sgu_w, sgu_b, attn_sinks, rel_bias,
                  conv_w, conv_b, conv_ln_g, conv_ln_b, w_branch, w_out, ln_g, ln_b):
    nl = len(ls)
    ws = np.stack([build_wstream(w_in[l], w_branch[l], w_out[l]) for l in ls])
    pp = np.stack([build_pp(l, pool_scale, sgu_ln_g, sgu_ln_b, conv_b, conv_ln_g, conv_ln_b, attn_sinks, conv_w)
                   for l in ls], axis=1)
    pw = np.stack([pool_w[l].reshape(4, 2, 128, 256).transpose(2, 0, 1, 3).reshape(128, 2048) for l in ls])
    wst = np.stack([sgu_w[l].transpose(2, 0, 1).reshape(128, 1024) for l in ls])
    bsb = np.stack([np.broadcast_to(sgu_b[l].reshape(1, 1024), (128, 1024)) for l in ls])
    lng = np.stack([np.broadcast_to(ln_g[l][None, :], (128, D)) for l in ls])
    lnb = np.stack([np.broadcast_to(ln_b[l][None, :], (128, D)) for l in ls])
    return dict(ws=np.ascontiguousarray(ws), pp=np.ascontiguousarray(pp), pw=np.ascontiguousarray(pw),
                wst=np.ascontiguousarray(wst), bsb=np.ascontiguousarray(bsb), lng=np.ascontiguousarray(lng),
                lnb=np.ascontiguousarray(lnb), bt=build_bias_table(rel_bias))


def _shard_x(x2d, nblk_front):
    pad = nblk_front * 128
    xp = np.concatenate([np.zeros((pad, D), np.float32), x2d], axis=0)
    out = []
    for c in range(NCORE):
        sts = []
        for st in range(NST):
            t0 = c * TOK_CORE + st * ST_TOK
            if nblk_front == 2 and st == 1:
                blk = np.zeros((pad + ST_TOK, D), np.float32)
                blk[0:128 + ST_TOK] = xp[t0 + 128:t0 + pad + ST_TOK]
                sts.append(blk)
            else:
                sts.append(xp[t0:t0 + pad + ST_TOK])
        out.append(np.ascontiguousarray(np.stack(sts)))
    return out


_PROG = {}


def _get_prog(key, *a):
    if key not in _PROG:
        _PROG[key] = build_program(*a)
    return _PROG[key]


FUSED = True


def kernel(x, w_in, pool_w, pool_scale, sgu_ln_g, sgu_ln_b, sgu_w, sgu_b, attn_sinks, rel_bias,
           conv_w, conv_b, conv_ln_g, conv_ln_b, w_branch, w_out, ln_g, ln_b):
    args = [np.asarray(a, np.float32) for a in (w_in, pool_w, pool_scale, sgu_ln_g, sgu_ln_b, sgu_w, sgu_b,
                                                 attn_sinks, rel_bias, conv_w, conv_b, conv_ln_g, conv_ln_b,
                                                 w_branch, w_out, ln_g, ln_b)]
    x2d = np.asarray(x, np.float32).reshape(SEQ, D)
    if FUSED:
        nc = _get_prog("fused", [0, 1], True)
        li = _layer_inputs([0, 1], *args)
        xs = _shard_x(x2d, 2)
        in_maps = [dict(li, x_in=xs[c], cst=_consts(c, True)) for c in range(NCORE)]
        res = run_bass_kernel_spmd(nc, in_maps, core_ids=list(range(NCORE)))
        out = np.concatenate([r["y_out"].reshape(TOK_CORE, D) for r in res.results], axis=0)
        return out.reshape(1, SEQ, D)
    cur = x2d
    for l in range(2):
        nc = _get_prog("single", [0], False)
        li = _layer_inputs([l], *args)
        xs = _shard_x(cur, 1)
        in_maps = [dict(li, x_in=xs[c], cst=_consts(c, False)) for c in range(NCORE)]
        res = run_bass_kernel_spmd(nc, in_maps, core_ids=list(range(NCORE)))
        cur = np.concatenate([r["y_out"].reshape(TOK_CORE, D) for r in res.results], axis=0)
    return cur.reshape(1, SEQ, D)
```

```python
import contextlib
import numpy as np
import concourse.bass as bass
import concourse.mybir as mybir
from concourse.bass_utils import run_bass_kernel_spmd

F32 = mybir.dt.float32
BF16 = mybir.dt.bfloat16
AF = mybir.ActivationFunctionType
ALU = mybir.AluOpType

D = 2048
SEQ = 16384
NCORE = 8
TOK_CORE = SEQ // NCORE
NST = 2
ST_TOK = TOK_CORE // NST
KT = D // 128
ALPHA = (2 * 2) ** 0.25
LN_EPS = 1e-5
NEG = -30000.0
NW = 5
HALO = 32
PPC = 56 + 8 * 31

O_AIN, O_AG, O_U, O_V, O_BG, O_Q, O_K, O_VV, O_CG, O_DV, O_DG, O_DGATE, O_GL = (
    0, 1024, 2048, 3072, 4096, 5120, 6144, 6272, 6400, 7424, 8448, 9472, 10496)


def layer_units():
    u = []
    t8 = lambda base: [("in", base + 128 * t) for t in range(8)]
    def gates(i):
        r = []
        for d in range(16):
            if d % 2 == 0:
                r.append(("br", i, d))
            r.append(("in", O_GL + i * 2048 + d * 128))
        return r
    u += t8(O_AIN) + t8(O_AG) + gates(0)
    u += t8(O_V) + t8(O_BG) + t8(O_U) + gates(1)
    u += t8(O_CG) + t8(O_Q) + [("kd", 0), ("kd", 1), ("in", O_VV)] + gates(2)
    u += t8(O_DG) + t8(O_DV) + t8(O_DGATE) + gates(3)
    u += [("out", e) for e in range(16)]
    return u


UNITS = layer_units()
NU = len(UNITS)


def build_wstream(w_in, w_branch, w_out):
    ws = np.empty((NU, 128, 2048), np.float32)
    wk = w_in.reshape(KT, 128, -1)
    for n, un in enumerate(UNITS):
        if un[0] == "in":
            c = un[1]
            ws[n] = wk[:, :, c:c + 128].transpose(1, 0, 2).reshape(128, 2048)
        elif un[0] == "kd":
            c = O_K + 64 * un[1]
            blk = wk[:, :, c:c + 64]
            ws[n] = np.concatenate([blk, blk], axis=2).transpose(1, 0, 2).reshape(128, 2048)
        elif un[0] == "br":
            _, i, d = un
            wb = w_branch[i].reshape(8, 128, 2048)
            a = wb[:, :, d * 128:(d + 1) * 128].transpose(1, 0, 2).reshape(128, 1024)
            b = wb[:, :, (d + 1) * 128:(d + 2) * 128].transpose(1, 0, 2).reshape(128, 1024)
            ws[n] = np.concatenate([a, b], axis=1)
        else:
            e = un[1]
            wo = w_out.reshape(KT, 128, 2048)
            ws[n] = wo[:, :, e * 128:(e + 1) * 128].transpose(1, 0, 2).reshape(128, 2048)
    return ws


def t5_bucket_np(n):
    max_exact = 16
    nf = np.maximum(n, 1).astype(np.float32)
    large = max_exact + (np.log(nf / np.float32(max_exact)) / np.float32(np.log(128 / max_exact))
                         * np.float32(32 - max_exact)).astype(np.int32)
    large = np.minimum(large, 31)
    return np.where(n < max_exact, n, large)


def _bucket_table():
    return t5_bucket_np(np.arange(128))


def build_bias_table(rel_bias):
    bk = _bucket_table()
    kk = np.arange(128)[:, None]
    qq = np.arange(128)[None, :]
    bt = np.full((128, 2, 2, 2, 4, 128), NEG, np.float32)
    for kb in range(2):
        dist = qq - kk + (128 if kb == 0 else 0)
        valid = (dist >= 0) & (dist < 128)
        idx = bk[np.clip(dist, 0, 127)]
        for kv in range(2):
            for var in range(2):
                for j in range(4):
                    h = kv * 8 + 2 * j + var
                    vals = rel_bias[idx, h]
                    bt[:, kb, kv, var, j, :] = np.where(valid, vals, np.float32(NEG))
    return bt.reshape(128, 4096)


def build_pp(l, pool_scale, sgu_ln_g, sgu_ln_b, conv_b, conv_ln_g, conv_ln_b, attn_sinks, conv_w):
    pp = np.zeros((128, PPC), np.float32)
    col = lambda v: v.reshape(8, 128).T
    pp[:, 0:8] = col(pool_scale[l])
    pp[:, 8:16] = col(sgu_ln_g[l])
    pp[:, 16:24] = col(sgu_ln_b[l])
    pp[:, 24:32] = col(conv_b[l])
    pp[:, 32:40] = col(conv_ln_g[l])
    pp[:, 40:48] = col(conv_ln_b[l])
    for kv in range(2):
        for j in range(4):
            pp[0:64, 48 + kv * 4 + j] = attn_sinks[l, kv * 8 + 2 * j]
            pp[64:128, 48 + kv * 4 + j] = attn_sinks[l, kv * 8 + 2 * j + 1]
    cw = conv_w[l].reshape(31, 8, 128)
    pp[:, 56:] = cw.transpose(2, 1, 0).reshape(128, 8 * 31)
    return pp


class St:
    __slots__ = ("w", "r")

    def __init__(self):
        self.w = {}
        self.r = {}


def _merge(dst, src):
    for k, v in src.items():
        if dst.get(k, 0) < v:
            dst[k] = v


class Sync:
    def __init__(self, nc, es):
        self.nc = nc
        self.es = es
        self.engs = {"pe": nc.tensor, "act": nc.scalar, "dve": nc.vector, "pool": nc.gpsimd, "sp": nc.sync}
        self.sems = {}
        self.cnt = {}
        for e in ("pe", "act", "dve", "pool"):
            self.sems[e] = es.enter_context(nc.semaphore("s_" + e))
            self.cnt[e] = 0
        self.known = {e: {} for e in self.engs}
        self.ndma = 0

    def new_dma_sem(self, name):
        nm = "d_" + name
        self.sems[nm] = self.es.enter_context(self.nc.semaphore(nm))
        self.cnt[nm] = 0
        return nm

    def _wait(self, eng, toks):
        kn = self.known[eng]
        for s, v in toks.items():
            if eng == "pe" and s == "pe":
                continue
            if kn.get(s, 0) < v:
                self.engs[eng].wait_ge(self.sems[s], v)
                kn[s] = v

    def _deps(self, reads, writes):
        toks = {}
        for s in reads:
            _merge(toks, s.w)
        for s in writes:
            _merge(toks, s.w)
            _merge(toks, s.r)
        return toks

    def op(self, eng, reads, writes, fn):
        self._wait(eng, self._deps(reads, writes))
        inst = fn(self.engs[eng])
        self.cnt[eng] += 1
        inst.then_inc(self.sems[eng], 1)
        tok = {eng: self.cnt[eng]}
        for s in reads:
            _merge(s.r, tok)
        for s in writes:
            s.w = dict(tok)
            s.r = {}
        return tok

    def mm_group(self, reads, writes, fns):
        self._wait("pe", self._deps(reads, writes))
        for f in fns[:-1]:
            f(self.engs["pe"])
        inst = fns[-1](self.engs["pe"])
        self.cnt["pe"] += 1
        inst.then_inc(self.sems["pe"], 1)
        tok = {"pe": self.cnt["pe"]}
        for s in reads:
            _merge(s.r, tok)
        for s in writes:
            s.w = dict(tok)
            s.r = {}
        return tok

    def dma(self, q, dsem, reads, writes, out, in_):
        toks = self._deps(reads, writes)
        if self.cnt[dsem] > 0:
            _merge(toks, {dsem: self.cnt[dsem]})
        self._wait(q, toks)
        inst = self.engs[q].dma_start(out=out, in_=in_)
        self.cnt[dsem] += 16
        inst.then_inc(self.sems[dsem], 16)
        tok = {dsem: self.cnt[dsem]}
        for s in reads:
            _merge(s.r, tok)
        for s in writes:
            s.w = dict(tok)
            s.r = {}
        self.ndma += 1
        return tok

    def wait_all(self, eng, states):
        toks = {}
        for s in states:
            _merge(toks, s.w)
            _merge(toks, s.r)
        self._wait(eng, toks)


def inherit(dst, src):
    for d_ in dst:
        for s in src:
            _merge(d_.w, s.w)
            _merge(d_.w, s.r)


def chunks(lo, hi, mx=512):
    n = hi - lo
    k = -(-n // mx)
    base = -(-n // k)
    base = -(-base // 8) * 8
    out = []
    c = lo
    while c < hi:
        out.append((c, min(hi, c + base)))
        c += base
    return out


def build_program(layers, fused, branches=(0, 1, 2, 3)):
    nl = len(layers)
    NB1 = 9 if fused else 8
    XROWS = (NB1 + 1) * 128
    nc = bass.Bass("TRN2", target_bir_lowering=False)
    dt = nc.dram_tensor
    x_in = dt("x_in", [NST, XROWS, D], F32, kind="ExternalInput").ap()
    ws = dt("ws", [nl, NU, 128, 2048], F32, kind="ExternalInput").ap()
    pp_in = dt("pp", [128, nl, PPC], F32, kind="ExternalInput").ap()
    pw_in = dt("pw", [nl, 128, 2048], F32, kind="ExternalInput").ap()
    wst_in = dt("wst", [nl, 128, 1024], F32, kind="ExternalInput").ap()
    bsb_in = dt("bsb", [nl, 128, 1024], F32, kind="ExternalInput").ap()
    lng_in = dt("lng", [nl, 128, 2048], F32, kind="ExternalInput").ap()
    lnb_in = dt("lnb", [nl, 128, 2048], F32, kind="ExternalInput").ap()
    bt_in = dt("bt", [128, 4096], F32, kind="ExternalInput").ap()
    cst_in = dt("cst", [128, 384], F32, kind="ExternalInput").ap()
    y_out = dt("y_out", [NST, ST_TOK, D], F32, kind="ExternalOutput").ap()
    x1s = dt("x1s", [NST, 8 * 128, D], F32, kind="Internal").ap() if fused else None

    es = contextlib.ExitStack()
    with es:
        S = Sync(nc, es)
        off = [17536]

        def alloc(name, shape, dtype, at=None):
            nbytes = int(np.prod(shape[1:])) * (2 if dtype == BF16 else 4)
            if at is None:
                at = off[0]
                off[0] += (nbytes + 63) // 64 * 64
                assert off[0] <= 229344, (name, off[0])
            return nc.alloc_sbuf_tensor_at(name, list(shape), dtype, offset=at), at

        TW = HALO + NB1 * 128 if fused else HALO + 9 * 128
        TW = HALO + 9 * 128
        xT, _ = alloc("xT", [128, KT, 1280], BF16)
        MG, _ = alloc("MG", [128, KT, 1152], BF16)
        WR, _ = alloc("WR", [128, NW, 2048], BF16)
        Y, aY = alloc("Y", [128, 8, 1152], BF16)
        T0, aT0 = alloc("T0", [128, 8, TW], BF16)
        T1, aT1 = alloc("T1", [128, 8, TW], BF16)
        T2, aT2 = alloc("T2", [128, 9216], BF16)
        CB16, aCB16 = alloc("CB16", [128, 4096], BF16)
        CB16b, aCB16b = alloc("CB16b", [128, 2048], BF16)
        CF32, aCF32 = alloc("CF32", [128, 1024], F32)
        CF32b, aCF32b = alloc("CF32b", [128, 1024], F32)
        FT, _ = alloc("FT", [128, 4, 512], F32)
        SQ, _ = alloc("SQ", [128, 2, 512], BF16)
        PP, _ = alloc("PP", [128, nl, PPC], F32)
        IDB, _ = alloc("IDB", [128, 128], BF16)
        ONESV, _ = alloc("ONESV", [128, 2, 2, 128], BF16)
        ONES, _ = alloc("ONES", [128, 128], BF16)
        CST, _ = alloc("CST", [128, 384], F32)
        SC, _ = alloc("SC", [128, 32], F32)
        FX, _ = alloc("FX", [128, 2, 16], F32)
        R = [alloc("R0", [128, 2048], F32, at=aT0)[0], alloc("R1", [128, 2048], F32, at=aT0 + 8192)[0],
             alloc("R2", [128, 2048], F32, at=aT2)[0]]
        XB, _ = alloc("XB", [128, 2048], BF16, at=aT2 + 8192)
        XB2, _ = alloc("XB2", [128, 2048], BF16, at=aT2 + 12288)
        JK, _ = alloc("JK", [128, 2048], BF16, at=aY)
        XBS = [XB, XB2]
        VN, _ = alloc("VN", [128, 2, 1024], BF16, at=aT2 + 8192)
        LNG, _ = alloc("LNG", [128, 2048], F32, at=aT1)
        LNB, _ = alloc("LNB", [128, 2048], F32, at=aT1 + 8192)
        PQ, _ = alloc("PQ", [128, 2, TW], BF16, at=aT2)
        QQ, _ = alloc("QQ", [128, 2, TW], BF16, at=aY)
        KD, _ = alloc("KD", [128, 2, 1280], BF16, at=aT2)
        VT, _ = alloc("VT", [128, 1280], BF16, at=aT2 + 5120)
        VA, _ = alloc("VA", [128, 10, 2, 2, 128], BF16, at=aT2 + 7680)
        BT, _ = alloc("BT", [128, 2, 2, 2, 512], BF16, at=aCB16)
        DIAG, _ = alloc("DIAG", [128, 31, 128], BF16, at=aCB16)
        DIAG2, _ = alloc("DIAG2", [128, 31, 128], BF16, at=aT1)
        PW, _ = alloc("PW", [128, 4, 2, 256], BF16, at=aCB16b)
        WST, _ = alloc("WST", [128, 8, 128], BF16, at=aCB16b)
        PT, _ = alloc("PT", [128, 4, 512], BF16, at=aCB16b)
        BSB, _ = alloc("BSB", [128, 2, 512], F32, at=aCF32)
        SINKBC, _ = alloc("SINKBC", [128, 2, 4, 128], F32, at=aCF32)
        LT, _ = alloc("LT", [128, 2, 512], F32, at=aCF32b)
        STT, _ = alloc("STT", [128, 2, 512], F32, at=aCF32b)

        PBALL = es.enter_context(nc.psum_tensor("pball", [128, 8, 512], F32))
        PB = [PBALL[:, i, :] for i in range(6)]
        P16 = PBALL[:, 6:8, :].bitcast(BF16).rearrange("p a b -> p (a b)")
        P16B = PBALL[:, 4:6, :].bitcast(BF16).rearrange("p a b -> p (a b)")

        s_xT = [St() for _ in range(10)]
        s_MG = [St() for _ in range(16)]
        s_WR = [St() for _ in range(NW)]
        s_Y = [St() for _ in range(8)]
        s_T0 = [St() for _ in range(8)]
        s_T1 = [St() for _ in range(8)]
        s_T2 = St()
        s_CB16, s_CB16b, s_CF32, s_CF32b = St(), St(), St(), St()
        s_FT = [St(), St(), St(), St()]
        s_LT = [St(), St()]
        s_SQ = [St(), St()]
        s_PB = [St() for _ in range(6)]
        s_P16 = St()
        s_const = St()
        s_SC = St()
        s_SCP = [St(), St(), St()]
        s_FX = St()
        s_R = [St(), St(), St()]
        s_XB = St()
        s_XBS = [s_XB, St()]
        P16S = [(P16, None), (P16B, None)]
        s_LN = St()
        s_x1s = [St() for _ in range(NST)]
        d_w = [S.new_dma_sem("w%d" % i) for i in range(NW)]
        d_r = [S.new_dma_sem("r%d" % i) for i in range(3)]
        d_o = [S.new_dma_sem("o%d" % i) for i in range(3)]
        d_c = S.new_dma_sem("c")
        d_c2 = S.new_dma_sem("c2")
        d_cp = S.new_dma_sem("cp")
        d_ln = S.new_dma_sem("ln")

        bank_rr = [0]
        bank_lim = [6]

        def nbank():
            b = bank_rr[0] % bank_lim[0]
            bank_rr[0] = (b + 1) % bank_lim[0]
            return b

        wseq = []
        for _st in range(NST):
            for li in range(nl):
                for n in range(NU):
                    wseq.append((li, n))
        wstate = {"loaded": 0, "used": 0}
        wlive = set()

        def release(i):
            wlive.discard(i)
            prefetch()

        def prefetch():
            oldest = min(wlive) if wlive else wstate["used"]
            while wstate["loaded"] < len(wseq) and wstate["loaded"] < oldest + NW:
                i = wstate["loaded"]
                li, n = wseq[i]
                sl = i % NW
                S.dma("pool", d_w[sl], [], [s_WR[sl]], WR[:, sl, :], ws[li, n])
                wstate["loaded"] += 1

        def next_unit(li, kind):
            i = wstate["used"]
            assert wseq[i][0] == li and UNITS[wseq[i][1]][0] == kind, (wseq[i], UNITS[wseq[i][1]], kind)
            prefetch()
            wstate["used"] += 1
            wlive.add(i)
            return i

        S.dma("sp", d_c, [], [s_const], PP[:], pp_in)
        S.dma("sp", d_c2, [], [s_const], CST[:], cst_in)
        S.dma("pool", d_cp, [], [s_const], IDB[:], cst_in[:, 0:128])
        MASK = CST[:, 128:256]
        FLAG = CST[:, 256:257]
        POOLFIX = CST[:, 272:336]
        S.op("dve", [], [s_const], lambda e: e.memset(ONES[:], 1.0))
        S.op("dve", [], [s_const], lambda e: e.memset(ONESV[:], 0.0))
        S.op("dve", [], [s_const], lambda e: e.memset(ONESV[:, 1, 0, 0:64], 1.0))
        S.op("dve", [], [s_const], lambda e: e.memset(ONESV[:, 1, 1, 64:128], 1.0))
        S.op("dve", [s_const], [s_const], lambda e: e.tensor_scalar(
            out=ONESV[:, 0, 0, 0:64], in0=ONESV[:, 1, 0, 0:64], scalar1=FLAG, scalar2=None, op0=ALU.mult))
        S.op("dve", [s_const], [s_const], lambda e: e.tensor_scalar(
            out=ONESV[:, 0, 1, 64:128], in0=ONESV[:, 1, 1, 64:128], scalar1=FLAG, scalar2=None, op0=ALU.mult))
        prefetch()

        def xblocks(c0, c1):
            return s_xT[c0 // 128:(c1 - 1) // 128 + 1]

        alt = [0]

        def evac_copy(bank, src, dst, dst_states, scale=None):
            alt[0] ^= 1
            if alt[0]:
                if scale is None:
                    S.op("act", [s_PB[bank]], dst_states, lambda e: e.activation(out=dst, in_=src, func=AF.Copy))
                else:
                    S.op("act", [s_PB[bank]], dst_states,
                         lambda e: e.activation(out=dst, in_=src, func=AF.Copy, scale=scale))
            else:
                if scale is None:
                    S.op("dve", [s_PB[bank]], dst_states, lambda e: e.tensor_copy(out=dst, in_=src))
                else:
                    S.op("dve", [s_PB[bank]], dst_states, lambda e: e.tensor_scalar(
                        out=dst, in0=src, scalar1=scale, scalar2=None, op0=ALU.mult))

        def inproj(li, kind, c_lo, c_hi, evac):
            ui = next_unit(li, kind)
            sl = ui % NW
            for (c0, c1) in chunks(c_lo, c_hi):
                b = nbank()
                n = c1 - c0
                fns = []
                for k in range(KT):
                    fns.append(lambda e, k=k: e.matmul(PB[b][:, 0:n], lhsT=WR[:, sl, k * 128:(k + 1) * 128],
                                                       rhs=xT[:, k, c0:c1], start=(k == 0), stop=(k == KT - 1)))
                S.mm_group([s_WR[sl]] + xblocks(c0, c1), [s_PB[b]], fns)
                evac(b, PB[b][:, 0:n], c0, c1)
            release(ui)

        def inproj_gen(li, kind, c_lo, c_hi, evac):
            ui = next_unit(li, kind)
            sl = ui % NW
            for (c0, c1) in chunks(c_lo, c_hi):
                b = nbank()
                n = c1 - c0
                fns = []
                for k in range(KT):
                    fns.append(lambda e, k=k: e.matmul(PB[b][:, 0:n], lhsT=WR[:, sl, k * 128:(k + 1) * 128],
                                                       rhs=xT[:, k, c0:c1], start=(k == 0), stop=(k == KT - 1)))
                S.mm_group([s_WR[sl]] + xblocks(c0, c1), [s_PB[b]], fns)
                evac(b, PB[b][:, 0:n], c0, c1)
                yield
            release(ui)

        def chain(gens):
            for g in gens:
                for _ in g:
                    yield

        def ln_stats(Tb, s_T, c0, c1, nfeat):
            n = c1 - c0
            b1, b2 = nbank(), nbank()
            for t in range(8):
                j = t % 2
                S.op("act", [s_T[t]], [s_SQ[j]], lambda e: e.activation(
                    out=SQ[:, j, 0:n], in_=Tb[:, t, c0:c1], func=AF.Square))
                S.mm_group([s_T[t], s_const], [s_PB[b1]], [lambda e: e.matmul(
                    PB[b1][:, 0:n], lhsT=ONES[:], rhs=Tb[:, t, c0:c1], start=(t == 0), stop=(t == 7))])
                S.mm_group([s_SQ[j], s_const], [s_PB[b2]], [lambda e: e.matmul(
                    PB[b2][:, 0:n], lhsT=ONES[:], rhs=SQ[:, j, 0:n], start=(t == 0), stop=(t == 7))])
            inv = 1.0 / nfeat
            S.op("dve", [s_PB[b1]], [s_CF32b], lambda e: e.tensor_scalar(
                out=STT[:, 0, 0:n], in0=PB[b1][:, 0:n], scalar1=inv, scalar2=None, op0=ALU.mult))
            S.op("dve", [s_CF32b], [s_FT[0]], lambda e: e.tensor_tensor(
                out=FT[:, 0, 0:n], in0=STT[:, 0, 0:n], in1=STT[:, 0, 0:n], op=ALU.mult))
            S.op("dve", [s_PB[b2], s_FT[0]], [s_CF32b], lambda e: e.scalar_tensor_tensor(
                out=STT[:, 1, 0:n], in0=PB[b2][:, 0:n], scalar=inv, in1=FT[:, 0, 0:n],
                op0=ALU.mult, op1=ALU.subtract))
            S.op("dve", [s_CF32b], [s_CF32b], lambda e: e.tensor_scalar(
                out=STT[:, 1, 0:n], in0=STT[:, 1, 0:n], scalar1=LN_EPS, scalar2=None, op0=ALU.add))
            S.op("act", [s_CF32b], [s_CF32b], lambda e: e.activation(
                out=STT[:, 1, 0:n], in_=STT[:, 1, 0:n], func=AF.Ln))
            S.op("act", [s_CF32b], [s_CF32b], lambda e: e.activation(
                out=STT[:, 1, 0:n], in_=STT[:, 1, 0:n], func=AF.Exp, scale=-0.5))

        d_x = [S.new_dma_sem("x0"), S.new_dma_sem("x1")]

        def phase0_block(st, blk):
            xi = blk % 2
            xb, sxb = XBS[xi], s_XBS[xi]
            sp16 = [s_PB[4], s_PB[5]]
            S.dma("pool", d_x[xi], [], [sxb], xb[:], x_in[st, blk * 128:(blk + 1) * 128, :])
            fns = [lambda e, k=k: e.transpose(P16B[:, k * 128:(k + 1) * 128], xb[:, k * 128:(k + 1) * 128], IDB[:])
                   for k in range(KT)]
            S.mm_group([sxb, s_const], sp16, fns)
            S.op("dve", sp16, [s_xT[blk]], lambda e: e.tensor_copy(
                out=xT[:, :, blk * 128:(blk + 1) * 128], in_=P16B.rearrange("p (k c) -> p k c", k=KT)))

        def emit_layer(st, li, NB, first_in_prog, last_in_prog, skip_phase0=False, hook=None):
            l = li
            Tm = NB * 128
            W = HALO + Tm
            MAIN0 = 128
            tcol = lambda c: c - 96
            ppc = lambda c0, c1=None: PP[:, l, c0:(c0 + 1 if c1 is None else c1)]

            if first_in_prog and not skip_phase0:
                for blk in range(NB + 1):
                    rb = blk % 3
                    S.dma("sp", d_r[rb], [], [s_R[rb]], R[rb][:], x_in[st, blk * 128:(blk + 1) * 128, :])
                    xi = blk % 2
                    xb, sxb = XBS[xi], s_XBS[xi]
                    pp16 = P16 if xi == 0 else P16B
                    sp16 = [s_P16] if xi == 0 else [s_PB[4], s_PB[5]]
                    S.op("act", [s_R[rb]], [sxb], lambda e: e.activation(out=xb[:], in_=R[rb][:], func=AF.Copy))
                    fns = [lambda e, k=k: e.transpose(pp16[:, k * 128:(k + 1) * 128], xb[:, k * 128:(k + 1) * 128], IDB[:])
                           for k in range(KT)]
                    S.mm_group([sxb, s_const], sp16, fns)
                    S.op("dve", sp16, [s_xT[blk]], lambda e: e.tensor_copy(
                        out=xT[:, :, blk * 128:(blk + 1) * 128], in_=pp16.rearrange("p (k c) -> p k c", k=KT)))

            mchunks = chunks(MAIN0, MAIN0 + Tm)
            first_merge = [True]

            def phase2(i):
                for d in range(16):
                    if d % 2 == 0:
                        uib = next_unit(li, "br")
                        slb = uib % NW
                    uig = next_unit(li, "in")
                    slg = uig % NW
                    for (c0, c1) in mchunks:
                        n = c1 - c0
                        bg, bp = nbank(), nbank()
                        fns = [lambda e, k=k: e.matmul(PB[bg][:, 0:n], lhsT=WR[:, slg, k * 128:(k + 1) * 128],
                                                       rhs=xT[:, k, c0:c1], start=(k == 0), stop=(k == KT - 1))
                               for k in range(KT)]
                        S.mm_group([s_WR[slg]] + xblocks(c0, c1), [s_PB[bg]], fns)
                        o = (d % 2) * 1024
                        fns = [lambda e, k=k: e.matmul(PB[bp][:, 0:n], lhsT=WR[:, slb, o + k * 128:o + (k + 1) * 128],
                                                       rhs=Y[:, k, c0 - MAIN0:c1 - MAIN0], start=(k == 0), stop=(k == 7))
                               for k in range(8)]
                        S.mm_group([s_WR[slb]] + s_Y, [s_PB[bp]], fns)
                        j = d % 2
                        S.op("act", [s_PB[bg]], [s_SQ[j]], lambda e: e.activation(
                            out=SQ[:, j, 0:n], in_=PB[bg][:, 0:n], func=AF.Sigmoid))
                        mg = MG[:, d, c0 - MAIN0:c1 - MAIN0]
                        if first_merge[0]:
                            S.op("dve", [s_PB[bp], s_SQ[j]], [s_MG[d]], lambda e: e.tensor_tensor(
                                out=mg, in0=PB[bp][:, 0:n], in1=SQ[:, j, 0:n], op=ALU.mult))
                        else:
                            S.op("dve", [s_PB[bp], s_SQ[j]], [s_FT[j]], lambda e: e.tensor_tensor(
                                out=FT[:, j, 0:n], in0=PB[bp][:, 0:n], in1=SQ[:, j, 0:n], op=ALU.mult))
                            S.op("dve", [s_FT[j], s_MG[d]], [s_MG[d]], lambda e: e.tensor_tensor(
                                out=mg, in0=FT[:, j, 0:n], in1=mg, op=ALU.add))
                    release(uig)
                    if d % 2 == 1:
                        release(uib)
                first_merge[0] = False

            def skip_units(kinds):
                for k_ in kinds:
                    next_unit(li, k_)

            def branch_A():
                S.dma("pool", d_cp, [s_CB16b], [s_CB16b], PW[:].rearrange("p a b c -> p (a b c)"), pw_in[l])
                for t in range(8):
                    inproj(li, "in", 96, MAIN0 + Tm, lambda b, ps, c0, c1, t=t: evac_copy(
                        b, ps, T1[:, t, tcol(c0):tcol(c1)], [s_T1[t]]))
                for g in range(4):
                    win = 2 ** (g + 1)
                    X = T1[:, 2 * g:2 * g + 2, :]
                    sX = [s_T1[2 * g], s_T1[2 * g + 1]]
                    sP, sQ = [s_T2], s_Y
                    S.op("dve", sX, sP, lambda e: e.tensor_tensor(
                        out=PQ[:, :, 1:W], in0=X[:, :, 1:W], in1=X[:, :, 0:W - 1], op=ALU.add))
                    cur, scur = PQ, sP
                    if win >= 4:
                        S.op("dve", sP, sQ, lambda e: e.tensor_tensor(
                            out=QQ[:, :, 3:W], in0=PQ[:, :, 3:W], in1=PQ[:, :, 1:W - 2], op=ALU.add))
                        cur, scur = QQ, sQ
                    if win >= 8:
                        S.op("dve", sQ, sP, lambda e: e.tensor_tensor(
                            out=PQ[:, :, 7:W], in0=QQ[:, :, 7:W], in1=QQ[:, :, 3:W - 4], op=ALU.add))
                        cur, scur = PQ, sP
                    if win >= 16:
                        S.op("dve", sP, sQ, lambda e: e.tensor_tensor(
                            out=QQ[:, :, 15:W], in0=PQ[:, :, 15:W], in1=PQ[:, :, 7:W - 8], op=ALU.add))
                        cur, scur = QQ, sQ
                    fixc = None
                    if st == 0:
                        fixc = HALO + (128 if (fused and li == 0) else 0)
                        S.op("dve", scur + [s_const], [s_FX], lambda e: e.tensor_tensor(
                            out=FX[:], in0=cur[:, :, fixc:fixc + 16],
                            in1=POOLFIX[:, g * 16:(g + 1) * 16].unsqueeze(1).broadcast_to([128, 2, 16]), op=ALU.mult))
                        S.op("dve", sX + [s_FX], [s_FX], lambda e: e.tensor_tensor(
                            out=FX[:], in0=FX[:], in1=X[:, :, fixc:fixc + 16], op=ALU.subtract))
                    S.op("dve", scur + sX, sX, lambda e: e.scalar_tensor_tensor(
                        out=X[:, :, HALO:W], in0=cur[:, :, HALO:W], scalar=1.0 / win, in1=X[:, :, HALO:W],
                        op0=ALU.mult, op1=ALU.subtract))
                    if fixc is not None:
                        S.op("dve", [s_FX], sX, lambda e: e.tensor_copy(out=X[:, :, fixc:fixc + 16], in_=FX[:]))
                for t in range(8):
                    inproj(li, "in", MAIN0, MAIN0 + Tm, lambda b, ps, c0, c1, t=t: S.op(
                        "act", [s_PB[b]], [s_T0[t]], lambda e: e.activation(
                            out=T0[:, t, tcol(c0):tcol(c1)], in_=ps, func=AF.Silu)))
                for g in range(4):
                    for dtl in range(2):
                        t = 2 * g + dtl
                        for (c0, c1) in chunks(HALO, W):
                            n = c1 - c0
                            b = nbank()
                            fns = [lambda e, ct=ct: e.matmul(PB[b][:, 0:n], lhsT=PW[:, g, ct, dtl * 128:(dtl + 1) * 128],
                                                             rhs=T1[:, 2 * g + ct, c0:c1], start=(ct == 0), stop=(ct == 1))
                                   for ct in range(2)]
                            S.mm_group([s_T1[2 * g], s_T1[2 * g + 1], s_CB16b], [s_PB[b]], fns)
                            S.op("dve", [s_PB[b], s_T0[t], s_const], [s_Y[t]], lambda e: e.scalar_tensor_tensor(
                                out=Y[:, t, c0 - HALO:c1 - HALO], in0=PB[b][:, 0:n], scalar=ppc(t),
                                in1=T0[:, t, c0:c1], op0=ALU.mult, op1=ALU.mult))

            def branch_B():
                S.dma("sp", d_c2, [s_FT[0], s_FT[1]], [s_FT[0], s_FT[1]],
                      FT[:, 0:2, :].rearrange("p a b -> p (a b)"), wst_in[l])
                for h in range(8):
                    S.op("dve", [s_FT[0], s_FT[1], s_const], [s_CB16b], lambda e: e.tensor_tensor(
                        out=WST[:, h, :], in0=FT[:, 0:2, :].rearrange("p a b -> p (a b)")[:, h * 128:(h + 1) * 128],
                        in1=MASK, op=ALU.mult))
                S.dma("sp", d_c2, [s_CF32], [s_CF32], BSB[:].rearrange("p a b -> p (a b)"), bsb_in[l])
                for t in range(8):
                    inproj(li, "in", MAIN0, MAIN0 + Tm, lambda b, ps, c0, c1, t=t: evac_copy(
                        b, ps, T1[:, t, tcol(c0):tcol(c1)], [s_T1[t]]))
                def ev_bg(t):
                    return lambda b, ps, c0, c1: S.op(
                        "act", [s_PB[b]], [s_T0[t]], lambda e: e.activation(
                            out=T0[:, t, tcol(c0):tcol(c1)], in_=ps, func=AF.Silu))

                def ev_u(t):
                    return lambda b, ps, c0, c1: S.op(
                        "dve", [s_PB[b], s_T0[t]], [s_T0[t]], lambda e: e.tensor_tensor(
                            out=T0[:, t, tcol(c0):tcol(c1)], in0=ps, in1=T0[:, t, tcol(c0):tcol(c1)], op=ALU.mult))

                gq = chain([inproj_gen(li, "in", MAIN0, MAIN0 + Tm, ev_bg(t)) for t in range(8)] +
                           [inproj_gen(li, "in", MAIN0, MAIN0 + Tm, ev_u(t)) for t in range(8)])
                for (c0, c1) in chunks(HALO, W):
                    n = c1 - c0
                    ln_stats(T1, s_T1, c0, c1, 1024)
                    for t in range(8):
                        j = t % 4
                        S.op("dve", [s_T1[t], s_CF32b], [s_FT[j]], lambda e: e.tensor_tensor(
                            out=FT[:, j, 0:n], in0=T1[:, t, c0:c1], in1=STT[:, 0, 0:n], op=ALU.subtract))
                        S.op("dve", [s_FT[j], s_CF32b], [s_FT[j]], lambda e: e.tensor_tensor(
                            out=FT[:, j, 0:n], in0=FT[:, j, 0:n], in1=STT[:, 1, 0:n], op=ALU.mult))
                        S.op("act", [s_FT[j], s_const], [s_T1[t]], lambda e: e.activation(
                            out=T1[:, t, c0:c1], in_=FT[:, j, 0:n], func=AF.Identity,
                            scale=ppc(8 + t), bias=ppc(16 + t)))
                        next(gq, None)
                for _ in gq:
                    pass
                s_VN = [St(), St()]
                inherit(s_VN, [s_T2])
                bank_lim[0] = 4

                def stage_T(blk):
                    tc0 = HALO + blk * 128
                    vb = blk % 2
                    p16, sp = (P16, [s_P16]) if vb == 0 else (P16B, [s_PB[4], s_PB[5]])
                    fns = [lambda e, h=h: e.transpose(p16[:, h * 128:(h + 1) * 128], T1[:, h, tc0:tc0 + 128], IDB[:])
                           for h in range(8)]
                    S.mm_group(s_T1 + [s_const], sp, fns)
                    S.op("dve", sp, [s_VN[vb]], lambda e: e.tensor_copy(out=VN[:, vb, :], in_=p16[:, 0:1024]))

                def stage_S(blk):
                    tc0 = HALO + blk * 128
                    vb = blk % 2
                    for hh in range(2):
                        b = nbank()
                        for h4 in range(4):
                            h = hh * 4 + h4
                            S.mm_group([s_VN[vb], s_CB16b], [s_PB[b]], [lambda e: e.matmul(
                                PB[b][:, h4 * 128:(h4 + 1) * 128], lhsT=VN[:, vb, h * 128:(h + 1) * 128],
                                rhs=WST[:, h, :], start=True, stop=True)])
                        j = hh
                        S.op("dve", [s_PB[b], s_CF32], [s_FT[j]], lambda e: e.tensor_tensor(
                            out=FT[:, j, :], in0=PB[b][:], in1=BSB[:, hh, :], op=ALU.add))
                        S.op("dve", [s_FT[j]] + s_T0[hh * 4:hh * 4 + 4], s_Y[hh * 4:hh * 4 + 4], lambda e: e.tensor_tensor(
                            out=Y[:, hh * 4:hh * 4 + 4, blk * 128:(blk + 1) * 128],
                            in0=FT[:, j, :].rearrange("p (a b) -> p a b", a=4),
                            in1=T0[:, hh * 4:hh * 4 + 4, tc0:tc0 + 128], op=ALU.mult))

                stage_T(0)
                for blk in range(NB):
                    if blk + 1 < NB:
                        stage_T(blk + 1)
                    stage_S(blk)
                bank_lim[0] = 6
                inherit([s_T2], s_VN)

            def branch_C():
                inherit(s_LT, [s_CF32b])
                S.dma("pool", d_cp, [s_CB16], [s_CB16], BT[:].rearrange("p a b c d -> p (a b c d)"), bt_in)
                S.op("act", [s_const], [s_SC], lambda e: e.activation(out=SC[:, 0:8], in_=ppc(48, 56), func=AF.Exp))
                for i8 in range(8):
                    S.op("dve", [s_SC, s_const], [s_CF32], lambda e: e.tensor_scalar(
                        out=SINKBC[:, i8 // 4, i8 % 4, :], in0=CST[:, 0:128], scalar1=0.0, scalar2=SC[:, i8:i8 + 1],
                        op0=ALU.mult, op1=ALU.add))
                for t in range(8):
                    inproj(li, "in", MAIN0, MAIN0 + Tm, lambda b, ps, c0, c1, t=t: S.op(
                        "act", [s_PB[b]], [s_T1[t]], lambda e: e.activation(
                            out=T1[:, t, tcol(c0):tcol(c1)], in_=ps, func=AF.Silu)))
                for t in range(8):
                    inproj(li, "in", MAIN0, MAIN0 + Tm, lambda b, ps, c0, c1, t=t: evac_copy(
                        b, ps, T0[:, t, tcol(c0):tcol(c1)], [s_T0[t]], scale=0.125))
                Te = MAIN0 + Tm
                for kvi in range(2):
                    inproj(li, "kd", 0, Te, lambda b, ps, c0, c1, kvi=kvi: evac_copy(
                        b, ps, KD[:, kvi, c0:c1], [s_T2]))
                inproj(li, "in", 0, Te, lambda b, ps, c0, c1: evac_copy(b, ps, VT[:, c0:c1], [s_T2]))
                S.op("dve", [], [s_T2], lambda e: e.memset(VA[:].rearrange("p a b c d -> p (a b c d)"), 0.0))
                nbe = NB + 1
                fns = [lambda e, bb=bb: e.transpose(P16[:, bb * 128:(bb + 1) * 128], VT[:, bb * 128:(bb + 1) * 128], IDB[:])
                       for bb in range(nbe)]
                S.mm_group([s_T2, s_const], [s_P16], fns)
                pv = P16[:, 0:nbe * 128].rearrange("p (a b) -> p a b", a=nbe)
                for kvi in range(2):
                    for var in range(2):
                        S.op("dve", [s_P16], [s_T2], lambda e: e.tensor_copy(
                            out=VA[:, 0:nbe, kvi, var, var * 64:var * 64 + 64], in_=pv[:, :, kvi * 64:kvi * 64 + 64]))
                PTB = [PT, FT[:, 2:4, :].bitcast(BF16).rearrange("p a (b c) -> p (a b) c", b=2)]
                sPTB = [[s_CB16b], [s_FT[2], s_FT[3]]]
                its = [(blk, kvi) for blk in range(1, NB + 1) for kvi in range(2)]

                def stage_L(it):
                    blk, kvi = its[it]
                    tc0 = HALO + (blk - 1) * 128
                    ptb, sptb = PTB[it % 2], sPTB[it % 2]
                    for kbi, kb in enumerate((blk - 1, blk)):
                        b0, b1 = kbi * 2, kbi * 2 + 1
                        fns = []
                        for var in range(2):
                            fns.append(lambda e, var=var, bb=kbi * 2 + var: e.matmul(
                                PB[bb][:].rearrange("p (a b) -> p a b", a=4),
                                lhsT=KD[var * 64:var * 64 + 64, kvi, kb * 128:(kb + 1) * 128],
                                rhs=T0[var * 64:var * 64 + 64, 4 * kvi:4 * kvi + 4, tc0:tc0 + 128],
                                start=True, stop=False))
                        for var in range(2):
                            fns.append(lambda e, var=var, bb=kbi * 2 + var: e.matmul(
                                PB[bb][:], lhsT=IDB[:], rhs=BT[:, kbi, kvi, var, :], start=False, stop=True))
                        S.mm_group([s_T2, s_CB16, s_const] + s_T0[4 * kvi:4 * kvi + 4], [s_PB[b0], s_PB[b1]], fns)
                        for var in range(2):
                            pidx = kbi * 2 + var
                            S.op("act", [s_PB[pidx]], sptb, lambda e: e.activation(
                                out=ptb[:, pidx, :], in_=PB[pidx][:], func=AF.Exp))

                def stage_V(it):
                    blk, kvi = its[it]
                    tc0 = HALO + (blk - 1) * 128
                    ptb, sptb = PTB[it % 2], sPTB[it % 2]
                    bo, bd = 4, 5
                    fo, fd = [], []
                    for kbi, kb in enumerate((blk - 1, blk)):
                        for var in range(2):
                            pidx = kbi * 2 + var
                            first, last = (pidx == 0), (pidx == 3)
                            fo.append(lambda e, kb=kb, var=var, pidx=pidx, first=first, last=last: e.matmul(
                                PB[bo][:], lhsT=VA[:, kb, kvi, var, :], rhs=ptb[:, pidx, :], start=first, stop=last))
                            hsel = 0 if (st == 0 and (kb == 0 or (fused and li == 0 and kb == 1))) else 1
                            fd.append(lambda e, hsel=hsel, var=var, pidx=pidx, first=first, last=last: e.matmul(
                                PB[bd][:], lhsT=ONESV[:, hsel, var, :], rhs=ptb[:, pidx, :], start=first, stop=last))
                    S.mm_group([s_T2] + sptb, [s_PB[bo]], fo)
                    S.mm_group([s_const] + sptb, [s_PB[bd]], fd)
                    S.op("dve", [s_PB[bd], s_CF32], [s_FT[0]], lambda e: e.tensor_tensor(
                        out=FT[:, 0, :], in0=PB[bd][:], in1=SINKBC[:, kvi].rearrange("p a b -> p (a b)"), op=ALU.add))
                    S.op("act", [s_FT[0]], [s_FT[0]], lambda e: e.activation(out=FT[:, 0, :], in_=FT[:, 0, :], func=AF.Ln))
                    S.op("act", [s_FT[0]], [s_FT[0]], lambda e: e.activation(
                        out=FT[:, 0, :], in_=FT[:, 0, :], func=AF.Exp, scale=-1.0))
                    S.op("dve", [s_PB[bo], s_FT[0]], [s_FT[1]], lambda e: e.tensor_tensor(
                        out=FT[:, 1, :], in0=PB[bo][:], in1=FT[:, 0, :], op=ALU.mult))
                    S.op("dve", [s_FT[1]] + s_T1[4 * kvi:4 * kvi + 4], s_Y[4 * kvi:4 * kvi + 4], lambda e: e.tensor_tensor(
                        out=Y[:, 4 * kvi:4 * kvi + 4, (blk - 1) * 128:blk * 128],
                        in0=FT[:, 1, :].rearrange("p (a b) -> p a b", a=4),
                        in1=T1[:, 4 * kvi:4 * kvi + 4, tc0:tc0 + 128], op=ALU.mult))

                stage_L(0)
                for it in range(len(its)):
                    if it + 1 < len(its):
                        stage_L(it + 1)
                    stage_V(it)
                bank_rr[0] = 0
                inherit([s_CF32b], s_LT)

            def branch_D():
                for t in range(8):
                    inproj(li, "in", 96, MAIN0 + Tm, lambda b, ps, c0, c1, t=t: S.op(
                        "act", [s_PB[b]], [s_T0[t]], lambda e: e.activation(
                            out=T0[:, t, tcol(c0):tcol(c1)], in_=ps, func=AF.Sigmoid)))
                for t in range(8):
                    inproj(li, "in", 96, MAIN0 + Tm, lambda b, ps, c0, c1, t=t: S.op(
                        "dve", [s_PB[b], s_T0[t]], [s_T0[t]], lambda e: e.tensor_tensor(
                            out=T0[:, t, tcol(c0):tcol(c1)], in0=ps, in1=T0[:, t, tcol(c0):tcol(c1)], op=ALU.mult)))
                s_DG2 = St()
                inherit([s_DG2], s_T1)
                for t in range(8):
                    cw = PP[:, l, 56 + t * 31:56 + (t + 1) * 31]
                    DG, sdg = (DIAG, s_CB16) if t % 2 == 0 else (DIAG2, s_DG2)
                    S.op("dve", [s_const], [sdg], lambda e: e.tensor_tensor(
                        out=DG[:], in0=IDB[:].unsqueeze(1).broadcast_to([128, 31, 128]),
                        in1=cw.unsqueeze(2).broadcast_to([128, 31, 128]), op=ALU.mult))
                    for (c0, c1) in reversed(chunks(HALO, W)):
                        n = c1 - c0
                        b = nbank()
                        fns = [lambda e, j=j: e.matmul(PB[b][:, 0:n], lhsT=DG[:, j, :],
                                                       rhs=T0[:, t, c0 - 30 + j:c1 - 30 + j], start=(j == 0), stop=(j == 30))
                               for j in range(31)]
                        S.mm_group([s_T0[t], sdg], [s_PB[b]], fns)
                        S.op("act", [s_PB[b], s_const], [s_T0[t]], lambda e: e.activation(
                            out=T0[:, t, c0:c1], in_=PB[b][:, 0:n], func=AF.Identity, bias=ppc(24 + t)))
                inherit(s_T1, [s_DG2])

                def ev_dg(t):
                    return lambda b, ps, c0, c1: S.op(
                        "act", [s_PB[b]], [s_T1[t]], lambda e: e.activation(
                            out=T1[:, t, tcol(c0):tcol(c1)], in_=ps, func=AF.Silu))

                gq = chain([inproj_gen(li, "in", MAIN0, MAIN0 + Tm, ev_dg(t)) for t in range(8)])
                for (c0, c1) in chunks(HALO, W):
                    n = c1 - c0
                    ln_stats(T0, s_T0, c0, c1, 1024)
                    for t in range(8):
                        fa = t % 4
                        S.op("dve", [s_T0[t], s_CF32b], [s_FT[fa]], lambda e: e.tensor_tensor(
                            out=FT[:, fa, 0:n], in0=T0[:, t, c0:c1], in1=STT[:, 0, 0:n], op=ALU.subtract))
                        S.op("dve", [s_FT[fa], s_CF32b], [s_FT[fa]], lambda e: e.tensor_tensor(
                            out=FT[:, fa, 0:n], in0=FT[:, fa, 0:n], in1=STT[:, 1, 0:n], op=ALU.mult))
                        S.op("act", [s_FT[fa], s_const], [s_T0[t]], lambda e: e.activation(
                            out=T0[:, t, c0:c1], in_=FT[:, fa, 0:n], func=AF.Silu, scale=ppc(32 + t), bias=ppc(40 + t)))
                        next(gq, None)
                for _ in gq:
                    pass
                for t in range(8):
                    S.op("dve", [s_T0[t], s_T1[t]], [s_Y[t]], lambda e: e.tensor_tensor(
                        out=Y[:, t, 0:Tm], in0=T0[:, t, HALO:W], in1=T1[:, t, HALO:W], op=ALU.mult))

            inherit(s_T0, [s_R[0], s_R[1]])
            inherit(s_T1, [s_LN])
            inherit([s_T2], [s_R[2], s_XB, s_XBS[1]])
            bank_lim[0] = 6
            if len(branches) < 4:
                for d_ in range(16):
                    S.op("dve", [], [s_MG[d_]], lambda e: e.memset(MG[:, d_, :], 0.0))
                first_merge[0] = False
            brs = [branch_A, branch_B, branch_C, branch_D]
            nin = [16, 24, 19, 24]
            for i in range(4):
                if i in branches:
                    brs[i]()
                    phase2(i)
                else:
                    for _ in range(nin[i] + 24):
                        prefetch()
                        wstate["used"] += 1
                        prefetch()

            inherit([s_LN], s_T1)
            S.dma("sp", d_ln, [s_LN], [s_LN], LNG[:], lng_in[l])
            S.dma("sp", d_ln, [s_LN], [s_LN], LNB[:], lnb_in[l])
            inherit([s_R[0], s_R[1]], s_T0)
            inherit([s_R[2], s_XB, s_XBS[1]], [s_T2])
            bank_lim[0] = 4
            if li == 0 and first_in_prog:
                rsrc = lambda blk: x_in[st, blk * 128:(blk + 1) * 128, :]
                rst = None
            else:
                rsrc = lambda blk: x1s[st, (blk - 1) * 128:blk * 128, :]
                rst = s_x1s[st]

            def load_resid(blk):
                rb = blk % 3
                S.dma("sp", d_r[rb], [rst] if rst is not None else [], [s_R[rb]], R[rb][:], rsrc(blk))

            load_resid(1)
            load_resid(2)
            load_resid(3)
            for e_ in range(16):
                uio = next_unit(li, "out")
                sl = uio % NW
                for (c0, c1) in mchunks:
                    n = c1 - c0
                    b = nbank()
                    fns = [lambda e, k=k: e.matmul(PB[b][:, 0:n], lhsT=WR[:, sl, k * 128:(k + 1) * 128],
                                                   rhs=MG[:, k, c0 - MAIN0:c1 - MAIN0], start=(k == 0), stop=(k == KT - 1))
                           for k in range(KT)]
                    S.mm_group([s_WR[sl]] + s_MG, [s_PB[b]], fns)
                    evac_copy(b, PB[b][:, 0:n], xT[:, e_, c0:c1], xblocks(c0, c1))
                release(uio)
            sshift = 1 if (fused and li == 0 and st == 0) else 0

            def stage_A(blk):
                rb = blk % 3
                fns = [lambda e, k=k: e.transpose(P16[:, k * 128:(k + 1) * 128], xT[:, k, blk * 128:(blk + 1) * 128], IDB[:])
                       for k in range(KT)]
                S.mm_group([s_xT[blk], s_const], [s_P16], fns)
                S.op("dve", [s_P16, s_R[rb]], [s_R[rb]], lambda e: e.scalar_tensor_tensor(
                    out=R[rb][:], in0=R[rb][:], scalar=ALPHA, in1=P16[:], op0=ALU.mult, op1=ALU.add))
                o = 8 * (blk % 3)
                ssc = s_SCP[blk % 3]
                S.op("dve", [], [ssc], lambda e: e.memset(SC[:, o + 8:o + 10], 0.0))
                S.op("act", [s_R[rb]], s_Y + [ssc], lambda e: e.activation(
                    out=JK[:], in_=R[rb][:], func=AF.Identity, accum_out=SC[:, o + 8:o + 9]))
                S.op("act", [s_R[rb]], s_Y + [ssc], lambda e: e.activation(
                    out=JK[:], in_=R[rb][:], func=AF.Square, accum_out=SC[:, o + 9:o + 10]))

            def stage_A2(blk):
                o = 8 * (blk % 3)
                ssc = s_SCP[blk % 3]
                S.op("dve", [ssc], [ssc], lambda e: e.tensor_scalar(
                    out=SC[:, o + 10:o + 12], in0=SC[:, o + 8:o + 10], scalar1=1.0 / D, scalar2=None, op0=ALU.mult))
                S.op("dve", [ssc], [ssc], lambda e: e.tensor_tensor(
                    out=SC[:, o + 12:o + 13], in0=SC[:, o + 10:o + 11], in1=SC[:, o + 10:o + 11], op=ALU.mult))
                S.op("dve", [ssc], [ssc], lambda e: e.tensor_tensor(
                    out=SC[:, o + 13:o + 14], in0=SC[:, o + 11:o + 12], in1=SC[:, o + 12:o + 13], op=ALU.subtract))
                S.op("dve", [ssc], [ssc], lambda e: e.tensor_scalar(
                    out=SC[:, o + 14:o + 15], in0=SC[:, o + 13:o + 14], scalar1=LN_EPS, scalar2=None, op0=ALU.add))
                S.op("act", [ssc], [ssc], lambda e: e.activation(out=SC[:, o + 14:o + 15], in_=SC[:, o + 14:o + 15], func=AF.Sqrt))

            def stage_B(blk):
                rb = blk % 3
                o = 8 * (blk % 3)
                ssc = s_SCP[blk % 3]
                S.op("dve", [ssc], [ssc], lambda e: e.reciprocal(out=SC[:, o + 14:o + 15], in_=SC[:, o + 14:o + 15]))
                S.op("dve", [ssc, s_LN, s_R[rb]], [s_R[rb]], lambda e: e.scalar_tensor_tensor(
                    out=R[rb][:], in0=R[rb][:], scalar=SC[:, o + 10:o + 11], in1=LNG[:],
                    op0=ALU.subtract, op1=ALU.mult))
                S.op("dve", [ssc, s_LN, s_R[rb]], [s_R[rb]], lambda e: e.scalar_tensor_tensor(
                    out=R[rb][:], in0=R[rb][:], scalar=SC[:, o + 14:o + 15], in1=LNB[:],
                    op0=ALU.mult, op1=ALU.add))
                if last_in_prog:
                    S.dma("sp", d_o[rb], [s_R[rb]], [], y_out[st, (blk - 1) * 128:blk * 128, :], R[rb][:])
                else:
                    if blk - sshift >= 1:
                        S.dma("sp", d_o[rb], [s_R[rb]], [s_x1s[st]],
                              x1s[st, (blk - 1 - sshift) * 128:(blk - sshift) * 128, :], R[rb][:])
                    xi = blk % 2
                    xb, sxb = XBS[xi], s_XBS[xi]
                    if blk == 1 and st == 0:
                        S.op("act", [s_R[rb], s_const], [sxb], lambda e: e.activation(
                            out=xb[:], in_=R[rb][:], func=AF.Copy, scale=FLAG))
                    else:
                        S.op("act", [s_R[rb]], [sxb], lambda e: e.activation(out=xb[:], in_=R[rb][:], func=AF.Copy))

            def stage_C(blk):
                if last_in_prog:
                    return
                xi = blk % 2
                xb, sxb = XBS[xi], s_XBS[xi]
                sp16 = [s_PB[4], s_PB[5]]
                fns = [lambda e, k=k: e.transpose(P16B[:, k * 128:(k + 1) * 128], xb[:, k * 128:(k + 1) * 128], IDB[:])
                       for k in range(KT)]
                S.mm_group([sxb, s_const], sp16, fns)
                sl_ = blk - sshift
                if sl_ == 0 and sshift == 0:
                    return
                S.op("dve", sp16, [s_xT[sl_]], lambda e: e.tensor_copy(
                    out=xT[:, :, sl_ * 128:(sl_ + 1) * 128], in_=P16B.rearrange("p (k c) -> p k c", k=KT)))
                if sshift == 1 and blk == NB:
                    S.op("dve", sp16, [s_xT[9]], lambda e: e.tensor_copy(
                        out=xT[:, :, 9 * 128:10 * 128], in_=P16B.rearrange("p (k c) -> p k c", k=KT)))

            for i in range(1, NB + 4):
                if i <= NB:
                    stage_A(i)
                if 1 <= i - 1 <= NB:
                    stage_A2(i - 1)
                if 1 <= i - 2 <= NB:
                    stage_B(i - 2)
                    if (i - 2) + 3 <= NB:
                        load_resid((i - 2) + 3)
                if 1 <= i - 3 <= NB:
                    stage_C(i - 3)
                if hook is not None:
                    hook(i)

        for st in range(NST):
            if fused:
                emit_layer(st, 0, 9 if st == 0 else 8, True, False, skip_phase0=(st == 1))
                if st == 1:
                    S.op("dve", [s_xT[9]], [s_xT[0]], lambda e: e.tensor_copy(
                        out=xT[:, :, 0:128], in_=xT[:, :, 9 * 128:10 * 128]))
                hk = None
                if st == 0:
                    def hk(i):
                        if i == 1:
                            phase0_block(1, 0)
                            phase0_block(1, 1)
                        elif 2 <= i <= 8:
                            phase0_block(1, i)
                emit_layer(st, 1, 8, False, True, hook=hk)
            else:
                emit_layer(st, 0, 8, True, True)
        allst = s_R + s_x1s + s_xT + s_MG + s_Y + s_T0 + s_T1 + [s_T2, s_P16] + s_PB + s_WR
        S.wait_all("sp", allst)
    return nc


def _consts(core, fused):
    cst = np.zeros((128, 384), np.float32)
    cst[:, 0:128] = np.eye(128, dtype=np.float32)
    s = np.arange(128)[:, None]
    t = np.arange(128)[None, :]
    cst[:, 128:256] = (s <= t).astype(np.float32)
    cst[:, 256] = 0.0 if core == 0 else 1.0
    for g in range(4):
        win = 2 ** (g + 1)
        tt = np.arange(16)
        if core == 0:
            cst[:, 272 + g * 16:272 + (g + 1) * 16] = (1.0 / np.minimum(tt + 1, win)).astype(np.float32)[None, :]
        else:
            cst[:, 272 + g * 16:272 + (g + 1) * 16] = np.float32(1.0 / win)
    return cst


def _layer_inputs(ls, w_in, pool_w, pool_scale, sgu_ln_g, sgu_ln_b, sgu_w, sgu_b, attn_sinks, rel_bias,
                  conv_w, conv_b, conv_ln_g, conv_ln_b, w_branch, w_out, ln_g, ln_b):
    nl = len(ls)
    ws = np.stack([build_wstream(w_in[l], w_branch[l], w_out[l]) for l in ls])
    pp = np.stack([build_pp(l, pool_scale, sgu_ln_g, sgu_ln_b, conv_b, conv_ln_g, conv_ln_b, attn_sinks, conv_w)
                   for l in ls], axis=1)
    pw = np.stack([pool_w[l].reshape(4, 2, 128, 256).transpose(2, 0, 1, 3).reshape(128, 2048) for l in ls])
    wst = np.stack([sgu_w[l].transpose(2, 0, 1).reshape(128, 1024) for l in ls])
    bsb = np.stack([np.broadcast_to(sgu_b[l].reshape(1, 1024), (128, 1024)) for l in ls])
    lng = np.stack([np.broadcast_to(ln_g[l][None, :], (128, D)) for l in ls])
    lnb = np.stack([np.broadcast_to(ln_b[l][None, :], (128, D)) for l in ls])
    return dict(ws=np.ascontiguousarray(ws), pp=np.ascontiguousarray(pp), pw=np.ascontiguousarray(pw),
                wst=np.ascontiguousarray(wst), bsb=np.ascontiguousarray(bsb), lng=np.ascontiguousarray(lng),
                lnb=np.ascontiguousarray(lnb), bt=build_bias_table(rel_bias))


def _shard_x(x2d, nblk_front):
    pad = nblk_front * 128
    xp = np.concatenate([np.zeros((pad, D), np.float32), x2d], axis=0)
    out = []
    for c in range(NCORE):
        sts = []
        for st in range(NST):
            t0 = c * TOK_CORE + st * ST_TOK
            if nblk_front == 2 and st == 1:
                blk = np.zeros((pad + ST_TOK, D), np.float32)
                blk[0:128 + ST_TOK] = xp[t0 + 128:t0 + pad + ST_TOK]
                sts.append(blk)
            else:
                sts.append(xp[t0:t0 + pad + ST_TOK])
        out.append(np.ascontiguousarray(np.stack(sts)))
    return out


_PROG = {}


def _get_prog(key, *a):
    if key not in _PROG:
        _PROG[key] = build_program(*a)
    return _PROG[key]


FUSED = True


def kernel(x, w_in, pool_w, pool_scale, sgu_ln_g, sgu_ln_b, sgu_w, sgu_b, attn_sinks, rel_bias,
           conv_w, conv_b, conv_ln_g, conv_ln_b, w_branch, w_out, ln_g, ln_b):
    args = [np.asarray(a, np.float32) for a in (w_in, pool_w, pool_scale, sgu_ln_g, sgu_ln_b, sgu_w, sgu_b,
                                                 attn_sinks, rel_bias, conv_w, conv_b, conv_ln_g, conv_ln_b,
                                                 w_branch, w_out, ln_g, ln_b)]
    x2d = np.asarray(x, np.float32).reshape(SEQ, D)
    if FUSED:
        nc = _get_prog("fused", [0, 1], True)
        li = _layer_inputs([0, 1], *args)
        xs = _shard_x(x2d, 2)
        in_maps = [dict(li, x_in=xs[c], cst=_consts(c, True)) for c in range(NCORE)]
        res = run_bass_kernel_spmd(nc, in_maps, core_ids=list(range(NCORE)))
        out = np.concatenate([r["y_out"].reshape(TOK_CORE, D) for r in res.results], axis=0)
        return out.reshape(1, SEQ, D)
    cur = x2d
    for l in range(2):
        nc = _get_prog("single", [0], False)
        li = _layer_inputs([l], *args)
        xs = _shard_x(cur, 1)
        in_maps = [dict(li, x_in=xs[c], cst=_consts(c, False)) for c in range(NCORE)]
        res = run_bass_kernel_spmd(nc, in_maps, core_ids=list(range(NCORE)))
        cur = np.concatenate([r["y_out"].reshape(TOK_CORE, D) for r in res.results], axis=0)
    return cur.reshape(1, SEQ, D)
```
